# Optimizing a Trainium2 kernel written in Bass

```python
import math
import jax, jax.numpy as jnp
from jax import lax
import numpy as np

D_MODEL = 1024
BATCH = 16
SEQ = 2048
DEPTH = 4
DEC_BATCH = 8
DEC_SEQ = 2048
PAST_LEN = 128

RET_HEADS = 4
RET_QK_DIM = 128
RET_V_DIM = 256
RET_QK_W = RET_HEADS * RET_QK_DIM
RET_V_W = RET_HEADS * RET_V_DIM
RET_CHUNK = 128
MLA_HEADS = 8
MLA_NOPE = 128
MLA_ROPE = 64
MLA_V = 128
Q_LORA = 256
KV_LORA = 128
MLA_QK = MLA_NOPE + MLA_ROPE
MLA_V_W = MLA_HEADS * MLA_V
Q_BLOCK = 128
IN_SPLITS = (RET_QK_W, RET_QK_W, RET_V_W, RET_V_W, Q_LORA, KV_LORA, MLA_ROPE, D_MODEL, D_MODEL)
IN_WIDTH = sum(IN_SPLITS)
D_FF = 2816
CONV_W = 3
ROPE_THETA = 10000.0
LN_EPS = 1e-5
RMS_EPS = 1e-6
DEEPNORM_ALPHA = (2.0 * DEPTH) ** 0.25
DEEPNORM_BETA = (8.0 * DEPTH) ** -0.25

kernel_name = "hybrid_retention_mla_convffn_encoder"


def _layernorm(x):
    xf = x.astype(jnp.float32)
    mu = jnp.mean(xf, axis=-1, keepdims=True)
    xc = xf - mu
    var = jnp.mean(xc * xc, axis=-1, keepdims=True)
    return (xc * lax.rsqrt(var + LN_EPS)).astype(x.dtype)


def _rmsnorm(x, g):
    xf = x.astype(jnp.float32)
    y = xf * lax.rsqrt(jnp.mean(xf * xf, axis=-1, keepdims=True) + RMS_EPS)
    return y.astype(x.dtype) * g


def _rope_tables(seq, dim):
    inv = 1.0 / (ROPE_THETA ** (jnp.arange(0, dim, 2, dtype=jnp.float32) / dim))
    ang = jnp.arange(seq, dtype=jnp.float32)[:, None] * inv[None, :]
    return jnp.cos(ang), jnp.sin(ang)


def _apply_rope(x, cos, sin):
    extra = x.ndim - 3
    c = cos.reshape(cos.shape[0], *([1] * extra), cos.shape[-1])
    s = sin.reshape(sin.shape[0], *([1] * extra), sin.shape[-1])
    xf = x.astype(jnp.float32)
    x1, x2 = jnp.split(xf, 2, axis=-1)
    return jnp.concatenate([x1 * c - x2 * s, x2 * c + x1 * s], axis=-1).astype(x.dtype)


def _retention_one_direction(q, k, v, log_gamma, strict):
    B, S, H, DK = q.shape
    DV = v.shape[-1]
    C = RET_CHUNK
    N = S // C
    qc = q.reshape(B, N, C, H, DK)
    kc = k.reshape(B, N, C, H, DK)
    vc = v.reshape(B, N, C, H, DV)
    idx = jnp.arange(C, dtype=jnp.float32)
    diff = idx[:, None] - idx[None, :]
    keep = (diff > 0) if strict else (diff >= 0)
    decay = jnp.where(keep[None], jnp.exp(log_gamma[:, None, None] * jnp.maximum(diff, 0.0)[None]), 0.0)
    scores = jnp.einsum('bnqhd,bnkhd->bnhqk', qc, kc) * decay[None, None]
    intra = jnp.einsum('bnhqk,bnkhe->bnqhe', scores, vc)
    q_decay = jnp.exp(log_gamma[None, :] * (idx[:, None] + 1.0))
    k_decay = jnp.exp(log_gamma[None, :] * (C - 1.0 - idx)[:, None])
    chunk_decay = jnp.exp(log_gamma * C)
    kv_chunk = jnp.einsum('bnchd,bnche->nbhde', kc * k_decay[None, None, :, :, None], vc)

    def step(state, kv_n):
        return chunk_decay[None, :, None, None] * state + kv_n, state

    _, states = lax.scan(step, jnp.zeros((B, H, DK, DV), jnp.float32), kv_chunk)
    cross = jnp.einsum('bnchd,nbhde->bnche', qc * q_decay[None, None, :, :, None], states)
    return (intra + cross).reshape(B, S, H, DV)


def _bidirectional_retention(q, k, v, decay_fwd, decay_bwd):
    dt = v.dtype
    qf, kf, vf = q.astype(jnp.float32), k.astype(jnp.float32), v.astype(jnp.float32)
    lg_f = jnp.log(jax.nn.sigmoid(decay_fwd.astype(jnp.float32)))
    lg_b = jnp.log(jax.nn.sigmoid(decay_bwd.astype(jnp.float32)))
    o_f = _retention_one_direction(qf, kf, vf, lg_f, False)
    o_b = jnp.flip(_retention_one_direction(jnp.flip(qf, 1), jnp.flip(kf, 1), jnp.flip(vf, 1), lg_b, True), 1)
    return (o_f + o_b).astype(dt)


def _mla_attention(q_nope, q_rope, k_nope, k_rope, v):
    B, S, H, _ = q_nope.shape
    NB = S // Q_BLOCK
    scale = MLA_QK ** -0.5

    def blockify(t):
        return jnp.moveaxis(t.reshape(B, NB, Q_BLOCK, *t.shape[2:]), 1, 0)

    def one_block(args):
        qn, qr = args
        s = jnp.einsum('bqhd,bkhd->bhqk', qn, k_nope) + jnp.einsum('bqhr,bkr->bhqk', qr, k_rope)
        p = jax.nn.softmax(s.astype(jnp.float32) * scale, axis=-1).astype(v.dtype)
        return jnp.einsum('bhqk,bkhe->bqhe', p, v)

    out = lax.map(one_block, (blockify(q_nope), blockify(q_rope)))
    return jnp.moveaxis(out, 0, 1).reshape(B, S, H * MLA_V)


def _dwconv3(a, w, b):
    ap = jnp.pad(a, ((0, 0), (1, 1), (0, 0)))
    return ap[:, :-2] * w[0] + ap[:, 1:-1] * w[1] + ap[:, 2:] * w[2] + b


def _encoder_layer(x, c, w_ada, b_ada, w_in, ret_decay_fwd, ret_decay_bwd, ret_gn_g, w_ret_o,
                   q_norm_g, kv_norm_g, w_uq, w_uk, w_uv, w_mla_o, w_out, ln1_g, ln1_b,
                   w_up, conv_w, conv_b, w_down, ln2_g, ln2_b):
    B, S, D = x.shape
    ada = jax.nn.silu(c) @ w_ada + b_ada
    sh1, sc1, g1, sh2, sc2, g2 = jnp.split(ada[:, None, :], 6, axis=-1)
    cos_r, sin_r = _rope_tables(S, RET_QK_DIM)
    cos_m, sin_m = _rope_tables(S, MLA_ROPE)

    h = _layernorm(x) * (1.0 + sc1) + sh1
    proj = h @ w_in
    offs, acc = [], 0
    for w in IN_SPLITS[:-1]:
        acc += w
        offs.append(acc)
    rq, rk, rv, rg, dq, dkv, kr, gA, gB = jnp.split(proj, offs, axis=-1)

    rq = _apply_rope(rq.reshape(B, S, RET_HEADS, RET_QK_DIM), cos_r, sin_r)
    rk = _apply_rope(rk.reshape(B, S, RET_HEADS, RET_QK_DIM), cos_r, sin_r) * (RET_QK_DIM ** -0.5)
    rv = rv.reshape(B, S, RET_HEADS, RET_V_DIM)
    ro = _bidirectional_retention(rq, rk, rv, ret_decay_fwd, ret_decay_bwd)
    ro = _layernorm(ro).reshape(B, S, RET_V_W) * ret_gn_g
    y_a = (jax.nn.silu(rg) * ro) @ w_ret_o

    cq = _rmsnorm(dq, q_norm_g)
    qm = (cq @ w_uq).reshape(B, S, MLA_HEADS, MLA_QK)
    q_nope, q_rope = qm[..., :MLA_NOPE], _apply_rope(qm[..., MLA_NOPE:], cos_m, sin_m)
    ckv = _rmsnorm(dkv, kv_norm_g)
    k_nope = (ckv @ w_uk).reshape(B, S, MLA_HEADS, MLA_NOPE)
    v_m = (ckv @ w_uv).reshape(B, S, MLA_HEADS, MLA_V)
    k_rope = _apply_rope(kr, cos_m, sin_m)
    y_b = _mla_attention(q_nope, q_rope, k_nope, k_rope, v_m) @ w_mla_o

    merged = jax.nn.sigmoid(gA) * y_a + jax.nn.sigmoid(gB) * y_b
    f = merged @ w_out
    x = _layernorm(DEEPNORM_ALPHA * x + (1.0 + g1) * f) * ln1_g + ln1_b

    h = _layernorm(x) * (1.0 + sc2) + sh2
    a, bgate = jnp.split(h @ w_up, 2, axis=-1)
    a = _dwconv3(a, conv_w, conv_b)
    y = (jax.nn.gelu(a, approximate=False) * bgate) @ w_down
    x = _layernorm(DEEPNORM_ALPHA * x + (1.0 + g2) * y) * ln2_g + ln2_b
    return x


def setup_inputs(seed: int = 0) -> dict:
    key = jax.random.key(seed)
    ks = jax.random.split(key, 32)
    nrm = lambda k, shp, s: jax.random.normal(k, shp, jnp.float32) * s
    L, D = DEPTH, D_MODEL
    base_decay = jnp.log(2.0 ** (5.0 + jnp.arange(RET_HEADS, dtype=jnp.float32)) - 1.0)
    return {
        "x_prompt": nrm(ks[0], (BATCH, SEQ, D), 1.0),
        "x_sample": nrm(ks[1], (DEC_BATCH, DEC_SEQ, D), 1.0),
        "c_prompt": nrm(ks[2], (BATCH, D), 1.0),
        "c_sample": nrm(ks[3], (DEC_BATCH, D), 1.0),
        "w_ada": nrm(ks[4], (L, D, 6 * D), 0.1 * D ** -0.5),
        "b_ada": nrm(ks[5], (L, 6 * D), 0.01),
        "w_in": nrm(ks[6], (L, D, IN_WIDTH), D ** -0.5),
        "ret_decay_fwd": base_decay[None] + nrm(ks[7], (L, RET_HEADS), 0.01),
        "ret_decay_bwd": base_decay[None] + nrm(ks[8], (L, RET_HEADS), 0.01),
        "ret_gn_g": 1.0 + nrm(ks[9], (L, RET_V_W), 0.01),
        "w_ret_o": nrm(ks[10], (L, RET_V_W, D), RET_V_W ** -0.5),
        "q_norm_g": 1.0 + nrm(ks[11], (L, Q_LORA), 0.01),
        "kv_norm_g": 1.0 + nrm(ks[12], (L, KV_LORA), 0.01),
        "w_uq": nrm(ks[13], (L, Q_LORA, MLA_HEADS * MLA_QK), Q_LORA ** -0.5),
        "w_uk": nrm(ks[14], (L, KV_LORA, MLA_HEADS * MLA_NOPE), KV_LORA ** -0.5),
        "w_uv": nrm(ks[15], (L, KV_LORA, MLA_V_W), KV_LORA ** -0.5),
        "w_mla_o": nrm(ks[16], (L, MLA_V_W, D), MLA_V_W ** -0.5),
        "w_out": nrm(ks[17], (L, D, D), DEEPNORM_BETA * D ** -0.5),
        "ln1_g": 1.0 + nrm(ks[18], (L, D), 0.01),
        "ln1_b": nrm(ks[19], (L, D), 0.01),
        "w_up": nrm(ks[20], (L, D, 2 * D_FF), D ** -0.5),
        "conv_w": nrm(ks[21], (L, CONV_W, D_FF), CONV_W ** -0.5),
        "conv_b": nrm(ks[22], (L, D_FF), 0.01),
        "w_down": nrm(ks[23], (L, D_FF, D), DEEPNORM_BETA * D_FF ** -0.5),
        "ln2_g": 1.0 + nrm(ks[24], (L, D), 0.01),
        "ln2_b": nrm(ks[25], (L, D), 0.01),
    }


def reference(x_prompt, x_sample, c_prompt, c_sample, w_ada, b_ada, w_in, ret_decay_fwd, ret_decay_bwd,
              ret_gn_g, w_ret_o, q_norm_g, kv_norm_g, w_uq, w_uk, w_uv, w_mla_o, w_out, ln1_g, ln1_b,
              w_up, conv_w, conv_b, w_down, ln2_g, ln2_b):
    params = (w_ada, b_ada, w_in, ret_decay_fwd, ret_decay_bwd, ret_gn_g, w_ret_o, q_norm_g, kv_norm_g,
              w_uq, w_uk, w_uv, w_mla_o, w_out, ln1_g, ln1_b, w_up, conv_w, conv_b, w_down, ln2_g, ln2_b)
    y_prompt, y_sample = x_prompt, x_sample
    for l in range(DEPTH):
        layer_params = [p[l] for p in params]
        y_prompt = _encoder_layer(y_prompt, c_prompt, *layer_params)
        y_sample = _encoder_layer(y_sample, c_sample, *layer_params)
    return (y_prompt, y_sample)
```

```python
import math
from contextlib import ExitStack

import numpy as np
import concourse.bass as bass
import concourse.mybir as mybir
from concourse.bass_utils import run_bass_kernel_spmd

F32 = mybir.dt.float32
BF16 = mybir.dt.bfloat16
AF = mybir.ActivationFunctionType
ALU = mybir.AluOpType

D = 1024
KC = 8
FF = 2816
NFC = 22
INW = 5568
LN_EPS = 1e-5
RMS_EPS = 1e-6
DEPTH_FULL = 4
ALPHA = (2.0 * DEPTH_FULL) ** 0.25
NSLOT = 12
AKEY = "__arena__"


class Op:
    __slots__ = ("eng", "fn", "deps", "dma", "slot", "slotval", "signal", "sigval", "idx", "epoch", "fence", "cost", "pos")


NEP = 8
SCHED = True
import os as _os
WINDOW = int(_os.environ.get('K_WINDOW', '48'))
PRIO = True
PRIO_EPS = 1.0
LAT_NS = float(_os.environ.get('K_LAT', '120'))
ENGS = ("pe", "act", "dve", "pool", "sp")


class Prog:
    def __init__(self):
        self.ops = []
        self.lastw = {}
        self.rd = {}
        self.epoch = 0

    def add(self, eng, fn, r=(), w=(), dma=False, fence=False, c=200.0):
        op = Op()
        op.eng, op.fn, op.dma, op.idx = eng, fn, dma, len(self.ops)
        op.signal = dma
        op.sigval = 0
        op.epoch = self.epoch
        op.fence = fence
        op.cost = c
        if fence:
            r, w = (), (AKEY,)
            self.epoch += 1
        else:
            r = tuple(r) + (AKEY,)
        deps = set()
        for k in r:
            d = self.lastw.get(k)
            if d is not None:
                deps.add(d)
            if isinstance(k, tuple) and k[0] == "ps":
                for d in self.rd.get(k, ()):
                    if self.ops[d].eng != eng:
                        deps.add(d)
        for k in w:
            d = self.lastw.get(k)
            if d is not None:
                deps.add(d)
            deps.update(self.rd.get(k, ()))
        deps.discard(op.idx)
        op.deps = deps
        for k in w:
            self.lastw[k] = op.idx
            self.rd[k] = []
        for k in r:
            self.rd.setdefault(k, []).append(op.idx)
        self.ops.append(op)
        return op

    def schedule(self):
        import heapq
        ops = self.ops
        order = {e: [] for e in ENGS}
        nep = self.epoch + 1
        byep = [[] for _ in range(nep)]
        for op in ops:
            byep[op.epoch].append(op.idx)
        done = [None] * len(ops)
        LAT = LAT_NS
        for ep in range(nep):
            idxs = byep[ep]
            if not idxs:
                continue
            if not SCHED:
                for i in idxs:
                    order[ops[i].eng].append(i)
                continue
            pend = {e: [i for i in idxs if ops[i].eng == e] for e in ENGS}
            head = {e: 0 for e in ENGS}
            taken = set()
            free = {e: 0.0 for e in ENGS}
            dmapipe = 0.0
            ev = [0.0]
            remaining = len(idxs)
            ldeps = {}
            for i in idxs:
                ldeps[i] = [d for d in ops[i].deps if ops[d].epoch == ep and not ops[d].fence]
            blev = {}
            if PRIO:
                for i in reversed(idxs):
                    b_ = blev.get(i, 0.0) + ops[i].cost + (2000.0 if ops[i].dma else LAT)
                    blev[i] = b_
                    for d in ldeps[i]:
                        if blev.get(d, 0.0) < b_:
                            blev[d] = b_
            while remaining:
                T = heapq.heappop(ev)
                while ev and ev[0] <= T:
                    heapq.heappop(ev)
                for e in ENGS:
                    if free[e] > T:
                        continue
                    pl = pend[e]
                    h = head[e]
                    while h < len(pl) and pl[h] in taken:
                        h += 1
                    head[e] = h
                    if h >= len(pl):
                        continue
                    pick = None
                    scanned = 0
                    j = h
                    while j < len(pl) and scanned < WINDOW:
                        i = pl[j]
                        j += 1
                        if i in taken:
                            continue
                        scanned += 1
                        ok = True
                        for d in ldeps[i]:
                            dt_ = done[d]
                            if dt_ is None or dt_ + LAT > T:
                                ok = False
                                break
                        if ok:
                            if not PRIO:
                                pick = i
                                break
                            if pick is None or blev[i] > blev[pick] + PRIO_EPS:
                                pick = i
                        if ops[i].fence:
                            break
                    if pick is None:
                        continue
                    op = ops[pick]
                    taken.add(pick)
                    remaining -= 1
                    order[e].append(pick)
                    if op.dma:
                        issue = 100.0 if e != "pool" else 600.0
                        free[e] = T + issue
                        st = max(T + issue, dmapipe)
                        dmapipe = st + op.cost
                        done[pick] = dmapipe + 2000.0
                    else:
                        free[e] = T + op.cost
                        done[pick] = T + op.cost
                    heapq.heappush(ev, free[e])
                    heapq.heappush(ev, done[pick] + LAT)
                if not ev and remaining:
                    raise RuntimeError("scheduler stuck")
        return order

    def finalize(self):
        ops = self.ops
        order = self.schedule()
        self.order = order
        for e in ENGS:
            for p, i in enumerate(order[e]):
                ops[i].pos = p
        dmak = {}
        for e in ENGS:
            for i in order[e]:
                op = ops[i]
                if op.dma:
                    k = dmak.get(op.eng, 0)
                    op.slot = (op.eng, k % NSLOT)
                    op.slotval = 16 * (k // NSLOT + 1)
                    dmak[op.eng] = k + 1
        lastfence = None
        fences = {}
        for op in ops:
            if op.fence:
                fences[op.epoch] = op.idx
        for op in ops:
            best = {}
            bestd = {}
            for d in op.deps:
                o = ops[d]
                if o.fence or o.epoch != op.epoch:
                    continue
                if o.dma:
                    if o.slot not in bestd or ops[bestd[o.slot]].slotval < o.slotval:
                        bestd[o.slot] = d
                else:
                    if o.eng == "pe" and op.eng == "pe" and not op.dma:
                        continue
                    if o.eng not in best or ops[best[o.eng]].pos < o.pos:
                        best[o.eng] = d
            op.deps = list(best.values()) + list(bestd.values())
            if op.epoch > 0:
                op.deps = [fences[op.epoch - 1]] + op.deps
            for d in op.deps:
                ops[d].signal = True
        cnt = {}
        nf = 0
        for op in ops:
            if op.fence:
                nf += 1
                op.sigval = nf
        for e in ENGS:
            for i in order[e]:
                op = ops[i]
                if not op.fence and not op.dma and op.signal:
                    key = (op.eng, op.epoch % NEP)
                    cnt[key] = cnt.get(key, 0) + 1
                    op.sigval = cnt[key]
        self.sigcounts = cnt
        self.dmacounts = dmak

    def emit(self, nc, es):
        self.finalize()
        ops = self.ops
        csem = {(e, j): es.enter_context(nc.semaphore("c_%s%d" % (e, j))) for e in ENGS for j in range(NEP)}
        fsem = es.enter_context(nc.semaphore("fence"))
        dsem = {}
        for q in ENGS:
            if self.dmacounts.get(q, 0) > 0:
                for s in range(NSLOT):
                    dsem[(q, s)] = es.enter_context(nc.semaphore("d_%s%d" % (q, s)))
        block = es.enter_context(nc.Block())
        order = self.order

        def run(e, eng):
            seen = {}
            for i in order[e]:
                op = ops[i]
                for d in op.deps:
                    o = ops[d]
                    if o.fence:
                        key, sem, val = "fence", fsem, o.sigval
                    elif o.dma:
                        key, sem, val = ("d",) + o.slot, dsem[o.slot], o.slotval
                    else:
                        key = ("c", o.eng, o.epoch % NEP)
                        sem, val = csem[key[1:]], o.sigval
                    if seen.get(key, 0) < val:
                        eng.wait_ge(sem, val)
                        seen[key] = val
                if op.dma:
                    prev = op.slotval - 16
                    if prev > 0 and seen.get(("d",) + op.slot, 0) < prev:
                        eng.wait_ge(dsem[op.slot], prev)
                        seen[("d",) + op.slot] = prev
                ins = op.fn(eng)
                if ins is None:
                    continue
                if op.fence:
                    ins.then_inc(fsem, 1)
                elif op.dma:
                    ins.then_inc(dsem[op.slot], 16)
                elif op.signal:
                    ins.then_inc(csem[(op.eng, op.epoch % NEP)], 1)

        @block.tensor
        def _(eng):
            run("pe", eng)

        @block.scalar
        def _(eng):
            run("act", eng)

        @block.vector
        def _(eng):
            run("dve", eng)

        @block.gpsimd
        def _(eng):
            run("pool", eng)

        @block.sync
        def _(eng):
            run("sp", eng)


_uid = [0]


class Arena:
    def __init__(self, nc, lo, hi):
        self.nc, self.lo, self.hi, self.top = nc, lo, hi, lo

    def alloc(self, shape, dt, name="t"):
        esz = 4 if dt == F32 else 2
        nb = esz
        for s in shape[1:]:
            nb *= s
        nb = (nb + 31) // 32 * 32
        off = self.top
        self.top += nb
        assert self.top <= self.hi, "SBUF arena overflow: %s %s need=%d over=%d" % (name, shape, nb, self.top - self.hi)
        _uid[0] += 1
        return self.nc.alloc_sbuf_tensor_at("%s_%d" % (name, _uid[0]), list(shape), dt, offset=off)

    def reset(self):
        self.top = self.lo


class Ring:
    def __init__(self, ar, n, shape, dt, name):
        self.t = [ar.alloc(shape, dt, name) for _ in range(n)]
        self.i = -1
        _uid[0] += 1
        self.name = "%s#%d" % (name, _uid[0])

    def next(self):
        self.i += 1
        j = self.i % len(self.t)
        return self.t[j], (self.name, j)


def build(NSEQ, S, L):
    NT = S // 128
    QB = min(512, S)
    NQB = S // QB
    TPQ = QB // 128
    C = 128
    nc = bass.Bass("TRN2", target_bir_lowering=False)

    def din(name, shape):
        return nc.dram_tensor(name, list(shape), F32, kind="ExternalInput").ap()

    x_d = din("x", [NSEQ, S, D])
    cT_d = din("cT", [128, KC, NSEQ])
    wada_d = din("w_ada", [L, 128, KC, 6 * D])
    bada_d = din("b_ada", [L, 6 * D])
    bcol_d = din("b_adacol", [128, L, 48])
    win_d = din("w_in", [L, 128, KC, INW])
    dec_d = din("dec", [L, 8])
    gn_d = din("gncol", [L, 128, 8])
    wro_d = din("w_ret_o", [L, 128, KC, D])
    qg_d = din("qgcol", [L, 128, 2])
    kvg_d = din("kvgcol", [L, 128, 1])
    wuq_d = din("w_uq", [L, 128, 2, 1536])
    wuk_d = din("w_uk", [L, 128, 1024])
    wuv_d = din("w_uv", [L, 128, 1024])
    wmo_d = din("w_mla_o", [L, 128, KC, D])
    wout_d = din("w_out", [L, 128, KC, D])
    lnr_d = din("lnrows", [L, 4, D])
    wup_d = din("w_up", [L, 128, KC, 2 * FF])
    convc_d = din("convcol", [L, 128, NFC, 4])
    wdn_d = din("w_down", [L, 128, NFC, D])
    ident_d = din("ident", [128, 128])
    cosr_d = din("cosr", [128, NT, 64])
    ssr_d = din("ssr", [128, NT, 128])
    cosm_d = din("cosm", [128, NT, 32])
    ssm_d = din("ssm", [128, NT, 64])
    cst4_d = din("cst4", [128, 4, 128])
    dexp_d = din("dexp", [128, 4])
    y_d = nc.dram_tensor("y", [NSEQ, S, D], F32, kind="ExternalOutput").ap()
    adag_d = nc.dram_tensor("adag_scr", [NSEQ, L, 2, D], F32, kind="Internal").ap()
    M_d = nc.dram_tensor("M_scr", [KC, 128, S], BF16, kind="Internal").ap()
    G_d = nc.dram_tensor("G_scr", [NFC, 128, S], BF16, kind="Internal").ap()

    P = Prog()
    ps = [nc.alloc_psum_tensor("ps%d" % i, [128, 512], F32) for i in range(8)]

    def pk(b):
        return ("ps", b)

    LO, HI = 16512, 229376
    XB = NT * 1024 * 4
    HB = max(8 * S * 2, 32768)
    off = LO
    arX = Arena(nc, off, off + XB); off += XB
    arH = Arena(nc, off, off + HB); off += HB
    arR = Arena(nc, off, off + HB); off += HB
    arC = Arena(nc, off, off + 26 * 1024); off += 26 * 1024
    arT = Arena(nc, off, HI)
    X = arX.alloc([128, NT, 1024], F32, "X")
    HT = arH.alloc([128, 8, S], BF16, "HT")
    R2 = arR.alloc([128, 8, S], BF16, "R2")
    ident = arC.alloc([128, 128], BF16, "ident")
    ones = arC.alloc([128, 128], F32, "ones")
    onesb = arC.alloc([128, 128], BF16, "onesb")
    cosr = arC.alloc([128, NT, 64], F32, "cosr")
    ssr = arC.alloc([128, NT, 128], F32, "ssr")
    cosm = arC.alloc([128, NT, 32], F32, "cosm")
    ssm = arC.alloc([128, NT, 64], F32, "ssm")
    cst4 = arC.alloc([128, 4, 128], F32, "cst4")
    dexp = arC.alloc([128, 4], F32, "dexp")
    cvals = arC.alloc([128, 4], F32, "cvals")
    adaT = arC.alloc([128, L, 4, 8, NSEQ], F32, "adaT")
    siluT = arC.alloc([128, KC, NSEQ], BF16, "siluT")
    cTs = arC.alloc([128, KC, NSEQ], F32, "cTs")
    dec = arC.alloc([128, 8], F32, "dec")
    lg = arC.alloc([128, 8], F32, "lg")
    gC = arC.alloc([128, 8], F32, "gC")
    dcol = arC.alloc([128, 4, 4], F32, "dcol")
    kplain = arC.alloc([128, 1], F32, "kplain")
    maskT = arC.alloc([128, 4, 128], F32, "maskT")
    gncol = arC.alloc([128, 8], F32, "gncol")
    qgcol = arC.alloc([128, 2], F32, "qgcol")
    kvgcol = arC.alloc([128, 1], F32, "kvgcol")
    stat = [arC.alloc([128, 24], F32, "stat%d" % i) for i in range(4)]
    stat_i = [-1]

    def nstat():
        stat_i[0] += 1
        j = stat_i[0] % 4
        return stat[j], ("stat", j)

    def mm(out, lhsT, rhs, st, sp, r, w):
        P.add("pe", lambda e: e.matmul(out, lhsT=lhsT, rhs=rhs, start=st, stop=sp), r=r, w=w, c=0.5 * rhs.free_size() + 30.0)

    def tp(out, in_, n, r, w):
        P.add("pe", lambda e: e.transpose(out, in_, ident[0:n, 0:n]), r=tuple(r) + ("ident",), w=w, c=100.0)

    def act(out, in_, func, r, w, bias=None, scale=None, accum=None):
        kw = {}
        if bias is not None:
            kw["bias"] = bias
        if scale is not None:
            kw["scale"] = scale
        if accum is not None:
            kw["accum_out"] = accum
        P.add("act", lambda e: e.activation(out=out, in_=in_, func=func, **kw), r=r, w=w, c=out.free_size() / 1.2 + 250.0)

    def tt(out, in0, in1, op, r, w, eng="dve"):
        P.add(eng, lambda e: e.tensor_tensor(out=out, in0=in0, in1=in1, op=op), r=r, w=w,
              c=out.free_size() / (0.9 if eng == "dve" else 0.45) + 150.0)

    def ts(out, in0, s1, s2, op0, op1, r, w):
        if s2 is None:
            P.add("dve", lambda e: e.tensor_scalar(out=out, in0=in0, scalar1=s1, scalar2=None, op0=op0), r=r, w=w, c=out.free_size() / 0.9 + 150.0)
        else:
            P.add("dve", lambda e: e.tensor_scalar(out=out, in0=in0, scalar1=s1, scalar2=s2, op0=op0, op1=op1), r=r, w=w, c=out.free_size() / 0.9 + 150.0)

    def stt(out, in0, sc, in1, op0, op1, r, w, eng="dve"):
        P.add(eng, lambda e: e.scalar_tensor_tensor(out=out, in0=in0, scalar=sc, in1=in1, op0=op0, op1=op1), r=r, w=w,
              c=out.free_size() / (0.9 if eng == "dve" else 0.45) + 150.0)

    def cp(out, in_, r, w, eng="dve"):
        if eng == "act":
            act(out, in_, AF.Copy, r, w)
            return
        k = 0.9 if eng == "dve" else 0.45
        P.add(eng, lambda e: e.tensor_copy(out=out, in_=in_), r=r, w=w, c=out.free_size() / k + 150.0)

    def dma(q, out, in_, r, w):
        P.add(q, lambda e: e.dma_start(out=out, in_=in_), r=r, w=w, dma=True, c=out.free_nbytes() * out.partition_size() / 120.0 + 300.0)

    def fence():
        P.add("sp", lambda e: e.nop(), fence=True)

    def rsq(out, in_, n, r, w):
        P.add("pool", lambda e: e.tensor_tensor(out=out, in0=in_, in1=cvals[:, 2:3], op=ALU.pow),
              r=tuple(r) + ("cvals",), w=w, c=600.0)

    def ln_stats(src_ap, srckeys, eps_col):
        stt_, sk = nstat()
        P.add("dve", lambda e: e.bn_stats(out=stt_[:, 0:6], in_=src_ap[:, 0:512]), r=srckeys, w=[sk])
        P.add("dve", lambda e: e.bn_stats(out=stt_[:, 6:12], in_=src_ap[:, 512:1024]), r=list(srckeys) + [sk], w=[sk])
        P.add("dve", lambda e: e.bn_aggr(out=stt_[:, 12:14], in_=stt_[:, 0:12].rearrange("p (a b) -> p a b", a=2)), r=[sk], w=[sk])
        ts(stt_[:, 14:15], stt_[:, 13:14], LN_EPS, None, ALU.add, None, [sk], [sk])
        rsq(stt_[:, 16:17], stt_[:, 14:15], 1, [sk], [sk])
        stt(stt_[:, 17:18], stt_[:, 12:13], -1.0, stt_[:, 16:17], ALU.mult, ALU.mult, [sk], [sk])
        return stt_, sk

    dma("pool", ident[:], ident_d, [], ["ident"])
    dma("sp", cosr[:], cosr_d, [], ["rope"])
    dma("sp", ssr[:], ssr_d, [], ["rope"])
    dma("sp", cosm[:], cosm_d, [], ["rope"])
    dma("sp", ssm[:], ssm_d, [], ["rope"])
    dma("sp", cst4[:], cst4_d, [], ["cst4"])
    dma("sp", dexp[:], dexp_d, [], ["dexp"])
    dma("sp", cTs[:], cT_d, [], ["cTs"])
    P.add("dve", lambda e: e.memset(ones[:], 1.0), w=["ones"])
    P.add("dve", lambda e: e.memset(onesb[:], 1.0), w=["onesb"])
    P.add("dve", lambda e: e.memset(cvals[:, 0:1], LN_EPS), w=["cvals"])
    P.add("dve", lambda e: e.memset(cvals[:, 1:2], RMS_EPS), r=["cvals"], w=["cvals"])
    P.add("dve", lambda e: e.memset(cvals[:, 2:3], -0.5), r=["cvals"], w=["cvals"])
    P.add("dve", lambda e: e.memset(cvals[:, 3:4], 1.0), r=["cvals"], w=["cvals"])
    P.add("dve", lambda e: e.memset(kplain[:], 128.0 ** -0.5), w=["kplain"])
    act(siluT[:], cTs[:], AF.Silu, ["cTs"], ["siluT"])

    bcol = arT.alloc([128, L, 48], F32, "bcol")
    browR = Ring(arT, 2, [1, 512], F32, "brow")
    dma("sp", bcol[:], bcol_d, [], ["bcol"])
    waR = Ring(arT, 2, [128, KC, 512], BF16, "wa")
    growR = Ring(arT, 2, [1, 512], F32, "grow")
    pr = [0]
    for l in range(L):
        for nb in range(12):
            wa, wak = waR.next()
            dma("pool", wa[:], wada_d[l, :, :, nb * 512:(nb + 1) * 512], [], [wak])
            blk = nb // 2
            if blk in (2, 5):
                brow, browk = browR.next()
                dma("sp", brow[:], bada_d[l:l + 1, nb * 512:(nb + 1) * 512], [], [browk])
                for s in range(NSEQ):
                    b = pr[0] % 4
                    pr[0] += 1
                    for kc in range(KC):
                        mm(ps[b][0:1, :], siluT[:, kc, s:s + 1], wa[:, kc, :], kc == 0, kc == KC - 1, [wak, "siluT"], [pk(b)])
                    gr, grk = growR.next()
                    n0 = nb * 512
                    tt(gr[:], ps[b][0:1, :], brow[:], ALU.add, [pk(b), browk], [grk])
                    ts(gr[:], gr[:], 1.0, None, ALU.add, None, [grk], [grk])
                    wi = 0 if blk == 2 else 1
                    hf = nb % 2
                    dma("sp", adag_d[s, l, wi:wi + 1, hf * 512:(hf + 1) * 512], gr[:], [grk], [("adag", s, l, wi)])
            else:
                wi = {0: 0, 1: 1, 3: 2, 4: 3}[blk]
                b = 4 + pr[0] % 4
                pr[0] += 1
                for j in range(4):
                    for kc in range(KC):
                        mm(ps[b][:, j * 8:j * 8 + NSEQ], wa[:, kc, j * 128:(j + 1) * 128], siluT[:, kc, :], kc == 0, kc == KC - 1,
                           [wak, "siluT"], [pk(b)])
                for j in range(4):
                    fb = (nb % 2) * 4 + j
                    cb = nb * 4 + j
                    act(adaT[:, l, wi, fb, :], ps[b][:, j * 8:j * 8 + NSEQ], AF.Identity, [pk(b), "bcol"], ["adaT"],
                        bias=bcol[:, l, cb:cb + 1], scale=1.0)
    for l in range(L):
        for wi in (1, 3):
            ts(adaT[:, l, wi], adaT[:, l, wi], 1.0, None, ALU.add, None, ["adaT"], ["adaT"])
    fence()
    arT.reset()

    def h_tile(s, l, which, t, xnR, banks):
        shw, scw = (0, 1) if which == 0 else (2, 3)
        st_, sk = ln_stats(X[:, t, :], [("X", t)], 0)
        xn, xk = xnR.next()
        act(xn[:], X[:, t, :], AF.Identity, [("X", t), sk], [xk], bias=st_[:, 17:18], scale=st_[:, 16:17])
        b = banks[t % len(banks)]
        pb = ps[b][:].bitcast(BF16)
        for kc in range(KC):
            tp(pb[:, kc * 128:(kc + 1) * 128], xn[:, kc * 128:(kc + 1) * 128], 128, [xk], [pk(b)])
        for kc in range(KC):
            act(HT[:, kc, t * 128:(t + 1) * 128], pb[:, kc * 128:(kc + 1) * 128], AF.Identity, [pk(b), "adaT"], [("HT", t)],
                bias=adaT[:, l, shw, kc, s:s + 1], scale=adaT[:, l, scw, kc, s:s + 1])

    def phase_H(s, l, which):
        xnR = Ring(arT, 3, [128, 1024], BF16, "xn")
        for t in range(NT):
            h_tile(s, l, which, t, xnR, (0, 1, 2, 3))
        fence()
        arT.reset()

    def layer_consts(l):
        dma("sp", dec[:], dec_d[l:l + 1, :].partition_broadcast(128), [], ["dec"])
        dma("sp", gncol[:], gn_d[l], [], ["gncol"])
        dma("sp", qgcol[:], qg_d[l], [], ["qgcol"])
        dma("sp", kvgcol[:], kvg_d[l], [], ["kvgcol"])
        act(lg[:], dec[:], AF.Sigmoid, ["dec"], ["lg"])
        act(lg[:], lg[:], AF.Ln, ["lg"], ["lg"])
        act(gC[:], lg[:], AF.Exp, ["lg"], ["gC"], scale=float(C))
        e1 = arT.alloc([128, 128], F32, "e1")
        e2 = arT.alloc([128, 128], F32, "e2")
        for h in range(4):
            act(e1[:], cst4[:, 0, :], AF.Exp, ["cst4", "lg"], ["e1"], scale=lg[:, h:h + 1])
            tt(e1[:], e1[:], cst4[:, 2, :], ALU.mult, ["e1", "cst4"], ["e1"])
            act(e2[:], cst4[:, 1, :], AF.Exp, ["cst4", "lg"], ["e2"], scale=lg[:, 4 + h:5 + h])
            tt(e2[:], e2[:], cst4[:, 3, :], ALU.mult, ["e2", "cst4"], ["e2"])
            tt(maskT[:, h, :], e1[:], e2[:], ALU.add, ["e1", "e2"], ["maskT"])
            act(dcol[:, h, 0:1], dexp[:, 0:1], AF.Exp, ["dexp", "lg"], ["dcol"], scale=lg[:, h:h + 1])
            act(dcol[:, h, 1:2], dexp[:, 1:2], AF.Exp, ["dexp", "lg"], ["dcol"], scale=lg[:, 4 + h:5 + h])
            act(dcol[:, h, 2:3], dexp[:, 2:3], AF.Exp, ["dexp", "lg"], ["dcol"], scale=lg[:, h:h + 1])
            act(dcol[:, h, 3:4], dexp[:, 3:4], AF.Exp, ["dexp", "lg"], ["dcol"], scale=lg[:, 4 + h:5 + h])
            ts(dcol[:, h, 2:4], dcol[:, h, 2:4], 128.0 ** -0.5, None, ALU.mult, None, ["dcol"], ["dcol"])
        fence()
        arT.reset()

    def rope(dst, src, cos_b, ss, half, nh, tmpA, tmpB, r, w, kA, kB, addeng="dve"):
        tt(tmpA, src, cos_b, ALU.mult, list(r) + ["rope"], [kA])
        tt(tmpB[:, :, 0, :], src[:, :, 1, :], ss[:, :, 0, :], ALU.mult, list(r) + ["rope"], [kB])
        tt(tmpB[:, :, 1, :], src[:, :, 0, :], ss[:, :, 1, :], ALU.mult, list(r) + ["rope", kB], [kB])
        tt(dst, tmpA, tmpB, ALU.add, [kA, kB], w, eng=addeng)

    def phase_mla(s, l):
        arT.reset()
        cqT = arT.alloc([128, 2, S], BF16, "cqT")
        ckvT = arT.alloc([128, S], BF16, "ckvT")
        krT = arT.alloc([128, 2, S], BF16, "krT")
        qrT = arT.alloc([128, 4, S], BF16, "qrT")
        P.add("dve", lambda e: e.memset(krT[:], 0.0), w=["krT0"], c=2 * S / 0.9 + 150.0)
        base_top = arT.top
        wl = arT.alloc([128, KC, 448], BF16, "wl")
        wuqr = arT.alloc([128, 2, 8, 64], BF16, "wuqr")
        dma("pool", wl[:], win_d[l, :, :, 3072:3520], [], ["wl"])
        dma("pool", wuqr[:], wuq_d[l].rearrange("p k (h d) -> p k h d", h=8)[:, :, :, 128:192], [], ["wuqr"])
        latR = Ring(arT, 2, [128, 512], BF16, "lat")
        tAR = Ring(arT, 1, [128, 512], F32, "tA")
        tBR = Ring(arT, 1, [128, 512], F32, "tB")
        sqj = arT.alloc([128, 256], BF16, "sqj")
        qrtR = Ring(arT, 2, [128, 512], BF16, "qrt")
        for t in range(NT):
            b = 0 + (t % 2)
            for kc in range(KC):
                mm(ps[b][:, 0:448], HT[:, kc, t * 128:(t + 1) * 128], wl[:, kc, :], kc == 0, kc == KC - 1, [("HT", t), "wl"], [pk(b)])
            st_, sk = nstat()
            act(sqj[:, 0:256], ps[b][:, 0:256], AF.Square, [pk(b)], ["sqj", sk], accum=st_[:, 0:1])
            act(sqj[:, 0:128], ps[b][:, 256:384], AF.Square, [pk(b), sk], ["sqj", sk], accum=st_[:, 1:2])
            ts(st_[:, 2:3], st_[:, 0:1], 1.0 / 256.0, RMS_EPS, ALU.mult, ALU.add, [sk], [sk])
            ts(st_[:, 3:4], st_[:, 1:2], 1.0 / 128.0, RMS_EPS, ALU.mult, ALU.add, [sk], [sk])
            rsq(st_[:, 4:5], st_[:, 2:3], 1, [sk], [sk])
            rsq(st_[:, 5:6], st_[:, 3:4], 1, [sk], [sk])
            lat, lk = latR.next()
            act(lat[:, 0:256], ps[b][:, 0:256], AF.Identity, [pk(b), sk], [lk], scale=st_[:, 4:5], bias=0.0)
            act(lat[:, 256:384], ps[b][:, 256:384], AF.Identity, [pk(b), sk, lk], [lk], scale=st_[:, 5:6], bias=0.0)
            tA, tAk = tAR.next()
            tB, tBk = tBR.next()
            v4 = lambda ap: ap.rearrange("p (h a d) -> p h a d", h=1, a=2)
            cosb = cosm[:, t, :].unsqueeze(1).unsqueeze(1).broadcast_to([128, 1, 2, 32])
            ssb = ssm[:, t, :].rearrange("p (h a d) -> p h a d", h=1, a=2)
            rope(v4(lat[:, 384:448]), v4(ps[b][:, 384:448]), cosb, ssb, 32, 1, v4(tA[:, 0:64]), v4(tB[:, 0:64]),
                 [pk(b), lk], [lk], tAk, tBk)
            cp(lat[:, 448:512], lat[:, 384:448], [lk], [lk])
            b2 = 2 + (t % 2)
            pb = ps[b2][:].bitcast(BF16)
            for j in range(4):
                tp(pb[:, j * 128:(j + 1) * 128], lat[:, j * 128:(j + 1) * 128], 128, [lk], [pk(b2)])
            tsl = slice(t * 128, (t + 1) * 128)
            act(cqT[:, 0, tsl], pb[:, 0:128], AF.Identity, [pk(b2), "qgcol"], [("cqT", t)], scale=qgcol[:, 0:1], bias=0.0)
            act(cqT[:, 1, tsl], pb[:, 128:256], AF.Identity, [pk(b2), "qgcol", ("cqT", t)], [("cqT", t)], scale=qgcol[:, 1:2], bias=0.0)
            act(ckvT[:, tsl], pb[:, 256:384], AF.Identity, [pk(b2), "kvgcol"], [("ckvT", t)], scale=kvgcol[:, 0:1], bias=0.0)
            cp(krT[0:64, 0, tsl], pb[0:64, 384:512], [pk(b2), "krT0"], [("krT", t)])
            cp(krT[64:128, 1, tsl], pb[64:128, 384:512], [pk(b2), "krT0", ("krT", t)], [("krT", t)])
            b3 = 4 + (t % 2)
            for kc in range(2):
                mm(ps[b3][:, 0:512], cqT[:, kc, tsl], wuqr[:, kc].rearrange("p h d -> p (h d)"), kc == 0, kc == 1,
                   [("cqT", t), "wuqr"], [pk(b3)])
            tA, tAk = tAR.next()
            tB, tBk = tBR.next()
            qrt, qk = qrtR.next()
            v8 = lambda ap: ap.rearrange("p (h a d) -> p h a d", h=8, a=2)
            cosb8 = cosm[:, t, :].unsqueeze(1).unsqueeze(1).broadcast_to([128, 8, 2, 32])
            ssb8 = ssm[:, t, :].rearrange("p (a d) -> p a d", a=2).unsqueeze(1).broadcast_to([128, 8, 2, 32])
            rope(v8(qrt[:]), v8(ps[b3][:, 0:512]), cosb8, ssb8, 32, 8, v8(tA[:]), v8(tB[:]), [pk(b3)], [qk], tAk, tBk)
            b4 = 6 + (t % 2)
            pb4 = ps[b4][:].bitcast(BF16)
            for j in range(4):
                tp(pb4[:, j * 128:(j + 1) * 128], qrt[:, j * 128:(j + 1) * 128], 128, [qk], [pk(b4)])
            cp(qrT[:, :, tsl], pb4[:, 0:512].rearrange("p (j t) -> p j t", j=4), [pk(b4)], [("qrT", t)])
        fence()
        arT.top = base_top
        wkvR = Ring(arT, 1, [128, 2, 128], BF16, "wkv")
        wqnR = Ring(arT, 2, [128, 2, 128], BF16, "wqn")
        knT = arT.alloc([128, S], BF16, "knT")
        vh = arT.alloc([128, NT, 128], BF16, "vh")
        qnR = Ring(arT, 1, [128, QB], BF16, "qn")
        ptR = Ring(arT, 5, [128, QB], BF16, "pt")
        accR = Ring(arT, 1, [128, QB], F32, "acc")
        scale = 192.0 ** -0.5
        sc_i = [0]
        allT = [("ckvT", t) for t in range(NT)]
        for h in range(8):
            wqn, wqk = wqnR.next()
            wkv, wkvk = wkvR.next()
            dma("pool", wqn[:], wuq_d[l, :, :, h * 192:h * 192 + 128], [], [wqk])
            dma("pool", wkv[:, 0, :], wuk_d[l, :, h * 128:(h + 1) * 128], [], [wkvk])
            dma("pool", wkv[:, 1, :], wuv_d[l, :, h * 128:(h + 1) * 128], [], [wkvk])
            for qb in range(NQB):
                qs = slice(qb * QB, (qb + 1) * QB)
                mm(ps[7][:, 0:QB], wkv[:, 0, :], ckvT[:, qs], True, True, allT + [wkvk], [pk(7)])
                cp(knT[:, qs], ps[7][:, 0:QB], [pk(7)], ["knT"])
            for t4 in range(0, NT, 4):
                n4 = min(4, NT - t4)
                for j in range(n4):
                    t = t4 + j
                    mm(ps[7][:, j * 128:(j + 1) * 128], ckvT[:, t * 128:(t + 1) * 128], wkv[:, 1, :], True, True,
                       allT + [wkvk], [pk(7)])
                act(vh[:, t4:t4 + n4, :], ps[7][:, 0:n4 * 128].rearrange("p (j d) -> p j d", j=n4), AF.Copy, [pk(7)], ["vh"])
            pb_ = (h % 2) * 64
            for qb in range(NQB):
                qs = slice(qb * QB, (qb + 1) * QB)
                qn, qnk = qnR.next()
                for kc in range(2):
                    mm(ps[7][:, 0:QB], wqn[:, kc, :], cqT[:, kc, qs], kc == 0, kc == 1, [("cqT", t) for t in range(NT)] + [wqk], [pk(7)])
                cp(qn[:], ps[7][:, 0:QB], [pk(7)], [qnk])
                bo = 3 + (qb % 2)
                bd = 5 + (qb % 2)
                acc, acck = accR.next()
                odd = [k_ for k_ in range(NT) if k_ % 2 == 1]
                pt1 = None
                for kt in range(NT):
                    bs = sc_i[0] % 3
                    sc_i[0] += 1
                    ks = slice(kt * 128, (kt + 1) * 128)
                    mm(ps[bs][:, 0:QB], knT[:, ks], qn[:], True, False, ["knT", qnk], [pk(bs)])
                    mm(ps[bs][:, 0:QB], krT[:, h % 2, ks], qrT[:, h // 2, qs], False, True,
                       [("krT", kt)] + [("qrT", t) for t in range(NT)], [pk(bs)])
                    pt, ptk = ptR.next()
                    act(pt[:], ps[bs][:, 0:QB], AF.Exp, [pk(bs)], [ptk], scale=scale)
                    mm(ps[bo][:, 0:QB], vh[:, kt, :], pt[:], kt == 0, kt == NT - 1, ["vh", ptk], [pk(bo)])
                    if kt % 2 == 0:
                        mm(ps[bd][:, 0:QB], onesb[:], pt[:], kt == 0, False, ["onesb", ptk], [pk(bd)])
                    elif len(odd) == 1:
                        cp(acc[:], pt[:], [ptk], [acck])
                    elif kt == 1:
                        pt1, pt1k = pt, ptk
                    elif kt == 3:
                        tt(acc[:], pt1[:], pt[:], ALU.add, [pt1k, ptk], [acck])
                    else:
                        tt(acc[:], acc[:], pt[:], ALU.add, [ptk, acck], [acck])
                mm(ps[bd][:, 0:QB], ones[:], acc[:], False, True, ["ones", acck], [pk(bd)])
                P.add("dve", lambda e, acc=acc, bd=bd: e.reciprocal(out=acc[:], in_=ps[bd][:, 0:QB]), r=[pk(bd)], w=[acck], c=700.0)
                tt(R2[:, h, qs], ps[bo][:, 0:QB], acc[:], ALU.mult, [pk(bo), acck], [("R2", h)])
        fence()
        arT.reset()

    keep = {}

    def phase_gate(l, wsrc_d, gcol0, first):
        arT.reset()
        if not first:
            keep["wout"] = arT.alloc([128, KC, D], BF16, "wout")
            dma("pool", keep["wout"][:], wout_d[l], [], ["wout"])
        keep_top = arT.top
        woR = Ring(arT, 2, [128, KC, 128], BF16, "wo")
        wgR = Ring(arT, 2, [128, KC, 128], BF16, "wg")
        sgR = Ring(arT, 2, [128, QB], F32, "sg")
        mbR = Ring(arT, 2, [128, QB], BF16, "mb")
        mpR = Ring(arT, 2, [128, QB], BF16, "mp")
        i = 0
        allH = [("HT", t) for t in range(NT)]
        allR = [("R2", h) for h in range(8)]
        for c in range(8):
            wo, wok = woR.next()
            wg, wgk = wgR.next()
            dma("pool", wo[:], wsrc_d[l, :, :, c * 128:(c + 1) * 128], [], [wok])
            dma("pool", wg[:], win_d[l, :, :, gcol0 + c * 128:gcol0 + (c + 1) * 128], [], [wgk])
            for qb in range(NQB):
                qs = slice(qb * QB, (qb + 1) * QB)
                by = (i % 2)
                bg = 2 + (i % 2)
                i += 1
                for kc in range(KC):
                    mm(ps[by][:, 0:QB], wo[:, kc, :], R2[:, kc, qs], kc == 0, kc == KC - 1, allR + [wok], [pk(by)])
                for kc in range(KC):
                    mm(ps[bg][:, 0:QB], wg[:, kc, :], HT[:, kc, qs], kc == 0, kc == KC - 1, allH + [wgk], [pk(bg)])
                sg, sgk = sgR.next()
                act(sg[:], ps[bg][:, 0:QB], AF.Sigmoid, [pk(bg)], [sgk])
                mb, mbk = mbR.next()
                if first:
                    tt(mb[:], ps[by][:, 0:QB], sg[:], ALU.mult, [pk(by), sgk], [mbk])
                else:
                    mp, mpk = mpR.next()
                    dma("sp", mp[:], M_d[c, :, qs], [("M", c, qb)], [mpk])
                    tt(sg[:], ps[by][:, 0:QB], sg[:], ALU.mult, [pk(by), sgk], [sgk])
                    tt(mb[:], sg[:], mp[:], ALU.add, [sgk, mpk], [mbk])
                dma("sp", M_d[c, :, qs], mb[:], [mbk], [("M", c, qb)])
        fence()
        arT.top = keep_top

    def phase_ret(s, l):
        arT.reset()
        wr = arT.alloc([128, KC, 768], BF16, "wr")
        kT = arT.alloc([128, S], BF16, "kT")
        kf = arT.alloc([128, NT, 128], BF16, "kf")
        vtm = arT.alloc([128, NT, 256], BF16, "vtm")
        Sb = arT.alloc([128, NT, 256], BF16, "Sb")
        Rf = arT.alloc([128, 256], F32, "Rf")
        Rb = arT.alloc([128, 256], F32, "Rb")
        SfR = Ring(arT, 2, [128, 256], BF16, "Sf")
        tAR = Ring(arT, 2, [128, 128], F32, "rA")
        tBR = Ring(arT, 2, [128, 128], F32, "rB")
        krR = Ring(arT, 2, [128, 128], F32, "kr")
        k3R = Ring(arT, 2, [128, 3, 128], BF16, "k3")
        q3TR = Ring(arT, 2, [128, 3, 128], BF16, "q3T")
        sgR = Ring(arT, 2, [128, 256], BF16, "sgr")
        ptR = Ring(arT, 2, [128, 128], BF16, "ptr")
        ronR = Ring(arT, 2, [128, 256], F32, "ron")
        abR = Ring(arT, 2, [128, 256], BF16, "ab")
        v3 = lambda ap: ap.rearrange("p (h a d) -> p h a d", h=1, a=2)
        for h in range(4):
            dma("pool", wr[:, :, 0:128], win_d[l, :, :, h * 128:(h + 1) * 128], [], ["wr"])
            dma("pool", wr[:, :, 128:384], win_d[l, :, :, 2048 + h * 256:2048 + (h + 1) * 256], [], ["wr"])
            dma("pool", wr[:, :, 384:512], win_d[l, :, :, 512 + h * 128:512 + (h + 1) * 128], [], ["wr"])
            dma("pool", wr[:, :, 512:768], win_d[l, :, :, 1024 + h * 256:1024 + (h + 1) * 256], [], ["wr"])
            P.add("dve", lambda e: e.memset(Rf[:], 0.0), w=["Rf"])
            P.add("dve", lambda e: e.memset(Rb[:], 0.0), w=["Rb"])
            for n in range(NT - 1, -1, -1):
                ns = slice(n * 128, (n + 1) * 128)
                b = n % 2
                for kc in range(KC):
                    mm(ps[b][:, 0:384], HT[:, kc, ns], wr[:, kc, 384:768], kc == 0, kc == KC - 1, [("HT", n), "wr"], [pk(b)])
                tA, tAk = tAR.next()
                tB, tBk = tBR.next()
                kr, krk = krR.next()
                cosb = cosr[:, n, :].unsqueeze(1).unsqueeze(1).broadcast_to([128, 1, 2, 64])
                ssb = ssr[:, n, :].rearrange("p (h a d) -> p h a d", h=1, a=2)
                rope(v3(kr[:]), v3(ps[b][:, 0:128]), cosb, ssb, 64, 1, v3(tA[:]), v3(tB[:]), [pk(b)], [krk], tAk, tBk)
                k3, k3k = k3R.next()
                act(k3[:, 0, :], kr[:], AF.Identity, [krk, "kplain"], [k3k], scale=kplain[:, 0:1], bias=0.0)
                act(kf[:, n, :], kr[:], AF.Identity, [krk, "dcol"], [("kf", n)], scale=dcol[:, h, 2:3], bias=0.0)
                act(k3[:, 2, :], kr[:], AF.Identity, [krk, "dcol", k3k], [k3k], scale=dcol[:, h, 3:4], bias=0.0)
                cp(vtm[:, n, :], ps[b][:, 128:384], [pk(b)], [("vtm", n)], eng="act")
                b2 = 2 + (n % 2)
                pb = ps[b2][:].bitcast(BF16)
                tp(pb[:, 0:128], k3[:, 0, :], 128, [k3k], [pk(b2)])
                cp(kT[:, ns], pb[:, 0:128], [pk(b2)], [("kT", n)], eng="act")
                cp(Sb[:, n, :], Rb[:], ["Rb"], [("Sb", n)])
                b3 = 4 + (n % 2)
                mm(ps[b3][:, 0:256], k3[:, 2, :], vtm[:, n, :], True, True, [k3k, ("vtm", n)], [pk(b3)])
                stt(Rb[:], Rb[:], gC[:, 4 + h:5 + h], ps[b3][:, 0:256], ALU.mult, ALU.add, ["Rb", "gC", pk(b3)], ["Rb"])
            for n in range(NT):
                ns = slice(n * 128, (n + 1) * 128)
                b = n % 2
                for kc in range(KC):
                    mm(ps[b][:, 0:384], HT[:, kc, ns], wr[:, kc, 0:384], kc == 0, kc == KC - 1, [("HT", n), "wr"], [pk(b)])
                tA, tAk = tAR.next()
                tB, tBk = tBR.next()
                kr, krk = krR.next()
                cosb = cosr[:, n, :].unsqueeze(1).unsqueeze(1).broadcast_to([128, 1, 2, 64])
                ssb = ssr[:, n, :].rearrange("p (h a d) -> p h a d", h=1, a=2)
                rope(v3(kr[:]), v3(ps[b][:, 0:128]), cosb, ssb, 64, 1, v3(tA[:]), v3(tB[:]), [pk(b)], [krk], tAk, tBk)
                q3, q3k = k3R.next()
                act(q3[:, 0, :], kr[:], AF.Copy, [krk], [q3k])
                act(q3[:, 1, :], kr[:], AF.Identity, [krk, "dcol", q3k], [q3k], scale=dcol[:, h, 0:1], bias=0.0)
                act(q3[:, 2, :], kr[:], AF.Identity, [krk, "dcol", q3k], [q3k], scale=dcol[:, h, 1:2], bias=0.0)
                sg, sgk = sgR.next()
                act(sg[:], ps[b][:, 128:384], AF.Silu, [pk(b)], [sgk])
                b2 = 2 + (n % 2)
                pb = ps[b2][:].bitcast(BF16)
                for j in range(3):
                    tp(pb[:, j * 128:(j + 1) * 128], q3[:, j, :], 128, [q3k], [pk(b2)])
                q3T, q3Tk = q3TR.next()
                cp(q3T[:].rearrange("p j t -> p (j t)"), pb[:, 0:384], [pk(b2)], [q3Tk])
                b3 = 4 + (n % 2)
                mm(ps[b3][:, 0:128], kT[:, ns], q3T[:, 0, :], True, True, [("kT", n), q3Tk], [pk(b3)])
                pt, ptk = ptR.next()
                tt(pt[:], ps[b3][:, 0:128], maskT[:, h, :], ALU.mult, [pk(b3), "maskT"], [ptk])
                Sf, Sfk = SfR.next()
                cp(Sf[:], Rf[:], ["Rf"], [Sfk])
                b4 = 6 + (n % 2)
                mm(ps[b4][:, 0:256], pt[:], vtm[:, n, :], True, False, [ptk, ("vtm", n)], [pk(b4)])
                mm(ps[b4][:, 0:256], q3T[:, 1, :], Sf[:], False, False, [q3Tk, Sfk], [pk(b4)])
                mm(ps[b4][:, 0:256], q3T[:, 2, :], Sb[:, n, :], False, True, [q3Tk, ("Sb", n)], [pk(b4)])
                mm(ps[b3][:, 256:512], kf[:, n, :], vtm[:, n, :], True, True, [("kf", n), ("vtm", n), ptk], [pk(b3)])
                stt(Rf[:], Rf[:], gC[:, h:h + 1], ps[b3][:, 256:512], ALU.mult, ALU.add, ["Rf", "gC", pk(b3), Sfk], ["Rf"])
                st_, sk = nstat()
                P.add("dve", lambda e, st_=st_, b4=b4: e.bn_stats(out=st_[:, 0:6], in_=ps[b4][:, 0:256]), r=[pk(b4)], w=[sk])
                P.add("dve", lambda e, st_=st_: e.bn_aggr(out=st_[:, 12:14], in_=st_[:, 0:6].rearrange("p (a b) -> p a b", a=1)), r=[sk], w=[sk])
                ts(st_[:, 14:15], st_[:, 13:14], LN_EPS, None, ALU.add, None, [sk], [sk])
                rsq(st_[:, 16:17], st_[:, 14:15], 1, [sk], [sk])
                stt(st_[:, 17:18], st_[:, 12:13], -1.0, st_[:, 16:17], ALU.mult, ALU.mult, [sk], [sk])
                ron, ronk = ronR.next()
                act(ron[:], ps[b4][:, 0:256], AF.Identity, [pk(b4), sk], [ronk], bias=st_[:, 17:18], scale=st_[:, 16:17])
                ab, abk = abR.next()
                tt(ab[:], ron[:], sg[:], ALU.mult, [ronk, sgk], [abk])
                for j in range(2):
                    tp(pb[:, 512 + j * 128:512 + (j + 1) * 128], ab[:, j * 128:(j + 1) * 128], 128, [abk, q3Tk], [pk(b2)])
                for j in range(2):
                    act(R2[:, 2 * h + j, ns], pb[:, 512 + j * 128:512 + (j + 1) * 128], AF.Identity, [pk(b2), "gncol"], [("R2", 2 * h + j)],
                        scale=gncol[:, 2 * h + j:2 * h + j + 1], bias=0.0)
        fence()
        arT.reset()

    def ln_epilogue(s, l, t, bps, gb, lng, lnb, zR, last, nexth=None):
        z, zk = zR.next()
        for hf in range(2):
            tt(z[:, hf * 512:(hf + 1) * 512], ps[bps[hf]][:, :], gb[:, hf * 512:(hf + 1) * 512], ALU.mult,
               [pk(bps[hf]), "gb"] + ([zk] if hf else []), [zk])
        stt(z[:], X[:, t, :], ALPHA, z[:], ALU.mult, ALU.add, [("X", t), zk], [zk])
        st_, sk = ln_stats(z, [zk], 0)
        act(z[:], z[:], AF.Identity, [zk, sk], [zk], bias=st_[:, 17:18], scale=st_[:, 16:17])
        tt(z[:], z[:], lng[:], ALU.mult, [zk, "lnrow"], [zk])
        tt(X[:, t, :], z[:], lnb[:], ALU.add, [zk, "lnrow"], [("X", t)])
        if last:
            dma("sp", y_d[s, t * 128:(t + 1) * 128, :], X[:, t, :], [("X", t)], [("y", s, t)])
        if nexth is not None:
            nexth(t)

    def phase_out(s, l):
        arR.reset()
        wout = keep["wout"]
        mTR = Ring(arT, 2, [128, KC, 128], BF16, "mT")
        xnR = Ring(arT, 2, [128, 1024], BF16, "xn")
        gb = arR.alloc([128, D], F32, "gb")
        lng = arR.alloc([128, D], F32, "lng")
        lnb = arR.alloc([128, D], F32, "lnb")
        zR = Ring(arR, 3, [128, D], F32, "z")
        dma("sp", gb[:], adag_d[s, l, 0:1, :].partition_broadcast(128), [("adag", s, l, 0)], ["gb"])
        dma("sp", lng[:], lnr_d[l, 0:1, :].partition_broadcast(128), [], ["lnrow"])
        dma("sp", lnb[:], lnr_d[l, 1:2, :].partition_broadcast(128), [], ["lnrow"])
        for t in range(NT):
            mT, mTk = mTR.next()
            qb = (t * 128) // QB
            dma("sp", mT[:], M_d[:, :, t * 128:(t + 1) * 128].rearrange("c p t -> p c t"), [("M", c, qb) for c in range(8)], [mTk])
            bps = (2 * (t % 2), 2 * (t % 2) + 1)
            for hf in range(2):
                for kc in range(KC):
                    mm(ps[bps[hf]][:, :], mT[:, kc, :], wout[:, kc, hf * 512:(hf + 1) * 512], kc == 0, kc == KC - 1, [mTk, "wout"], [pk(bps[hf])])
            ln_epilogue(s, l, t, bps, gb, lng, lnb, zR, False, nexth=lambda t_: h_tile(s, l, 1, t_, xnR, (4, 5, 6, 7)))
        fence()
        arT.reset()
        arR.reset()

    def phase_mlp(s, l, last):
        arT.reset()
        arR.reset()
        wd = arT.alloc([128, NFC, D], BF16, "wd")
        wd_top = arT.top
        convc = arT.alloc([128, NFC, 4], F32, "convc")
        dma("sp", convc[:], convc_d[l], [], ["convc"])
        wuR = Ring(arT, 2, [128, KC, 256], BF16, "wu")
        aext = arR.alloc([128, S + 2], F32, "aext")
        u = arR.alloc([128, S], F32, "u")
        geR = Ring(arR, 1, [128, S], BF16, "ge")
        gR = Ring(arR, 2, [128, S], BF16, "g")
        P.add("dve", lambda e: e.memset(aext[:, 0:1], 0.0), w=["aext"])
        P.add("dve", lambda e: e.memset(aext[:, S + 1:S + 2], 0.0), r=["aext"], w=["aext"])
        allH = [("HT", t) for t in range(NT)]
        i = 0
        for fc in range(NFC):
            wu, wuk_ = wuR.next()
            dma("pool", wu[:, :, 0:128], wup_d[l, :, :, fc * 128:(fc + 1) * 128], [], [wuk_])
            dma("pool", wu[:, :, 128:256], wup_d[l, :, :, FF + fc * 128:FF + (fc + 1) * 128], [], [wuk_])
            ge, gek = geR.next()
            g, gk = gR.next()
            bbs = []
            for qb in range(NQB):
                qs = slice(qb * QB, (qb + 1) * QB)
                ba = i % 4
                bb = 4 + (i % 4)
                i += 1
                bbs.append(bb)
                for kc in range(KC):
                    mm(ps[ba][:, 0:QB], wu[:, kc, 0:128], HT[:, kc, qs], kc == 0, kc == KC - 1, allH + [wuk_], [pk(ba)])
                for kc in range(KC):
                    mm(ps[bb][:, 0:QB], wu[:, kc, 128:256], HT[:, kc, qs], kc == 0, kc == KC - 1, allH + [wuk_], [pk(bb)])
                act(aext[:, 1 + qb * QB:1 + (qb + 1) * QB], ps[ba][:, 0:QB], AF.Copy, [pk(ba)], ["aext"])
            act(u[:], aext[:, 1:S + 1], AF.Identity, ["aext", "convc"], ["u"], scale=convc[:, fc, 1:2], bias=convc[:, fc, 3:4])
            stt(u[:], aext[:, 0:S], convc[:, fc, 0:1], u[:], ALU.mult, ALU.add, ["aext", "convc", "u"], ["u"])
            stt(u[:], aext[:, 2:S + 2], convc[:, fc, 2:3], u[:], ALU.mult, ALU.add, ["aext", "convc", "u"], ["u"])
            act(ge[:], u[:], AF.Gelu, ["u"], [gek])
            for qb in range(NQB):
                qs = slice(qb * QB, (qb + 1) * QB)
                tt(g[:, qs], ge[:, qs], ps[bbs[qb]][:, 0:QB], ALU.mult, [gek, pk(bbs[qb])] + ([gk] if qb else []), [gk])
            dma("sp", G_d[fc], g[:], [gk], [("G", fc)])
            if fc in (3, 10):
                q4 = 0 if fc == 3 else 1
                dma("pool", wd[:, q4 * 11:(q4 + 1) * 11, :], wdn_d[l, :, q4 * 11:(q4 + 1) * 11, :], [], ["wd"])
        fence()
        arR.reset()
        arT.top = wd_top
        xnR = Ring(arT, 2, [128, 1024], BF16, "xn")
        gtR = Ring(arR, 2, [128, NFC, 128], BF16, "gt")
        gb = arR.alloc([128, D], F32, "gb2")
        lng = arR.alloc([128, D], F32, "lng2")
        lnb = arR.alloc([128, D], F32, "lnb2")
        zR = Ring(arR, 2, [128, D], F32, "z2")
        dma("sp", gb[:], adag_d[s, l, 1:2, :].partition_broadcast(128), [("adag", s, l, 1)], ["gb"])
        dma("sp", lng[:], lnr_d[l, 2:3, :].partition_broadcast(128), [], ["lnrow"])
        dma("sp", lnb[:], lnr_d[l, 3:4, :].partition_broadcast(128), [], ["lnrow"])
        for t in range(NT):
            gt, gtk = gtR.next()
            dma("sp", gt[:], G_d[:, :, t * 128:(t + 1) * 128].rearrange("c p t -> p c t"), [("G", fc) for fc in range(NFC)], [gtk])
            bps = (2 * (t % 2), 2 * (t % 2) + 1)
            for hf in range(2):
                for fc in range(NFC):
                    mm(ps[bps[hf]][:, :], gt[:, fc, :], wd[:, fc, hf * 512:(hf + 1) * 512], fc == 0, fc == NFC - 1, [gtk, "wd"], [pk(bps[hf])])
            ln_epilogue(s, l, t, bps, gb, lng, lnb, zR, last,
                        nexth=None if last else (lambda t_: h_tile(s, l + 1, 0, t_, xnR, (4, 5, 6, 7))))
        fence()
        arT.reset()
        arR.reset()

    for s in range(NSEQ):
        for t in range(NT):
            dma("sp", X[:, t, :], x_d[s, t * 128:(t + 1) * 128, :], [], [("X", t)])
        for l in range(L):
            layer_consts(l)
            if l == 0:
                phase_H(s, l, 0)
            phase_mla(s, l)
            phase_gate(l, wmo_d, 3520 + 1024, True)
            phase_ret(s, l)
            phase_gate(l, wro_d, 3520, False)
            phase_out(s, l)
            phase_mlp(s, l, l == L - 1)
    P.add("sp", lambda e: None, r=[("y", s, t) for s in range(NSEQ) for t in range(NT)])

    es = ExitStack()
    P.emit(nc, es)
    es.close()
    return nc, P


def host_consts(S):
    NT = S // 128
    pos = np.arange(S, dtype=np.float32)

    def tables(dim):
        inv = (1.0 / (10000.0 ** (np.arange(0, dim, 2, dtype=np.float32) / np.float32(dim)))).astype(np.float32)
        ang = pos[:, None] * inv[None, :]
        return np.cos(ang).astype(np.float32), np.sin(ang).astype(np.float32)

    def tm(a):
        return np.ascontiguousarray(a.reshape(NT, 128, -1).transpose(1, 0, 2))

    cr, sr = tables(128)
    cm, sm = tables(64)
    k = np.arange(128, dtype=np.float32)[:, None]
    q = np.arange(128, dtype=np.float32)[None, :]
    cst4 = np.stack([np.maximum(q - k, 0.0), np.maximum(k - q, 0.0), (q >= k).astype(np.float32), (k > q).astype(np.float32)], axis=1)
    c = np.arange(128, dtype=np.float32)
    dexp = np.stack([c + 1.0, 128.0 - c, 127.0 - c, c], axis=1)
    return {
        "ident": np.eye(128, dtype=np.float32),
        "cosr": tm(cr), "ssr": tm(np.concatenate([-sr, sr], axis=1)),
        "cosm": tm(cm), "ssm": tm(np.concatenate([-sm, sm], axis=1)),
        "cst4": np.ascontiguousarray(cst4.astype(np.float32)), "dexp": np.ascontiguousarray(dexp.astype(np.float32)),
    }


def host_weights(inp, L):
    f = lambda a: np.ascontiguousarray(np.asarray(a, dtype=np.float32))

    def pk(w, kc):
        w = np.asarray(w, dtype=np.float32)
        return np.ascontiguousarray(w.reshape(L, kc, 128, w.shape[-1]).transpose(0, 2, 1, 3))

    def col(v, nb):
        v = np.asarray(v, dtype=np.float32)
        return np.ascontiguousarray(v.reshape(L, nb, 128).transpose(0, 2, 1))

    conv = np.concatenate([np.asarray(inp["conv_w"], np.float32), np.asarray(inp["conv_b"], np.float32)[:, None, :]], axis=1)
    convcol = np.ascontiguousarray(conv.reshape(L, 4, NFC, 128).transpose(0, 3, 2, 1))
    b_ada = np.asarray(inp["b_ada"], np.float32)
    return {
        "w_ada": pk(inp["w_ada"], KC), "b_ada": f(b_ada),
        "b_adacol": np.ascontiguousarray(b_ada.reshape(L, 48, 128).transpose(2, 0, 1)),
        "w_in": pk(inp["w_in"], KC),
        "dec": f(np.concatenate([np.asarray(inp["ret_decay_fwd"], np.float32), np.asarray(inp["ret_decay_bwd"], np.float32)], axis=1)),
        "gncol": col(inp["ret_gn_g"], 8), "w_ret_o": pk(inp["w_ret_o"], KC),
        "qgcol": col(inp["q_norm_g"], 2), "kvgcol": col(inp["kv_norm_g"], 1),
        "w_uq": pk(inp["w_uq"], 2), "w_uk": f(inp["w_uk"]), "w_uv": f(inp["w_uv"]),
        "w_mla_o": pk(inp["w_mla_o"], KC), "w_out": pk(inp["w_out"], KC),
        "lnrows": f(np.stack([np.asarray(inp[k], np.float32) for k in ("ln1_g", "ln1_b", "ln2_g", "ln2_b")], axis=1)),
        "w_up": pk(inp["w_up"], KC), "convcol": convcol, "w_down": pk(inp["w_down"], NFC),
    }


_cache = {}
_runkw = {}
_last = [None]


def run(xs, cs, inp, n_cores, L):
    NTOT, S, _ = xs.shape
    NSEQ = NTOT // n_cores
    key = (NSEQ, S, L)
    if key not in _cache:
        _cache[key] = build(NSEQ, S, L)[0]
    nc = _cache[key]
    shared = dict(host_consts(S))
    shared.update(host_weights(inp, L))
    in_maps = []
    for i in range(n_cores):
        m = dict(shared)
        m["x"] = np.ascontiguousarray(xs[i * NSEQ:(i + 1) * NSEQ])
        cc = cs[i * NSEQ:(i + 1) * NSEQ]
        m["cT"] = np.ascontiguousarray(cc.reshape(NSEQ, KC, 128).transpose(2, 1, 0))
        in_maps.append(m)
    res = run_bass_kernel_spmd(nc, in_maps, core_ids=list(range(n_cores)), **_runkw)
    _last[0] = res
    return np.concatenate([r["y"] for r in res.results], axis=0)


def kernel(**inp):
    xp = np.asarray(inp["x_prompt"], np.float32)
    xsm = np.asarray(inp["x_sample"], np.float32)
    xs = np.concatenate([xp, xsm], axis=0)
    cs = np.concatenate([np.asarray(inp["c_prompt"], np.float32), np.asarray(inp["c_sample"], np.float32)], axis=0)
    L = np.asarray(inp["w_in"]).shape[0]
    y = run(xs, cs, inp, 8, L)
    nb = xp.shape[0]
    return (np.ascontiguousarray(y[:nb]), np.ascontiguousarray(y[nb:]))
```

```python
import math
from contextlib import ExitStack

import numpy as np
import concourse.bass as bass
import concourse.mybir as mybir
from concourse.bass_utils import run_bass_kernel_spmd

F32 = mybir.dt.float32
BF16 = mybir.dt.bfloat16
AF = mybir.ActivationFunctionType
ALU = mybir.AluOpType

D = 1024
KC = 8
FF = 2816
NFC = 22
INW = 5568
LN_EPS = 1e-5
RMS_EPS = 1e-6
DEPTH_FULL = 4
ALPHA = (2.0 * DEPTH_FULL) ** 0.25
NSLOT = 12
AKEY = "__arena__"


class Op:
    __slots__ = ("eng", "fn", "deps", "dma", "slot", "slotval", "signal", "sigval", "idx", "epoch", "fence", "cost", "pos")


NEP = 8
SCHED = True
import os as _os
WINDOW = int(_os.environ.get('K_WINDOW', '48'))
PRIO = True
PRIO_EPS = 1.0
LAT_NS = float(_os.environ.get('K_LAT', '120'))
ENGS = ("pe", "act", "dve", "pool", "sp")


class Prog:
    def __init__(self):
        self.ops = []
        self.lastw = {}
        self.rd = {}
        self.epoch = 0

    def add(self, eng, fn, r=(), w=(), dma=False, fence=False, c=200.0):
        op = Op()
        op.eng, op.fn, op.dma, op.idx = eng, fn, dma, len(self.ops)
        op.signal = dma
        op.sigval = 0
        op.epoch = self.epoch
        op.fence = fence
        op.cost = c
        if fence:
            r, w = (), (AKEY,)
            self.epoch += 1
        else:
            r = tuple(r) + (AKEY,)
        deps = set()
        for k in r:
            d = self.lastw.get(k)
            if d is not None:
                deps.add(d)
            if isinstance(k, tuple) and k[0] == "ps":
                for d in self.rd.get(k, ()):
                    if self.ops[d].eng != eng:
                        deps.add(d)
        for k in w:
            d = self.lastw.get(k)
            if d is not None:
                deps.add(d)
            deps.update(self.rd.get(k, ()))
        deps.discard(op.idx)
        op.deps = deps
        for k in w:
            self.lastw[k] = op.idx
            self.rd[k] = []
        for k in r:
            self.rd.setdefault(k, []).append(op.idx)
        self.ops.append(op)
        return op

    def schedule(self):
        import heapq
        ops = self.ops
        order = {e: [] for e in ENGS}
        nep = self.epoch + 1
        byep = [[] for _ in range(nep)]
        for op in ops:
            byep[op.epoch].append(op.idx)
        done = [None] * len(ops)
        LAT = LAT_NS
        for ep in range(nep):
            idxs = byep[ep]
            if not idxs:
                continue
            if not SCHED:
                for i in idxs:
                    order[ops[i].eng].append(i)
                continue
            pend = {e: [i for i in idxs if ops[i].eng == e] for e in ENGS}
            head = {e: 0 for e in ENGS}
            taken = set()
            free = {e: 0.0 for e in ENGS}
            dmapipe = 0.0
            ev = [0.0]
            remaining = len(idxs)
            ldeps = {}
            for i in idxs:
                ldeps[i] = [d for d in ops[i].deps if ops[d].epoch == ep and not ops[d].fence]
            blev = {}
            if PRIO:
                for i in reversed(idxs):
                    b_ = blev.get(i, 0.0) + ops[i].cost + (2000.0 if ops[i].dma else LAT)
                    blev[i] = b_
                    for d in ldeps[i]:
                        if blev.get(d, 0.0) < b_:
                            blev[d] = b_
            while remaining:
                T = heapq.heappop(ev)
                while ev and ev[0] <= T:
                    heapq.heappop(ev)
                for e in ENGS:
                    if free[e] > T:
                        continue
                    pl = pend[e]
                    h = head[e]
                    while h < len(pl) and pl[h] in taken:
                        h += 1
                    head[e] = h
                    if h >= len(pl):
                        continue
                    pick = None
                    scanned = 0
                    j = h
                    while j < len(pl) and scanned < WINDOW:
                        i = pl[j]
                        j += 1
                        if i in taken:
                            continue
                        scanned += 1
                        ok = True
                        for d in ldeps[i]:
                            dt_ = done[d]
                            if dt_ is None or dt_ + LAT > T:
                                ok = False
                                break
                        if ok:
                            if not PRIO:
                                pick = i
                                break
                            if pick is None or blev[i] > blev[pick] + PRIO_EPS:
                                pick = i
                        if ops[i].fence:
                            break
                    if pick is None:
                        continue
                    op = ops[pick]
                    taken.add(pick)
                    remaining -= 1
                    order[e].append(pick)
                    if op.dma:
                        issue = 100.0 if e != "pool" else 600.0
                        free[e] = T + issue
                        st = max(T + issue, dmapipe)
                        dmapipe = st + op.cost
                        done[pick] = dmapipe + 2000.0
                    else:
                        free[e] = T + op.cost
                        done[pick] = T + op.cost
                    heapq.heappush(ev, free[e])
                    heapq.heappush(ev, done[pick] + LAT)
                if not ev and remaining:
                    raise RuntimeError("scheduler stuck")
        return order

    def finalize(self):
        ops = self.ops
        order = self.schedule()
        self.order = order
        for e in ENGS:
            for p, i in enumerate(order[e]):
                ops[i].pos = p
        dmak = {}
        for e in ENGS:
            for i in order[e]:
                op = ops[i]
                if op.dma:
                    k = dmak.get(op.eng, 0)
                    op.slot = (op.eng, k % NSLOT)
                    op.slotval = 16 * (k // NSLOT + 1)
                    dmak[op.eng] = k + 1
        lastfence = None
        fences = {}
        for op in ops:
            if op.fence:
                fences[op.epoch] = op.idx
        for op in ops:
            best = {}
            bestd = {}
            for d in op.deps:
                o = ops[d]
                if o.fence or o.epoch != op.epoch:
                    continue
                if o.dma:
                    if o.slot not in bestd or ops[bestd[o.slot]].slotval < o.slotval:
                        bestd[o.slot] = d
                else:
                    if o.eng == "pe" and op.eng == "pe" and not op.dma:
                        continue
                    if o.eng not in best or ops[best[o.eng]].pos < o.pos:
                        best[o.eng] = d
            op.deps = list(best.values()) + list(bestd.values())
            if op.epoch > 0:
                op.deps = [fences[op.epoch - 1]] + op.deps
            for d in op.deps:
                ops[d].signal = True
        cnt = {}
        nf = 0
        for op in ops:
            if op.fence:
                nf += 1
                op.sigval = nf
        for e in ENGS:
            for i in order[e]:
                op = ops[i]
                if not op.fence and not op.dma and op.signal:
                    key = (op.eng, op.epoch % NEP)
                    cnt[key] = cnt.get(key, 0) + 1
                    op.sigval = cnt[key]
        self.sigcounts = cnt
        self.dmacounts = dmak

    def emit(self, nc, es):
        self.finalize()
        ops = self.ops
        csem = {(e, j): es.enter_context(nc.semaphore("c_%s%d" % (e, j))) for e in ENGS for j in range(NEP)}
        fsem = es.enter_context(nc.semaphore("fence"))
        dsem = {}
        for q in ENGS:
            if self.dmacounts.get(q, 0) > 0:
                for s in range(NSLOT):
                    dsem[(q, s)] = es.enter_context(nc.semaphore("d_%s%d" % (q, s)))
        block = es.enter_context(nc.Block())
        order = self.order

        def run(e, eng):
            seen = {}
            for i in order[e]:
                op = ops[i]
                for d in op.deps:
                    o = ops[d]
                    if o.fence:
                        key, sem, val = "fence", fsem, o.sigval
                    elif o.dma:
                        key, sem, val = ("d",) + o.slot, dsem[o.slot], o.slotval
                    else:
                        key = ("c", o.eng, o.epoch % NEP)
                        sem, val = csem[key[1:]], o.sigval
                    if seen.get(key, 0) < val:
                        eng.wait_ge(sem, val)
                        seen[key] = val
                if op.dma:
                    prev = op.slotval - 16
                    if prev > 0 and seen.get(("d",) + op.slot, 0) < prev:
                        eng.wait_ge(dsem[op.slot], prev)
                        seen[("d",) + op.slot] = prev
                ins = op.fn(eng)
                if ins is None:
                    continue
                if op.fence:
                    ins.then_inc(fsem, 1)
                elif op.dma:
                    ins.then_inc(dsem[op.slot], 16)
                elif op.signal:
                    ins.then_inc(csem[(op.eng, op.epoch % NEP)], 1)

        @block.tensor
        def _(eng):
            run("pe", eng)

        @block.scalar
        def _(eng):
            run("act", eng)

        @block.vector
        def _(eng):
            run("dve", eng)

        @block.gpsimd
        def _(eng):
            run("pool", eng)

        @block.sync
        def _(eng):
            run("sp", eng)


_uid = [0]


class Arena:
    def __init__(self, nc, lo, hi):
        self.nc, self.lo, self.hi, self.top = nc, lo, hi, lo

    def alloc(self, shape, dt, name="t"):
        esz = 4 if dt == F32 else 2
        nb = esz
        for s in shape[1:]:
            nb *= s
        nb = (nb + 31) // 32 * 32
        off = self.top
        self.top += nb
        assert self.top <= self.hi, "SBUF arena overflow: %s %s need=%d over=%d" % (name, shape, nb, self.top - self.hi)
        _uid[0] += 1
        return self.nc.alloc_sbuf_tensor_at("%s_%d" % (name, _uid[0]), list(shape), dt, offset=off)

    def reset(self):
        self.top = self.lo


class Ring:
    def __init__(self, ar, n, shape, dt, name):
        self.t = [ar.alloc(shape, dt, name) for _ in range(n)]
        self.i = -1
        _uid[0] += 1
        self.name = "%s#%d" % (name, _uid[0])

    def next(self):
        self.i += 1
        j = self.i % len(self.t)
        return self.t[j], (self.name, j)


def build(NSEQ, S, L):
    NT = S // 128
    QB = min(512, S)
    NQB = S // QB
    TPQ = QB // 128
    C = 128
    nc = bass.Bass("TRN2", target_bir_lowering=False)

    def din(name, shape):
        return nc.dram_tensor(name, list(shape), F32, kind="ExternalInput").ap()

    x_d = din("x", [NSEQ, S, D])
    cT_d = din("cT", [128, KC, NSEQ])
    wada_d = din("w_ada", [L, 128, KC, 6 * D])
    bada_d = din("b_ada", [L, 6 * D])
    bcol_d = din("b_adacol", [128, L, 48])
    win_d = din("w_in", [L, 128, KC, INW])
    dec_d = din("dec", [L, 8])
    gn_d = din("gncol", [L, 128, 8])
    wro_d = din("w_ret_o", [L, 128, KC, D])
    qg_d = din("qgcol", [L, 128, 2])
    kvg_d = din("kvgcol", [L, 128, 1])
    wuq_d = din("w_uq", [L, 128, 2, 1536])
    wuk_d = din("w_uk", [L, 128, 1024])
    wuv_d = din("w_uv", [L, 128, 1024])
    wmo_d = din("w_mla_o", [L, 128, KC, D])
    wout_d = din("w_out", [L, 128, KC, D])
    lnr_d = din("lnrows", [L, 4, D])
    wup_d = din("w_up", [L, 128, KC, 2 * FF])
    convc_d = din("convcol", [L, 128, NFC, 4])
    wdn_d = din("w_down", [L, 128, NFC, D])
    ident_d = din("ident", [128, 128])
    cosr_d = din("cosr", [128, NT, 64])
    ssr_d = din("ssr", [128, NT, 128])
    cosm_d = din("cosm", [128, NT, 32])
    ssm_d = din("ssm", [128, NT, 64])
    cst4_d = din("cst4", [128, 4, 128])
    dexp_d = din("dexp", [128, 4])
    y_d = nc.dram_tensor("y", [NSEQ, S, D], F32, kind="ExternalOutput").ap()
    adag_d = nc.dram_tensor("adag_scr", [NSEQ, L, 2, D], F32, kind="Internal").ap()
    M_d = nc.dram_tensor("M_scr", [KC, 128, S], BF16, kind="Internal").ap()
    G_d = nc.dram_tensor("G_scr", [NFC, 128, S], BF16, kind="Internal").ap()

    P = Prog()
    ps = [nc.alloc_psum_tensor("ps%d" % i, [128, 512], F32) for i in range(8)]

    def pk(b):
        return ("ps", b)

    LO, HI = 16512, 229376
    XB = NT * 1024 * 4
    HB = max(8 * S * 2, 32768)
    off = LO
    arX = Arena(nc, off, off + XB); off += XB
    arH = Arena(nc, off, off + HB); off += HB
    arR = Arena(nc, off, off + HB); off += HB
    arC = Arena(nc, off, off + 26112); off += 26112
    arT = Arena(nc, off, HI)
    X = arX.alloc([128, NT, 1024], F32, "X")
    HT = arH.alloc([128, 8, S], BF16, "HT")
    R2 = arR.alloc([128, 8, S], BF16, "R2")
    ident = arC.alloc([128, 128], BF16, "ident")
    ones = arC.alloc([128, 128], F32, "ones")
    onesb = arC.alloc([128, 128], BF16, "onesb")
    cosr = arC.alloc([128, NT, 64], F32, "cosr")
    ssr = arC.alloc([128, NT, 128], F32, "ssr")
    cosm = arC.alloc([128, NT, 32], F32, "cosm")
    ssm = arC.alloc([128, NT, 64], F32, "ssm")
    cst4 = arC.alloc([128, 4, 128], F32, "cst4")
    dexp = arC.alloc([128, 4], F32, "dexp")
    cvals = arC.alloc([128, 4], F32, "cvals")
    adaT = arC.alloc([128, L, 4, 8, NSEQ], F32, "adaT")
    siluT = arC.alloc([128, KC, NSEQ], BF16, "siluT")
    cTs = arC.alloc([128, KC, NSEQ], F32, "cTs")
    dec = arC.alloc([128, 8], F32, "dec")
    lg = arC.alloc([128, 8], F32, "lg")
    gC = arC.alloc([128, 8], F32, "gC")
    dcol = arC.alloc([128, 4, 4], F32, "dcol")
    kplain = arC.alloc([128, 1], F32, "kplain")
    maskT = arC.alloc([128, 4, 128], F32, "maskT")
    gncol = arC.alloc([128, 8], F32, "gncol")
    qgcol = arC.alloc([128, 2], F32, "qgcol")
    kvgcol = arC.alloc([128, 1], F32, "kvgcol")
    stat = [arC.alloc([128, 24], F32, "stat%d" % i) for i in range(4)]
    stat_i = [-1]

    def nstat():
        stat_i[0] += 1
        j = stat_i[0] % 4
        return stat[j], ("stat", j)

    def mm(out, lhsT, rhs, st, sp, r, w):
        P.add("pe", lambda e: e.matmul(out, lhsT=lhsT, rhs=rhs, start=st, stop=sp), r=r, w=w, c=0.5 * rhs.free_size() + 30.0)

    def tp(out, in_, n, r, w):
        P.add("pe", lambda e: e.transpose(out, in_, ident[0:n, 0:n]), r=tuple(r) + ("ident",), w=w, c=100.0)

    def act(out, in_, func, r, w, bias=None, scale=None, accum=None):
        kw = {}
        if bias is not None:
            kw["bias"] = bias
        if scale is not None:
            kw["scale"] = scale
        if accum is not None:
            kw["accum_out"] = accum
        P.add("act", lambda e: e.activation(out=out, in_=in_, func=func, **kw), r=r, w=w, c=out.free_size() / 1.2 + 250.0)

    def tt(out, in0, in1, op, r, w, eng="dve"):
        P.add(eng, lambda e: e.tensor_tensor(out=out, in0=in0, in1=in1, op=op), r=r, w=w,
              c=out.free_size() / (0.9 if eng == "dve" else 0.45) + 150.0)

    def ts(out, in0, s1, s2, op0, op1, r, w):
        if s2 is None:
            P.add("dve", lambda e: e.tensor_scalar(out=out, in0=in0, scalar1=s1, scalar2=None, op0=op0), r=r, w=w, c=out.free_size() / 0.9 + 150.0)
        else:
            P.add("dve", lambda e: e.tensor_scalar(out=out, in0=in0, scalar1=s1, scalar2=s2, op0=op0, op1=op1), r=r, w=w, c=out.free_size() / 0.9 + 150.0)

    def stt(out, in0, sc, in1, op0, op1, r, w, eng="dve"):
        P.add(eng, lambda e: e.scalar_tensor_tensor(out=out, in0=in0, scalar=sc, in1=in1, op0=op0, op1=op1), r=r, w=w,
              c=out.free_size() / (0.9 if eng == "dve" else 0.45) + 150.0)

    def cp(out, in_, r, w, eng="dve"):
        if eng == "act":
            act(out, in_, AF.Copy, r, w)
            return
        k = 0.9 if eng == "dve" else 0.45
        P.add(eng, lambda e: e.tensor_copy(out=out, in_=in_), r=r, w=w, c=out.free_size() / k + 150.0)

    def dma(q, out, in_, r, w):
        P.add(q, lambda e: e.dma_start(out=out, in_=in_), r=r, w=w, dma=True, c=out.free_nbytes() * out.partition_size() / 120.0 + 300.0)

    def fence():
        P.add("sp", lambda e: e.nop(), fence=True)

    def rsq(out, in_, n, r, w):
        P.add("pool", lambda e: e.tensor_tensor(out=out, in0=in_, in1=cvals[:, 2:3], op=ALU.pow),
              r=tuple(r) + ("cvals",), w=w, c=600.0)

    def ln_stats(src_ap, srckeys, eps_col):
        stt_, sk = nstat()
        P.add("dve", lambda e: e.bn_stats(out=stt_[:, 0:6], in_=src_ap[:, 0:512]), r=srckeys, w=[sk])
        P.add("dve", lambda e: e.bn_stats(out=stt_[:, 6:12], in_=src_ap[:, 512:1024]), r=list(srckeys) + [sk], w=[sk])
        P.add("dve", lambda e: e.bn_aggr(out=stt_[:, 12:14], in_=stt_[:, 0:12].rearrange("p (a b) -> p a b", a=2)), r=[sk], w=[sk])
        ts(stt_[:, 14:15], stt_[:, 13:14], LN_EPS, None, ALU.add, None, [sk], [sk])
        rsq(stt_[:, 16:17], stt_[:, 14:15], 1, [sk], [sk])
        stt(stt_[:, 17:18], stt_[:, 12:13], -1.0, stt_[:, 16:17], ALU.mult, ALU.mult, [sk], [sk])
        return stt_, sk

    dma("pool", ident[:], ident_d, [], ["ident"])
    dma("sp", cosr[:], cosr_d, [], ["rope"])
    dma("sp", ssr[:], ssr_d, [], ["rope"])
    dma("sp", cosm[:], cosm_d, [], ["rope"])
    dma("sp", ssm[:], ssm_d, [], ["rope"])
    dma("sp", cst4[:], cst4_d, [], ["cst4"])
    dma("sp", dexp[:], dexp_d, [], ["dexp"])
    dma("sp", cTs[:], cT_d, [], ["cTs"])
    P.add("dve", lambda e: e.memset(ones[:], 1.0), w=["ones"])
    P.add("dve", lambda e: e.memset(onesb[:], 1.0), w=["onesb"])
    P.add("dve", lambda e: e.memset(cvals[:, 0:1], LN_EPS), w=["cvals"])
    P.add("dve", lambda e: e.memset(cvals[:, 1:2], RMS_EPS), r=["cvals"], w=["cvals"])
    P.add("dve", lambda e: e.memset(cvals[:, 2:3], -0.5), r=["cvals"], w=["cvals"])
    P.add("dve", lambda e: e.memset(cvals[:, 3:4], 1.0), r=["cvals"], w=["cvals"])
    P.add("dve", lambda e: e.memset(kplain[:], 128.0 ** -0.5), w=["kplain"])
    act(siluT[:], cTs[:], AF.Silu, ["cTs"], ["siluT"])

    bcol = arT.alloc([128, L, 48], F32, "bcol")
    browR = Ring(arT, 2, [1, 512], F32, "brow")
    dma("sp", bcol[:], bcol_d, [], ["bcol"])
    waR = Ring(arT, 2, [128, KC, 512], BF16, "wa")
    growR = Ring(arT, 2, [1, 512], F32, "grow")
    pr = [0]
    for l in range(L):
        for nb in range(12):
            wa, wak = waR.next()
            dma("pool", wa[:], wada_d[l, :, :, nb * 512:(nb + 1) * 512], [], [wak])
            blk = nb // 2
            if blk in (2, 5):
                brow, browk = browR.next()
                dma("sp", brow[:], bada_d[l:l + 1, nb * 512:(nb + 1) * 512], [], [browk])
                for s in range(NSEQ):
                    b = pr[0] % 4
                    pr[0] += 1
                    for kc in range(KC):
                        mm(ps[b][0:1, :], siluT[:, kc, s:s + 1], wa[:, kc, :], kc == 0, kc == KC - 1, [wak, "siluT"], [pk(b)])
                    gr, grk = growR.next()
                    n0 = nb * 512
                    tt(gr[:], ps[b][0:1, :], brow[:], ALU.add, [pk(b), browk], [grk])
                    ts(gr[:], gr[:], 1.0, None, ALU.add, None, [grk], [grk])
                    wi = 0 if blk == 2 else 1
                    hf = nb % 2
                    dma("sp", adag_d[s, l, wi:wi + 1, hf * 512:(hf + 1) * 512], gr[:], [grk], [("adag", s, l, wi)])
            else:
                wi = {0: 0, 1: 1, 3: 2, 4: 3}[blk]
                b = 4 + pr[0] % 4
                pr[0] += 1
                for j in range(4):
                    for kc in range(KC):
                        mm(ps[b][:, j * 8:j * 8 + NSEQ], wa[:, kc, j * 128:(j + 1) * 128], siluT[:, kc, :], kc == 0, kc == KC - 1,
                           [wak, "siluT"], [pk(b)])
                for j in range(4):
                    fb = (nb % 2) * 4 + j
                    cb = nb * 4 + j
                    act(adaT[:, l, wi, fb, :], ps[b][:, j * 8:j * 8 + NSEQ], AF.Identity, [pk(b), "bcol"], ["adaT"],
                        bias=bcol[:, l, cb:cb + 1], scale=1.0)
    for l in range(L):
        for wi in (1, 3):
            ts(adaT[:, l, wi], adaT[:, l, wi], 1.0, None, ALU.add, None, ["adaT"], ["adaT"])
    fence()
    arT.reset()

    def h_tile(s, l, which, t, xnR, banks):
        shw, scw = (0, 1) if which == 0 else (2, 3)
        st_, sk = ln_stats(X[:, t, :], [("X", t)], 0)
        xn, xk = xnR.next()
        act(xn[:], X[:, t, :], AF.Identity, [("X", t), sk], [xk], bias=st_[:, 17:18], scale=st_[:, 16:17])
        b = banks[t % len(banks)]
        pb = ps[b][:].bitcast(BF16)
        for kc in range(KC):
            tp(pb[:, kc * 128:(kc + 1) * 128], xn[:, kc * 128:(kc + 1) * 128], 128, [xk], [pk(b)])
        for kc in range(KC):
            act(HT[:, kc, t * 128:(t + 1) * 128], pb[:, kc * 128:(kc + 1) * 128], AF.Identity, [pk(b), "adaT"], [("HT", t)],
                bias=adaT[:, l, shw, kc, s:s + 1], scale=adaT[:, l, scw, kc, s:s + 1])

    def phase_H(s, l, which):
        xnR = Ring(arT, 3, [128, 1024], BF16, "xn")
        for t in range(NT):
            h_tile(s, l, which, t, xnR, (0, 1, 2, 3))
        fence()
        arT.reset()

    def layer_consts(l):
        dma("sp", dec[:], dec_d[l:l + 1, :].partition_broadcast(128), [], ["dec"])
        dma("sp", gncol[:], gn_d[l], [], ["gncol"])
        dma("sp", qgcol[:], qg_d[l], [], ["qgcol"])
        dma("sp", kvgcol[:], kvg_d[l], [], ["kvgcol"])
        act(lg[:], dec[:], AF.Sigmoid, ["dec"], ["lg"])
        act(lg[:], lg[:], AF.Ln, ["lg"], ["lg"])
        act(gC[:], lg[:], AF.Exp, ["lg"], ["gC"], scale=float(C))
        e1 = arT.alloc([128, 128], F32, "e1")
        e2 = arT.alloc([128, 128], F32, "e2")
        for h in range(4):
            act(e1[:], cst4[:, 0, :], AF.Exp, ["cst4", "lg"], ["e1"], scale=lg[:, h:h + 1])
            tt(e1[:], e1[:], cst4[:, 2, :], ALU.mult, ["e1", "cst4"], ["e1"])
            act(e2[:], cst4[:, 1, :], AF.Exp, ["cst4", "lg"], ["e2"], scale=lg[:, 4 + h:5 + h])
            tt(e2[:], e2[:], cst4[:, 3, :], ALU.mult, ["e2", "cst4"], ["e2"])
            tt(maskT[:, h, :], e1[:], e2[:], ALU.add, ["e1", "e2"], ["maskT"])
            act(dcol[:, h, 0:1], dexp[:, 0:1], AF.Exp, ["dexp", "lg"], ["dcol"], scale=lg[:, h:h + 1])
            act(dcol[:, h, 1:2], dexp[:, 1:2], AF.Exp, ["dexp", "lg"], ["dcol"], scale=lg[:, 4 + h:5 + h])
            act(dcol[:, h, 2:3], dexp[:, 2:3], AF.Exp, ["dexp", "lg"], ["dcol"], scale=lg[:, h:h + 1])
            act(dcol[:, h, 3:4], dexp[:, 3:4], AF.Exp, ["dexp", "lg"], ["dcol"], scale=lg[:, 4 + h:5 + h])
            ts(dcol[:, h, 2:4], dcol[:, h, 2:4], 128.0 ** -0.5, None, ALU.mult, None, ["dcol"], ["dcol"])
        fence()
        arT.reset()

    def rope(dst, src, cos_b, ss, half, nh, tmpA, tmpB, r, w, kA, kB, addeng="dve"):
        tt(tmpA, src, cos_b, ALU.mult, list(r) + ["rope"], [kA])
        tt(tmpB[:, :, 0, :], src[:, :, 1, :], ss[:, :, 0, :], ALU.mult, list(r) + ["rope"], [kB])
        tt(tmpB[:, :, 1, :], src[:, :, 0, :], ss[:, :, 1, :], ALU.mult, list(r) + ["rope", kB], [kB])
        tt(dst, tmpA, tmpB, ALU.add, [kA, kB], w, eng=addeng)

    def phase_mla(s, l):
        arT.reset()
        cqT = arT.alloc([128, 2, S], BF16, "cqT")
        ckvT = arT.alloc([128, S], BF16, "ckvT")
        krT = arT.alloc([128, 2, S], BF16, "krT")
        qrT = arT.alloc([128, 4, S], BF16, "qrT")
        P.add("dve", lambda e: e.memset(krT[:], 0.0), w=["krT0"], c=2 * S / 0.9 + 150.0)
        base_top = arT.top
        wl = arT.alloc([128, KC, 448], BF16, "wl")
        wuqr = arT.alloc([128, 2, 8, 64], BF16, "wuqr")
        dma("pool", wl[:], win_d[l, :, :, 3072:3520], [], ["wl"])
        dma("pool", wuqr[:], wuq_d[l].rearrange("p k (h d) -> p k h d", h=8)[:, :, :, 128:192], [], ["wuqr"])
        latR = Ring(arT, 2, [128, 512], BF16, "lat")
        tAR = Ring(arT, 1, [128, 512], F32, "tA")
        tBR = Ring(arT, 1, [128, 512], F32, "tB")
        tAsR = Ring(arT, 1, [128, 64], F32, "tAs")
        tBsR = Ring(arT, 1, [128, 64], F32, "tBs")
        sqj = arT.alloc([128, 256], BF16, "sqj")
        qrtR = Ring(arT, 2, [128, 512], BF16, "qrt")
        for t in range(NT):
            b = 0 + (t % 2)
            for kc in range(KC):
                mm(ps[b][:, 0:448], HT[:, kc, t * 128:(t + 1) * 128], wl[:, kc, :], kc == 0, kc == KC - 1, [("HT", t), "wl"], [pk(b)])
            st_, sk = nstat()
            act(sqj[:, 0:256], ps[b][:, 0:256], AF.Square, [pk(b)], ["sqj", sk], accum=st_[:, 0:1])
            act(sqj[:, 0:128], ps[b][:, 256:384], AF.Square, [pk(b), sk], ["sqj", sk], accum=st_[:, 1:2])
            ts(st_[:, 2:3], st_[:, 0:1], 1.0 / 256.0, RMS_EPS, ALU.mult, ALU.add, [sk], [sk])
            ts(st_[:, 3:4], st_[:, 1:2], 1.0 / 128.0, RMS_EPS, ALU.mult, ALU.add, [sk], [sk])
            rsq(st_[:, 4:5], st_[:, 2:3], 1, [sk], [sk])
            rsq(st_[:, 5:6], st_[:, 3:4], 1, [sk], [sk])
            lat, lk = latR.next()
            act(lat[:, 0:256], ps[b][:, 0:256], AF.Identity, [pk(b), sk], [lk], scale=st_[:, 4:5], bias=0.0)
            act(lat[:, 256:384], ps[b][:, 256:384], AF.Identity, [pk(b), sk, lk], [lk], scale=st_[:, 5:6], bias=0.0)
            tA, tAk = tAsR.next()
            tB, tBk = tBsR.next()
            v4 = lambda ap: ap.rearrange("p (h a d) -> p h a d", h=1, a=2)
            cosb = cosm[:, t, :].unsqueeze(1).unsqueeze(1).broadcast_to([128, 1, 2, 32])
            ssb = ssm[:, t, :].rearrange("p (h a d) -> p h a d", h=1, a=2)
            rope(v4(lat[:, 384:448]), v4(ps[b][:, 384:448]), cosb, ssb, 32, 1, v4(tA[:, 0:64]), v4(tB[:, 0:64]),
                 [pk(b), lk], [lk], tAk, tBk)
            cp(lat[:, 448:512], lat[:, 384:448], [lk], [lk])
            b2 = 2 + (t % 2)
            pb = ps[b2][:].bitcast(BF16)
            for j in range(4):
                tp(pb[:, j * 128:(j + 1) * 128], lat[:, j * 128:(j + 1) * 128], 128, [lk], [pk(b2)])
            tsl = slice(t * 128, (t + 1) * 128)
            act(cqT[:, 0, tsl], pb[:, 0:128], AF.Identity, [pk(b2), "qgcol"], [("cqT", t)], scale=qgcol[:, 0:1], bias=0.0)
            act(cqT[:, 1, tsl], pb[:, 128:256], AF.Identity, [pk(b2), "qgcol", ("cqT", t)], [("cqT", t)], scale=qgcol[:, 1:2], bias=0.0)
            act(ckvT[:, tsl], pb[:, 256:384], AF.Identity, [pk(b2), "kvgcol"], [("ckvT", t)], scale=kvgcol[:, 0:1], bias=0.0)
            cp(krT[0:64, 0, tsl], pb[0:64, 384:512], [pk(b2), "krT0"], [("krT", t)])
            cp(krT[64:128, 1, tsl], pb[64:128, 384:512], [pk(b2), "krT0", ("krT", t)], [("krT", t)])
            b3 = 4 + (t % 2)
            for kc in range(2):
                mm(ps[b3][:, 0:512], cqT[:, kc, tsl], wuqr[:, kc].rearrange("p h d -> p (h d)"), kc == 0, kc == 1,
                   [("cqT", t), "wuqr"], [pk(b3)])
            tA, tAk = tAR.next()
            tB, tBk = tBR.next()
            qrt, qk = qrtR.next()
            v8 = lambda ap: ap.rearrange("p (h a d) -> p h a d", h=8, a=2)
            cosb8 = cosm[:, t, :].unsqueeze(1).unsqueeze(1).broadcast_to([128, 8, 2, 32])
            ssb8 = ssm[:, t, :].rearrange("p (a d) -> p a d", a=2).unsqueeze(1).broadcast_to([128, 8, 2, 32])
            rope(v8(qrt[:]), v8(ps[b3][:, 0:512]), cosb8, ssb8, 32, 8, v8(tA[:]), v8(tB[:]), [pk(b3)], [qk], tAk, tBk)
            b4 = 6 + (t % 2)
            pb4 = ps[b4][:].bitcast(BF16)
            for j in range(4):
                tp(pb4[:, j * 128:(j + 1) * 128], qrt[:, j * 128:(j + 1) * 128], 128, [qk], [pk(b4)])
            cp(qrT[:, :, tsl], pb4[:, 0:512].rearrange("p (j t) -> p j t", j=4), [pk(b4)], [("qrT", t)])
        fence()
        arT.top = base_top
        wkvR = Ring(arT, 1, [128, 2, 128], BF16, "wkv")
        wqnR = Ring(arT, 2, [128, 2, 128], BF16, "wqn")
        knT = arT.alloc([128, S], BF16, "knT")
        vh = arT.alloc([128, NT, 128], BF16, "vh")
        qnR = Ring(arT, 1, [128, QB], BF16, "qn")
        ptR = Ring(arT, 5, [128, QB], BF16, "pt")
        accR = Ring(arT, 1, [128, QB], F32, "acc")
        scale = 192.0 ** -0.5
        sc_i = [0]
        allT = [("ckvT", t) for t in range(NT)]
        for h in range(8):
            wqn, wqk = wqnR.next()
            wkv, wkvk = wkvR.next()
            dma("pool", wqn[:], wuq_d[l, :, :, h * 192:h * 192 + 128], [], [wqk])
            dma("pool", wkv[:, 0, :], wuk_d[l, :, h * 128:(h + 1) * 128], [], [wkvk])
            dma("pool", wkv[:, 1, :], wuv_d[l, :, h * 128:(h + 1) * 128], [], [wkvk])
            for qb in range(NQB):
                qs = slice(qb * QB, (qb + 1) * QB)
                mm(ps[7][:, 0:QB], wkv[:, 0, :], ckvT[:, qs], True, True, allT + [wkvk], [pk(7)])
                cp(knT[:, qs], ps[7][:, 0:QB], [pk(7)], ["knT"])
            for t4 in range(0, NT, 4):
                n4 = min(4, NT - t4)
                for j in range(n4):
                    t = t4 + j
                    mm(ps[7][:, j * 128:(j + 1) * 128], ckvT[:, t * 128:(t + 1) * 128], wkv[:, 1, :], True, True,
                       allT + [wkvk], [pk(7)])
                act(vh[:, t4:t4 + n4, :], ps[7][:, 0:n4 * 128].rearrange("p (j d) -> p j d", j=n4), AF.Copy, [pk(7)], ["vh"])
            pb_ = (h % 2) * 64
            for qb in range(NQB):
                qs = slice(qb * QB, (qb + 1) * QB)
                qn, qnk = qnR.next()
                for kc in range(2):
                    mm(ps[7][:, 0:QB], wqn[:, kc, :], cqT[:, kc, qs], kc == 0, kc == 1, [("cqT", t) for t in range(NT)] + [wqk], [pk(7)])
                cp(qn[:], ps[7][:, 0:QB], [pk(7)], [qnk])
                bo = 3 + (qb % 2)
                bd = 5 + (qb % 2)
                acc, acck = accR.next()
                odd = [k_ for k_ in range(NT) if k_ % 2 == 1]
                pt1 = None
                for kt in range(NT):
                    bs = sc_i[0] % 3
                    sc_i[0] += 1
                    ks = slice(kt * 128, (kt + 1) * 128)
                    mm(ps[bs][:, 0:QB], knT[:, ks], qn[:], True, False, ["knT", qnk], [pk(bs)])
                    mm(ps[bs][:, 0:QB], krT[:, h % 2, ks], qrT[:, h // 2, qs], False, True,
                       [("krT", kt)] + [("qrT", t) for t in range(NT)], [pk(bs)])
                    pt, ptk = ptR.next()
                    act(pt[:], ps[bs][:, 0:QB], AF.Exp, [pk(bs)], [ptk], scale=scale)
                    mm(ps[bo][:, 0:QB], vh[:, kt, :], pt[:], kt == 0, kt == NT - 1, ["vh", ptk], [pk(bo)])
                    if kt % 2 == 0:
                        mm(ps[bd][:, 0:QB], onesb[:], pt[:], kt == 0, False, ["onesb", ptk], [pk(bd)])
                    elif len(odd) == 1:
                        cp(acc[:], pt[:], [ptk], [acck])
                    elif kt == 1:
                        pt1, pt1k = pt, ptk
                    elif kt == 3:
                        tt(acc[:], pt1[:], pt[:], ALU.add, [pt1k, ptk], [acck])
                    else:
                        tt(acc[:], acc[:], pt[:], ALU.add, [ptk, acck], [acck])
                mm(ps[bd][:, 0:QB], ones[:], acc[:], False, True, ["ones", acck], [pk(bd)])
                P.add("dve", lambda e, acc=acc, bd=bd: e.reciprocal(out=acc[:], in_=ps[bd][:, 0:QB]), r=[pk(bd)], w=[acck], c=700.0)
                tt(R2[:, h, qs], ps[bo][:, 0:QB], acc[:], ALU.mult, [pk(bo), acck], [("R2", h)])
        fence()
        arT.reset()

    keep = {}

    def phase_gate(l, wsrc_d, gcol0, first):
        arT.reset()
        if not first:
            keep["wout"] = arT.alloc([128, KC, D], BF16, "wout")
            dma("pool", keep["wout"][:], wout_d[l], [], ["wout"])
        keep_top = arT.top
        woR = Ring(arT, 2, [128, KC, 128], BF16, "wo")
        wgR = Ring(arT, 2, [128, KC, 128], BF16, "wg")
        sgR = Ring(arT, 2, [128, QB], F32, "sg")
        mbR = Ring(arT, 2, [128, QB], BF16, "mb")
        mpR = Ring(arT, 2, [128, QB], BF16, "mp")
        i = 0
        allH = [("HT", t) for t in range(NT)]
        allR = [("R2", h) for h in range(8)]
        for c in range(8):
            wo, wok = woR.next()
            wg, wgk = wgR.next()
            dma("pool", wo[:], wsrc_d[l, :, :, c * 128:(c + 1) * 128], [], [wok])
            dma("pool", wg[:], win_d[l, :, :, gcol0 + c * 128:gcol0 + (c + 1) * 128], [], [wgk])
            for qb in range(NQB):
                qs = slice(qb * QB, (qb + 1) * QB)
                by = (i % 2)
                bg = 2 + (i % 2)
                i += 1
                for kc in range(KC):
                    mm(ps[by][:, 0:QB], wo[:, kc, :], R2[:, kc, qs], kc == 0, kc == KC - 1, allR + [wok], [pk(by)])
                for kc in range(KC):
                    mm(ps[bg][:, 0:QB], wg[:, kc, :], HT[:, kc, qs], kc == 0, kc == KC - 1, allH + [wgk], [pk(bg)])
                sg, sgk = sgR.next()
                act(sg[:], ps[bg][:, 0:QB], AF.Sigmoid, [pk(bg)], [sgk])
                mb, mbk = mbR.next()
                if first:
                    tt(mb[:], ps[by][:, 0:QB], sg[:], ALU.mult, [pk(by), sgk], [mbk])
                else:
                    mp, mpk = mpR.next()
                    dma("sp", mp[:], M_d[c, :, qs], [("M", c, qb)], [mpk])
                    tt(sg[:], ps[by][:, 0:QB], sg[:], ALU.mult, [pk(by), sgk], [sgk])
                    tt(mb[:], sg[:], mp[:], ALU.add, [sgk, mpk], [mbk])
                dma("sp", M_d[c, :, qs], mb[:], [mbk], [("M", c, qb)])
        fence()
        arT.top = keep_top

    def phase_ret(s, l):
        arT.reset()
        wr = arT.alloc([128, KC, 768], BF16, "wr")
        kT = arT.alloc([128, S], BF16, "kT")
        kf = arT.alloc([128, NT, 128], BF16, "kf")
        vtm = arT.alloc([128, NT, 256], BF16, "vtm")
        Sb = arT.alloc([128, NT, 256], BF16, "Sb")
        Rf = arT.alloc([128, 256], F32, "Rf")
        Rb = arT.alloc([128, 256], F32, "Rb")
        SfR = Ring(arT, 2, [128, 256], BF16, "Sf")
        tAR = Ring(arT, 2, [128, 128], F32, "rA")
        tBR = Ring(arT, 2, [128, 128], F32, "rB")
        krR = Ring(arT, 2, [128, 128], F32, "kr")
        k3R = Ring(arT, 2, [128, 3, 128], BF16, "k3")
        q3TR = Ring(arT, 2, [128, 3, 128], BF16, "q3T")
        sgR = Ring(arT, 2, [128, 256], BF16, "sgr")
        ptR = Ring(arT, 2, [128, 128], BF16, "ptr")
        ronR = Ring(arT, 2, [128, 256], F32, "ron")
        abR = Ring(arT, 2, [128, 256], BF16, "ab")
        v3 = lambda ap: ap.rearrange("p (h a d) -> p h a d", h=1, a=2)
        for h in range(4):
            dma("pool", wr[:, :, 0:128], win_d[l, :, :, h * 128:(h + 1) * 128], [], ["wr"])
            dma("pool", wr[:, :, 128:384], win_d[l, :, :, 2048 + h * 256:2048 + (h + 1) * 256], [], ["wr"])
            dma("pool", wr[:, :, 384:512], win_d[l, :, :, 512 + h * 128:512 + (h + 1) * 128], [], ["wr"])
            dma("pool", wr[:, :, 512:768], win_d[l, :, :, 1024 + h * 256:1024 + (h + 1) * 256], [], ["wr"])
            P.add("dve", lambda e: e.memset(Rf[:], 0.0), w=["Rf"])
            P.add("dve", lambda e: e.memset(Rb[:], 0.0), w=["Rb"])
            for n in range(NT - 1, -1, -1):
                ns = slice(n * 128, (n + 1) * 128)
                b = n % 2
                for kc in range(KC):
                    mm(ps[b][:, 0:384], HT[:, kc, ns], wr[:, kc, 384:768], kc == 0, kc == KC - 1, [("HT", n), "wr"], [pk(b)])
                tA, tAk = tAR.next()
                tB, tBk = tBR.next()
                kr, krk = krR.next()
                cosb = cosr[:, n, :].unsqueeze(1).unsqueeze(1).broadcast_to([128, 1, 2, 64])
                ssb = ssr[:, n, :].rearrange("p (h a d) -> p h a d", h=1, a=2)
                rope(v3(kr[:]), v3(ps[b][:, 0:128]), cosb, ssb, 64, 1, v3(tA[:]), v3(tB[:]), [pk(b)], [krk], tAk, tBk)
                k3, k3k = k3R.next()
                act(k3[:, 0, :], kr[:], AF.Identity, [krk, "kplain"], [k3k], scale=kplain[:, 0:1], bias=0.0)
                act(kf[:, n, :], kr[:], AF.Identity, [krk, "dcol"], [("kf", n)], scale=dcol[:, h, 2:3], bias=0.0)
                act(k3[:, 2, :], kr[:], AF.Identity, [krk, "dcol", k3k], [k3k], scale=dcol[:, h, 3:4], bias=0.0)
                cp(vtm[:, n, :], ps[b][:, 128:384], [pk(b)], [("vtm", n)], eng="act")
                b2 = 2 + (n % 2)
                pb = ps[b2][:].bitcast(BF16)
                tp(pb[:, 0:128], k3[:, 0, :], 128, [k3k], [pk(b2)])
                cp(kT[:, ns], pb[:, 0:128], [pk(b2)], [("kT", n)], eng="act")
                cp(Sb[:, n, :], Rb[:], ["Rb"], [("Sb", n)])
                b3 = 4 + (n % 2)
                mm(ps[b3][:, 0:256], k3[:, 2, :], vtm[:, n, :], True, True, [k3k, ("vtm", n)], [pk(b3)])
                stt(Rb[:], Rb[:], gC[:, 4 + h:5 + h], ps[b3][:, 0:256], ALU.mult, ALU.add, ["Rb", "gC", pk(b3)], ["Rb"])
            for n in range(NT):
                ns = slice(n * 128, (n + 1) * 128)
                b = n % 2
                for kc in range(KC):
                    mm(ps[b][:, 0:384], HT[:, kc, ns], wr[:, kc, 0:384], kc == 0, kc == KC - 1, [("HT", n), "wr"], [pk(b)])
                tA, tAk = tAR.next()
                tB, tBk = tBR.next()
                kr, krk = krR.next()
                cosb = cosr[:, n, :].unsqueeze(1).unsqueeze(1).broadcast_to([128, 1, 2, 64])
                ssb = ssr[:, n, :].rearrange("p (h a d) -> p h a d", h=1, a=2)
                rope(v3(kr[:]), v3(ps[b][:, 0:128]), cosb, ssb, 64, 1, v3(tA[:]), v3(tB[:]), [pk(b)], [krk], tAk, tBk)
                q3, q3k = k3R.next()
                act(q3[:, 0, :], kr[:], AF.Copy, [krk], [q3k])
                act(q3[:, 1, :], kr[:], AF.Identity, [krk, "dcol", q3k], [q3k], scale=dcol[:, h, 0:1], bias=0.0)
                act(q3[:, 2, :], kr[:], AF.Identity, [krk, "dcol", q3k], [q3k], scale=dcol[:, h, 1:2], bias=0.0)
                sg, sgk = sgR.next()
                act(sg[:], ps[b][:, 128:384], AF.Silu, [pk(b)], [sgk])
                b2 = 2 + (n % 2)
                pb = ps[b2][:].bitcast(BF16)
                for j in range(3):
                    tp(pb[:, j * 128:(j + 1) * 128], q3[:, j, :], 128, [q3k], [pk(b2)])
                q3T, q3Tk = q3TR.next()
                cp(q3T[:].rearrange("p j t -> p (j t)"), pb[:, 0:384], [pk(b2)], [q3Tk])
                b3 = 4 + (n % 2)
                mm(ps[b3][:, 0:128], kT[:, ns], q3T[:, 0, :], True, True, [("kT", n), q3Tk], [pk(b3)])
                pt, ptk = ptR.next()
                tt(pt[:], ps[b3][:, 0:128], maskT[:, h, :], ALU.mult, [pk(b3), "maskT"], [ptk])
                Sf, Sfk = SfR.next()
                cp(Sf[:], Rf[:], ["Rf"], [Sfk])
                b4 = 6 + (n % 2)
                mm(ps[b4][:, 0:256], pt[:], vtm[:, n, :], True, False, [ptk, ("vtm", n)], [pk(b4)])
                mm(ps[b4][:, 0:256], q3T[:, 1, :], Sf[:], False, False, [q3Tk, Sfk], [pk(b4)])
                mm(ps[b4][:, 0:256], q3T[:, 2, :], Sb[:, n, :], False, True, [q3Tk, ("Sb", n)], [pk(b4)])
                mm(ps[b3][:, 256:512], kf[:, n, :], vtm[:, n, :], True, True, [("kf", n), ("vtm", n), ptk], [pk(b3)])
                stt(Rf[:], Rf[:], gC[:, h:h + 1], ps[b3][:, 256:512], ALU.mult, ALU.add, ["Rf", "gC", pk(b3), Sfk], ["Rf"])
                st_, sk = nstat()
                P.add("dve", lambda e, st_=st_, b4=b4: e.bn_stats(out=st_[:, 0:6], in_=ps[b4][:, 0:256]), r=[pk(b4)], w=[sk])
                P.add("dve", lambda e, st_=st_: e.bn_aggr(out=st_[:, 12:14], in_=st_[:, 0:6].rearrange("p (a b) -> p a b", a=1)), r=[sk], w=[sk])
                ts(st_[:, 14:15], st_[:, 13:14], LN_EPS, None, ALU.add, None, [sk], [sk])
                rsq(st_[:, 16:17], st_[:, 14:15], 1, [sk], [sk])
                stt(st_[:, 17:18], st_[:, 12:13], -1.0, st_[:, 16:17], ALU.mult, ALU.mult, [sk], [sk])
                ron, ronk = ronR.next()
                act(ron[:], ps[b4][:, 0:256], AF.Identity, [pk(b4), sk], [ronk], bias=st_[:, 17:18], scale=st_[:, 16:17])
                ab, abk = abR.next()
                tt(ab[:], ron[:], sg[:], ALU.mult, [ronk, sgk], [abk])
                for j in range(2):
                    tp(pb[:, 512 + j * 128:512 + (j + 1) * 128], ab[:, j * 128:(j + 1) * 128], 128, [abk, q3Tk], [pk(b2)])
                for j in range(2):
                    act(R2[:, 2 * h + j, ns], pb[:, 512 + j * 128:512 + (j + 1) * 128], AF.Identity, [pk(b2), "gncol"], [("R2", 2 * h + j)],
                        scale=gncol[:, 2 * h + j:2 * h + j + 1], bias=0.0)
        fence()
        arT.reset()

    def ln_epilogue(s, l, t, bps, gb, lng, lnb, zR, last, nexth=None):
        z, zk = zR.next()
        for hf in range(2):
            tt(z[:, hf * 512:(hf + 1) * 512], ps[bps[hf]][:, :], gb[:, hf * 512:(hf + 1) * 512], ALU.mult,
               [pk(bps[hf]), "gb"] + ([zk] if hf else []), [zk])
        stt(z[:], X[:, t, :], ALPHA, z[:], ALU.mult, ALU.add, [("X", t), zk], [zk])
        st_, sk = ln_stats(z, [zk], 0)
        act(z[:], z[:], AF.Identity, [zk, sk], [zk], bias=st_[:, 17:18], scale=st_[:, 16:17])
        tt(z[:], z[:], lng[:], ALU.mult, [zk, "lnrow"], [zk])
        tt(X[:, t, :], z[:], lnb[:], ALU.add, [zk, "lnrow"], [("X", t)])
        if last:
            dma("sp", y_d[s, t * 128:(t + 1) * 128, :], X[:, t, :], [("X", t)], [("y", s, t)])
        if nexth is not None:
            nexth(t)

    def phase_out(s, l):
        arR.reset()
        wout = keep["wout"]
        mTR = Ring(arT, 2, [128, KC, 128], BF16, "mT")
        xnR = Ring(arT, 2, [128, 1024], BF16, "xn")
        gb = arR.alloc([128, D], F32, "gb")
        lng = arR.alloc([128, D], F32, "lng")
        lnb = arR.alloc([128, D], F32, "lnb")
        zR = Ring(arR, 3, [128, D], F32, "z")
        dma("sp", gb[:], adag_d[s, l, 0:1, :].partition_broadcast(128), [("adag", s, l, 0)], ["gb"])
        dma("sp", lng[:], lnr_d[l, 0:1, :].partition_broadcast(128), [], ["lnrow"])
        dma("sp", lnb[:], lnr_d[l, 1:2, :].partition_broadcast(128), [], ["lnrow"])
        for t in range(NT):
            mT, mTk = mTR.next()
            qb = (t * 128) // QB
            dma("sp", mT[:], M_d[:, :, t * 128:(t + 1) * 128].rearrange("c p t -> p c t"), [("M", c, qb) for c in range(8)], [mTk])
            bps = (2 * (t % 2), 2 * (t % 2) + 1)
            for hf in range(2):
                for kc in range(KC):
                    mm(ps[bps[hf]][:, :], mT[:, kc, :], wout[:, kc, hf * 512:(hf + 1) * 512], kc == 0, kc == KC - 1, [mTk, "wout"], [pk(bps[hf])])
            ln_epilogue(s, l, t, bps, gb, lng, lnb, zR, False, nexth=lambda t_: h_tile(s, l, 1, t_, xnR, (4, 5, 6, 7)))
        fence()
        arT.reset()
        arR.reset()

    def phase_mlp(s, l, last):
        arT.reset()
        arR.reset()
        wd = arT.alloc([128, NFC, D], BF16, "wd")
        wd_top = arT.top
        convc = arT.alloc([128, NFC, 4], F32, "convc")
        dma("sp", convc[:], convc_d[l], [], ["convc"])
        wuR = Ring(arT, 2, [128, KC, 256], BF16, "wu")
        aext = arR.alloc([128, S + 2], F32, "aext")
        u = arR.alloc([128, S], F32, "u")
        geR = Ring(arR, 1, [128, S], BF16, "ge")
        gR = Ring(arR, 2, [128, S], BF16, "g")
        P.add("dve", lambda e: e.memset(aext[:, 0:1], 0.0), w=["aext"])
        P.add("dve", lambda e: e.memset(aext[:, S + 1:S + 2], 0.0), r=["aext"], w=["aext"])
        allH = [("HT", t) for t in range(NT)]
        i = 0
        for fc in range(NFC):
            wu, wuk_ = wuR.next()
            dma("pool", wu[:, :, 0:128], wup_d[l, :, :, fc * 128:(fc + 1) * 128], [], [wuk_])
            dma("pool", wu[:, :, 128:256], wup_d[l, :, :, FF + fc * 128:FF + (fc + 1) * 128], [], [wuk_])
            ge, gek = geR.next()
            g, gk = gR.next()
            bbs = []
            for qb in range(NQB):
                qs = slice(qb * QB, (qb + 1) * QB)
                ba = i % 4
                bb = 4 + (i % 4)
                i += 1
                bbs.append(bb)
                for kc in range(KC):
                    mm(ps[ba][:, 0:QB], wu[:, kc, 0:128], HT[:, kc, qs], kc == 0, kc == KC - 1, allH + [wuk_], [pk(ba)])
                for kc in range(KC):
                    mm(ps[bb][:, 0:QB], wu[:, kc, 128:256], HT[:, kc, qs], kc == 0, kc == KC - 1, allH + [wuk_], [pk(bb)])
                act(aext[:, 1 + qb * QB:1 + (qb + 1) * QB], ps[ba][:, 0:QB], AF.Copy, [pk(ba)], ["aext"])
            act(u[:], aext[:, 1:S + 1], AF.Identity, ["aext", "convc"], ["u"], scale=convc[:, fc, 1:2], bias=convc[:, fc, 3:4])
            stt(u[:], aext[:, 0:S], convc[:, fc, 0:1], u[:], ALU.mult, ALU.add, ["aext", "convc", "u"], ["u"])
            stt(u[:], aext[:, 2:S + 2], convc[:, fc, 2:3], u[:], ALU.mult, ALU.add, ["aext", "convc", "u"], ["u"])
            act(ge[:], u[:], AF.Gelu, ["u"], [gek])
            for qb in range(NQB):
                qs = slice(qb * QB, (qb + 1) * QB)
                tt(g[:, qs], ge[:, qs], ps[bbs[qb]][:, 0:QB], ALU.mult, [gek, pk(bbs[qb])] + ([gk] if qb else []), [gk])
            dma("sp", G_d[fc], g[:], [gk], [("G", fc)])
            if fc in (3, 10):
                q4 = 0 if fc == 3 else 1
                dma("pool", wd[:, q4 * 11:(q4 + 1) * 11, :], wdn_d[l, :, q4 * 11:(q4 + 1) * 11, :], [], ["wd"])
        fence()
        arR.reset()
        arT.top = wd_top
        xnR = Ring(arT, 2, [128, 1024], BF16, "xn")
        gtR = Ring(arR, 2, [128, NFC, 128], BF16, "gt")
        gb = arR.alloc([128, D], F32, "gb2")
        lng = arR.alloc([128, D], F32, "lng2")
        lnb = arR.alloc([128, D], F32, "lnb2")
        zR = Ring(arR, 2, [128, D], F32, "z2")
        dma("sp", gb[:], adag_d[s, l, 1:2, :].partition_broadcast(128), [("adag", s, l, 1)], ["gb"])
        dma("sp", lng[:], lnr_d[l, 2:3, :].partition_broadcast(128), [], ["lnrow"])
        dma("sp", lnb[:], lnr_d[l, 3:4, :].partition_broadcast(128), [], ["lnrow"])
        for t in range(NT):
            gt, gtk = gtR.next()
            dma("sp", gt[:], G_d[:, :, t * 128:(t + 1) * 128].rearrange("c p t -> p c t"), [("G", fc) for fc in range(NFC)], [gtk])
            bps = (2 * (t % 2), 2 * (t % 2) + 1)
            for hf in range(2):
                for fc in range(NFC):
                    mm(ps[bps[hf]][:, :], gt[:, fc, :], wd[:, fc, hf * 512:(hf + 1) * 512], fc == 0, fc == NFC - 1, [gtk, "wd"], [pk(bps[hf])])
            ln_epilogue(s, l, t, bps, gb, lng, lnb, zR, last,
                        nexth=None if last else (lambda t_: h_tile(s, l + 1, 0, t_, xnR, (4, 5, 6, 7))))
        fence()
        arT.reset()
        arR.reset()

    for s in range(NSEQ):
        for t in range(NT):
            dma("sp", X[:, t, :], x_d[s, t * 128:(t + 1) * 128, :], [], [("X", t)])
        for l in range(L):
            layer_consts(l)
            if l == 0:
                phase_H(s, l, 0)
            phase_mla(s, l)
            phase_gate(l, wmo_d, 3520 + 1024, True)
            phase_ret(s, l)
            phase_gate(l, wro_d, 3520, False)
            phase_out(s, l)
            phase_mlp(s, l, l == L - 1)
    P.add("sp", lambda e: None, r=[("y", s, t) for s in range(NSEQ) for t in range(NT)])

    es = ExitStack()
    P.emit(nc, es)
    es.close()
    return nc, P


def host_consts(S):
    NT = S // 128
    pos = np.arange(S, dtype=np.float32)

    def tables(dim):
        inv = (1.0 / (10000.0 ** (np.arange(0, dim, 2, dtype=np.float32) / np.float32(dim)))).astype(np.float32)
        ang = pos[:, None] * inv[None, :]
        return np.cos(ang).astype(np.float32), np.sin(ang).astype(np.float32)

    def tm(a):
        return np.ascontiguousarray(a.reshape(NT, 128, -1).transpose(1, 0, 2))

    cr, sr = tables(128)
    cm, sm = tables(64)
    k = np.arange(128, dtype=np.float32)[:, None]
    q = np.arange(128, dtype=np.float32)[None, :]
    cst4 = np.stack([np.maximum(q - k, 0.0), np.maximum(k - q, 0.0), (q >= k).astype(np.float32), (k > q).astype(np.float32)], axis=1)
    c = np.arange(128, dtype=np.float32)
    dexp = np.stack([c + 1.0, 128.0 - c, 127.0 - c, c], axis=1)
    return {
        "ident": np.eye(128, dtype=np.float32),
        "cosr": tm(cr), "ssr": tm(np.concatenate([-sr, sr], axis=1)),
        "cosm": tm(cm), "ssm": tm(np.concatenate([-sm, sm], axis=1)),
        "cst4": np.ascontiguousarray(cst4.astype(np.float32)), "dexp": np.ascontiguousarray(dexp.astype(np.float32)),
    }


def host_weights(inp, L):
    f = lambda a: np.ascontiguousarray(np.asarray(a, dtype=np.float32))

    def pk(w, kc):
        w = np.asarray(w, dtype=np.float32)
        return np.ascontiguousarray(w.reshape(L, kc, 128, w.shape[-1]).transpose(0, 2, 1, 3))

    def col(v, nb):
        v = np.asarray(v, dtype=np.float32)
        return np.ascontiguousarray(v.reshape(L, nb, 128).transpose(0, 2, 1))

    conv = np.concatenate([np.asarray(inp["conv_w"], np.float32), np.asarray(inp["conv_b"], np.float32)[:, None, :]], axis=1)
    convcol = np.ascontiguousarray(conv.reshape(L, 4, NFC, 128).transpose(0, 3, 2, 1))
    b_ada = np.asarray(inp["b_ada"], np.float32)
    return {
        "w_ada": pk(inp["w_ada"], KC), "b_ada": f(b_ada),
        "b_adacol": np.ascontiguousarray(b_ada.reshape(L, 48, 128).transpose(2, 0, 1)),
        "w_in": pk(inp["w_in"], KC),
        "dec": f(np.concatenate([np.asarray(inp["ret_decay_fwd"], np.float32), np.asarray(inp["ret_decay_bwd"], np.float32)], axis=1)),
        "gncol": col(inp["ret_gn_g"], 8), "w_ret_o": pk(inp["w_ret_o"], KC),
        "qgcol": col(inp["q_norm_g"], 2), "kvgcol": col(inp["kv_norm_g"], 1),
        "w_uq": pk(inp["w_uq"], 2), "w_uk": f(inp["w_uk"]), "w_uv": f(inp["w_uv"]),
        "w_mla_o": pk(inp["w_mla_o"], KC), "w_out": pk(inp["w_out"], KC),
        "lnrows": f(np.stack([np.asarray(inp[k], np.float32) for k in ("ln1_g", "ln1_b", "ln2_g", "ln2_b")], axis=1)),
        "w_up": pk(inp["w_up"], KC), "convcol": convcol, "w_down": pk(inp["w_down"], NFC),
    }


_cache = {}
_runkw = {}
_last = [None]


def run(xs, cs, inp, n_cores, L):
    NTOT, S, _ = xs.shape
    NSEQ = NTOT // n_cores
    key = (NSEQ, S, L)
    if key not in _cache:
        _cache[key] = build(NSEQ, S, L)[0]
    nc = _cache[key]
    shared = dict(host_consts(S))
    shared.update(host_weights(inp, L))
    in_maps = []
    for i in range(n_cores):
        m = dict(shared)
        m["x"] = np.ascontiguousarray(xs[i * NSEQ:(i + 1) * NSEQ])
        cc = cs[i * NSEQ:(i + 1) * NSEQ]
        m["cT"] = np.ascontiguousarray(cc.reshape(NSEQ, KC, 128).transpose(2, 1, 0))
        in_maps.append(m)
    res = run_bass_kernel_spmd(nc, in_maps, core_ids=list(range(n_cores)), **_runkw)
    _last[0] = res
    return np.concatenate([r["y"] for r in res.results], axis=0)


def kernel(**inp):
    xp = np.asarray(inp["x_prompt"], np.float32)
    xsm = np.asarray(inp["x_sample"], np.float32)
    xs = np.concatenate([xp, xsm], axis=0)
    cs = np.concatenate([np.asarray(inp["c_prompt"], np.float32), np.asarray(inp["c_sample"], np.float32)], axis=0)
    L = np.asarray(inp["w_in"]).shape[0]
    y = run(xs, cs, inp, 8, L)
    nb = xp.shape[0]
    return (np.ascontiguousarray(y[:nb]), np.ascontiguousarray(y[nb:]))
```

```python
import math
from contextlib import ExitStack

import numpy as np
import concourse.bass as bass
import concourse.mybir as mybir
from concourse.bass_utils import run_bass_kernel_spmd

F32 = mybir.dt.float32
BF16 = mybir.dt.bfloat16
AF = mybir.ActivationFunctionType
ALU = mybir.AluOpType

D = 1024
KC = 8
FF = 2816
NFC = 22
INW = 5568
LN_EPS = 1e-5
RMS_EPS = 1e-6
DEPTH_FULL = 4
ALPHA = (2.0 * DEPTH_FULL) ** 0.25
NSLOT = 12
AKEY = "__arena__"


class Op:
    __slots__ = ("eng", "fn", "deps", "dma", "slot", "slotval", "signal", "sigval", "idx", "epoch", "fence", "cost", "pos")


NEP = 8
SCHED = True
import os as _os
WINDOW = int(_os.environ.get('K_WINDOW', '48'))
PRIO = True
PRIO_EPS = 1.0
LAT_NS = float(_os.environ.get('K_LAT', '120'))
ENGS = ("pe", "act", "dve", "pool", "sp")


class Prog:
    def __init__(self):
        self.ops = []
        self.lastw = {}
        self.rd = {}
        self.epoch = 0

    def add(self, eng, fn, r=(), w=(), dma=False, fence=False, c=200.0):
        op = Op()
        op.eng, op.fn, op.dma, op.idx = eng, fn, dma, len(self.ops)
        op.signal = dma
        op.sigval = 0
        op.epoch = self.epoch
        op.fence = fence
        op.cost = c
        if fence:
            r, w = (), (AKEY,)
            self.epoch += 1
        else:
            r = tuple(r) + (AKEY,)
        deps = set()
        for k in r:
            d = self.lastw.get(k)
            if d is not None:
                deps.add(d)
            if isinstance(k, tuple) and k[0] == "ps":
                for d in self.rd.get(k, ()):
                    if self.ops[d].eng != eng:
                        deps.add(d)
        for k in w:
            d = self.lastw.get(k)
            if d is not None:
                deps.add(d)
            deps.update(self.rd.get(k, ()))
        deps.discard(op.idx)
        op.deps = deps
        for k in w:
            self.lastw[k] = op.idx
            self.rd[k] = []
        for k in r:
            self.rd.setdefault(k, []).append(op.idx)
        self.ops.append(op)
        return op

    def schedule(self):
        import heapq
        ops = self.ops
        order = {e: [] for e in ENGS}
        nep = self.epoch + 1
        byep = [[] for _ in range(nep)]
        for op in ops:
            byep[op.epoch].append(op.idx)
        done = [None] * len(ops)
        LAT = LAT_NS
        for ep in range(nep):
            idxs = byep[ep]
            if not idxs:
                continue
            if not SCHED:
                for i in idxs:
                    order[ops[i].eng].append(i)
                continue
            pend = {e: [i for i in idxs if ops[i].eng == e] for e in ENGS}
            head = {e: 0 for e in ENGS}
            taken = set()
            free = {e: 0.0 for e in ENGS}
            dmapipe = 0.0
            ev = [0.0]
            remaining = len(idxs)
            ldeps = {}
            for i in idxs:
                ldeps[i] = [d for d in ops[i].deps if ops[d].epoch == ep and not ops[d].fence]
            blev = {}
            if PRIO:
                for i in reversed(idxs):
                    b_ = blev.get(i, 0.0) + ops[i].cost + (2000.0 if ops[i].dma else LAT)
                    blev[i] = b_
                    for d in ldeps[i]:
                        if blev.get(d, 0.0) < b_:
                            blev[d] = b_
            while remaining:
                T = heapq.heappop(ev)
                while ev and ev[0] <= T:
                    heapq.heappop(ev)
                for e in ENGS:
                    if free[e] > T:
                        continue
                    pl = pend[e]
                    h = head[e]
                    while h < len(pl) and pl[h] in taken:
                        h += 1
                    head[e] = h
                    if h >= len(pl):
                        continue
                    pick = None
                    scanned = 0
                    j = h
                    while j < len(pl) and scanned < WINDOW:
                        i = pl[j]
                        j += 1
                        if i in taken:
                            continue
                        scanned += 1
                        ok = True
                        for d in ldeps[i]:
                            dt_ = done[d]
                            if dt_ is None or dt_ + LAT > T:
                                ok = False
                                break
                        if ok:
                            if not PRIO:
                                pick = i
                                break
                            if pick is None or blev[i] > blev[pick] + PRIO_EPS:
                                pick = i
                        if ops[i].fence:
                            break
                    if pick is None:
                        continue
                    op = ops[pick]
                    taken.add(pick)
                    remaining -= 1
                    order[e].append(pick)
                    if op.dma:
                        issue = 100.0 if e != "pool" else 600.0
                        free[e] = T + issue
                        st = max(T + issue, dmapipe)
                        dmapipe = st + op.cost
                        done[pick] = dmapipe + 2000.0
                    else:
                        free[e] = T + op.cost
                        done[pick] = T + op.cost
                    heapq.heappush(ev, free[e])
                    heapq.heappush(ev, done[pick] + LAT)
                if not ev and remaining:
                    raise RuntimeError("scheduler stuck")
        return order

    def finalize(self):
        ops = self.ops
        order = self.schedule()
        self.order = order
        for e in ENGS:
            for p, i in enumerate(order[e]):
                ops[i].pos = p
        dmak = {}
        for e in ENGS:
            for i in order[e]:
                op = ops[i]
                if op.dma:
                    k = dmak.get(op.eng, 0)
                    op.slot = (op.eng, k % NSLOT)
                    op.slotval = 16 * (k // NSLOT + 1)
                    dmak[op.eng] = k + 1
        lastfence = None
        fences = {}
        for op in ops:
            if op.fence:
                fences[op.epoch] = op.idx
        for op in ops:
            best = {}
            bestd = {}
            for d in op.deps:
                o = ops[d]
                if o.fence or o.epoch != op.epoch:
                    continue
                if o.dma:
                    if o.slot not in bestd or ops[bestd[o.slot]].slotval < o.slotval:
                        bestd[o.slot] = d
                else:
                    if o.eng == "pe" and op.eng == "pe" and not op.dma:
                        continue
                    if o.eng not in best or ops[best[o.eng]].pos < o.pos:
                        best[o.eng] = d
            op.deps = list(best.values()) + list(bestd.values())
            if op.epoch > 0:
                op.deps = [fences[op.epoch - 1]] + op.deps
            for d in op.deps:
                ops[d].signal = True
        cnt = {}
        nf = 0
        for op in ops:
            if op.fence:
                nf += 1
                op.sigval = nf
        for e in ENGS:
            for i in order[e]:
                op = ops[i]
                if not op.fence and not op.dma and op.signal:
                    key = (op.eng, op.epoch % NEP)
                    cnt[key] = cnt.get(key, 0) + 1
                    op.sigval = cnt[key]
        self.sigcounts = cnt
        self.dmacounts = dmak

    def emit(self, nc, es):
        self.finalize()
        ops = self.ops
        csem = {(e, j): es.enter_context(nc.semaphore("c_%s%d" % (e, j))) for e in ENGS for j in range(NEP)}
        fsem = es.enter_context(nc.semaphore("fence"))
        dsem = {}
        for q in ENGS:
            if self.dmacounts.get(q, 0) > 0:
                for s in range(NSLOT):
                    dsem[(q, s)] = es.enter_context(nc.semaphore("d_%s%d" % (q, s)))
        block = es.enter_context(nc.Block())
        order = self.order

        def run(e, eng):
            seen = {}
            for i in order[e]:
                op = ops[i]
                for d in op.deps:
                    o = ops[d]
                    if o.fence:
                        key, sem, val = "fence", fsem, o.sigval
                    elif o.dma:
                        key, sem, val = ("d",) + o.slot, dsem[o.slot], o.slotval
                    else:
                        key = ("c", o.eng, o.epoch % NEP)
                        sem, val = csem[key[1:]], o.sigval
                    if seen.get(key, 0) < val:
                        eng.wait_ge(sem, val)
                        seen[key] = val
                if op.dma:
                    prev = op.slotval - 16
                    if prev > 0 and seen.get(("d",) + op.slot, 0) < prev:
                        eng.wait_ge(dsem[op.slot], prev)
                        seen[("d",) + op.slot] = prev
                ins = op.fn(eng)
                if ins is None:
                    continue
                if op.fence:
                    ins.then_inc(fsem, 1)
                elif op.dma:
                    ins.then_inc(dsem[op.slot], 16)
                elif op.signal:
                    ins.then_inc(csem[(op.eng, op.epoch % NEP)], 1)

        @block.tensor
        def _(eng):
            run("pe", eng)

        @block.scalar
        def _(eng):
            run("act", eng)

        @block.vector
        def _(eng):
            run("dve", eng)

        @block.gpsimd
        def _(eng):
            run("pool", eng)

        @block.sync
        def _(eng):
            run("sp", eng)


_uid = [0]


class Arena:
    def __init__(self, nc, lo, hi):
        self.nc, self.lo, self.hi, self.top = nc, lo, hi, lo

    def alloc(self, shape, dt, name="t"):
        esz = 4 if dt == F32 else 2
        nb = esz
        for s in shape[1:]:
            nb *= s
        nb = (nb + 31) // 32 * 32
        off = self.top
        self.top += nb
        assert self.top <= self.hi, "SBUF arena overflow: %s %s need=%d over=%d" % (name, shape, nb, self.top - self.hi)
        _uid[0] += 1
        return self.nc.alloc_sbuf_tensor_at("%s_%d" % (name, _uid[0]), list(shape), dt, offset=off)

    def reset(self):
        self.top = self.lo


class Ring:
    def __init__(self, ar, n, shape, dt, name):
        self.t = [ar.alloc(shape, dt, name) for _ in range(n)]
        self.i = -1
        _uid[0] += 1
        self.name = "%s#%d" % (name, _uid[0])

    def next(self):
        self.i += 1
        j = self.i % len(self.t)
        return self.t[j], (self.name, j)


def build(NSEQ, S, L):
    NT = S // 128
    QB = min(512, S)
    NQB = S // QB
    TPQ = QB // 128
    C = 128
    nc = bass.Bass("TRN2", target_bir_lowering=False)

    def din(name, shape):
        return nc.dram_tensor(name, list(shape), F32, kind="ExternalInput").ap()

    x_d = din("x", [NSEQ, S, D])
    cT_d = din("cT", [128, KC, NSEQ])
    wada_d = din("w_ada", [L, 128, KC, 6 * D])
    bada_d = din("b_ada", [L, 6 * D])
    bcol_d = din("b_adacol", [128, L, 48])
    win_d = din("w_in", [L, 128, KC, INW])
    dec_d = din("dec", [L, 8])
    gn_d = din("gncol", [L, 128, 8])
    wro_d = din("w_ret_o", [L, 128, KC, D])
    qg_d = din("qgcol", [L, 128, 2])
    kvg_d = din("kvgcol", [L, 128, 1])
    wuq_d = din("w_uq", [L, 128, 2, 1536])
    wuk_d = din("w_uk", [L, 128, 1024])
    wuv_d = din("w_uv", [L, 128, 1024])
    wmo_d = din("w_mla_o", [L, 128, KC, D])
    wout_d = din("w_out", [L, 128, KC, D])
    lnr_d = din("lnrows", [L, 4, D])
    wup_d = din("w_up", [L, 128, KC, 2 * FF])
    convc_d = din("convcol", [L, 128, NFC, 4])
    wdn_d = din("w_down", [L, 128, NFC, D])
    ident_d = din("ident", [128, 128])
    cosr_d = din("cosr", [128, NT, 64])
    ssr_d = din("ssr", [128, NT, 128])
    cosm_d = din("cosm", [128, NT, 32])
    ssm_d = din("ssm", [128, NT, 64])
    cst4_d = din("cst4", [128, 4, 128])
    dexp_d = din("dexp", [128, 4])
    y_d = nc.dram_tensor("y", [NSEQ, S, D], F32, kind="ExternalOutput").ap()
    adag_d = nc.dram_tensor("adag_scr", [NSEQ, L, 2, D], F32, kind="Internal").ap()
    M_d = nc.dram_tensor("M_scr", [KC, 128, S], BF16, kind="Internal").ap()
    G_d = nc.dram_tensor("G_scr", [NFC, 128, S], BF16, kind="Internal").ap()

    P = Prog()
    ps = [nc.alloc_psum_tensor("ps%d" % i, [128, 512], F32) for i in range(8)]

    def pk(b):
        return ("ps", b)

    LO, HI = 16512, 229376
    XB = NT * 1024 * 4
    HB = max(8 * S * 2, 32768)
    off = LO
    arX = Arena(nc, off, off + XB); off += XB
    arH = Arena(nc, off, off + HB); off += HB
    arR = Arena(nc, off, off + HB); off += HB
    arC = Arena(nc, off, off + 26112); off += 26112
    arT = Arena(nc, off, HI)
    X = arX.alloc([128, NT, 1024], F32, "X")
    HT = arH.alloc([128, 8, S], BF16, "HT")
    R2 = arR.alloc([128, 8, S], BF16, "R2")
    ident = arC.alloc([128, 128], BF16, "ident")
    ones = arC.alloc([128, 128], F32, "ones")
    onesb = arC.alloc([128, 128], BF16, "onesb")
    cosr = arC.alloc([128, NT, 64], F32, "cosr")
    ssr = arC.alloc([128, NT, 128], F32, "ssr")
    cosm = arC.alloc([128, NT, 32], F32, "cosm")
    ssm = arC.alloc([128, NT, 64], F32, "ssm")
    cst4 = arC.alloc([128, 4, 128], F32, "cst4")
    dexp = arC.alloc([128, 4], F32, "dexp")
    cvals = arC.alloc([128, 4], F32, "cvals")
    adaT = arC.alloc([128, L, 4, 8, NSEQ], F32, "adaT")
    siluT = arC.alloc([128, KC, NSEQ], BF16, "siluT")
    cTs = arC.alloc([128, KC, NSEQ], F32, "cTs")
    dec = arC.alloc([128, 8], F32, "dec")
    lg = arC.alloc([128, 8], F32, "lg")
    gC = arC.alloc([128, 8], F32, "gC")
    dcol = arC.alloc([128, 4, 4], F32, "dcol")
    kplain = arC.alloc([128, 1], F32, "kplain")
    maskT = arC.alloc([128, 4, 128], F32, "maskT")
    gncol = arC.alloc([128, 8], F32, "gncol")
    qgcol = arC.alloc([128, 2], F32, "qgcol")
    kvgcol = arC.alloc([128, 1], F32, "kvgcol")
    stat = [arC.alloc([128, 24], F32, "stat%d" % i) for i in range(4)]
    stat_i = [-1]

    def nstat():
        stat_i[0] += 1
        j = stat_i[0] % 4
        return stat[j], ("stat", j)

    def mm(out, lhsT, rhs, st, sp, r, w):
        P.add("pe", lambda e: e.matmul(out, lhsT=lhsT, rhs=rhs, start=st, stop=sp), r=r, w=w, c=0.5 * rhs.free_size() + 30.0)

    def tp(out, in_, n, r, w):
        P.add("pe", lambda e: e.transpose(out, in_, ident[0:n, 0:n]), r=tuple(r) + ("ident",), w=w, c=100.0)

    def act(out, in_, func, r, w, bias=None, scale=None, accum=None):
        kw = {}
        if bias is not None:
            kw["bias"] = bias
        if scale is not None:
            kw["scale"] = scale
        if accum is not None:
            kw["accum_out"] = accum
        P.add("act", lambda e: e.activation(out=out, in_=in_, func=func, **kw), r=r, w=w, c=out.free_size() / 1.2 + 250.0)

    def tt(out, in0, in1, op, r, w, eng="dve"):
        P.add(eng, lambda e: e.tensor_tensor(out=out, in0=in0, in1=in1, op=op), r=r, w=w,
              c=out.free_size() / (0.9 if eng == "dve" else 0.45) + 150.0)

    def ts(out, in0, s1, s2, op0, op1, r, w):
        if s2 is None:
            P.add("dve", lambda e: e.tensor_scalar(out=out, in0=in0, scalar1=s1, scalar2=None, op0=op0), r=r, w=w, c=out.free_size() / 0.9 + 150.0)
        else:
            P.add("dve", lambda e: e.tensor_scalar(out=out, in0=in0, scalar1=s1, scalar2=s2, op0=op0, op1=op1), r=r, w=w, c=out.free_size() / 0.9 + 150.0)

    def stt(out, in0, sc, in1, op0, op1, r, w, eng="dve"):
        P.add(eng, lambda e: e.scalar_tensor_tensor(out=out, in0=in0, scalar=sc, in1=in1, op0=op0, op1=op1), r=r, w=w,
              c=out.free_size() / (0.9 if eng == "dve" else 0.45) + 150.0)

    def cp(out, in_, r, w, eng="dve"):
        if eng == "act":
            act(out, in_, AF.Copy, r, w)
            return
        k = 0.9 if eng == "dve" else 0.45
        P.add(eng, lambda e: e.tensor_copy(out=out, in_=in_), r=r, w=w, c=out.free_size() / k + 150.0)

    def dma(q, out, in_, r, w):
        P.add(q, lambda e: e.dma_start(out=out, in_=in_), r=r, w=w, dma=True, c=out.free_nbytes() * out.partition_size() / 120.0 + 300.0)

    def fence():
        P.add("sp", lambda e: e.nop(), fence=True)

    def rsq(out, in_, n, r, w):
        P.add("pool", lambda e: e.tensor_tensor(out=out, in0=in_, in1=cvals[:, 2:3], op=ALU.pow),
              r=tuple(r) + ("cvals",), w=w, c=600.0)

    def ln_stats(src_ap, srckeys, eps_col):
        stt_, sk = nstat()
        P.add("dve", lambda e: e.bn_stats(out=stt_[:, 0:6], in_=src_ap[:, 0:512]), r=srckeys, w=[sk])
        P.add("dve", lambda e: e.bn_stats(out=stt_[:, 6:12], in_=src_ap[:, 512:1024]), r=list(srckeys) + [sk], w=[sk])
        P.add("dve", lambda e: e.bn_aggr(out=stt_[:, 12:14], in_=stt_[:, 0:12].rearrange("p (a b) -> p a b", a=2)), r=[sk], w=[sk])
        ts(stt_[:, 14:15], stt_[:, 13:14], LN_EPS, None, ALU.add, None, [sk], [sk])
        rsq(stt_[:, 16:17], stt_[:, 14:15], 1, [sk], [sk])
        stt(stt_[:, 17:18], stt_[:, 12:13], -1.0, stt_[:, 16:17], ALU.mult, ALU.mult, [sk], [sk])
        return stt_, sk

    dma("pool", ident[:], ident_d, [], ["ident"])
    dma("sp", cosr[:], cosr_d, [], ["rope"])
    dma("sp", ssr[:], ssr_d, [], ["rope"])
    dma("sp", cosm[:], cosm_d, [], ["rope"])
    dma("sp", ssm[:], ssm_d, [], ["rope"])
    dma("sp", cst4[:], cst4_d, [], ["cst4"])
    dma("sp", dexp[:], dexp_d, [], ["dexp"])
    dma("sp", cTs[:], cT_d, [], ["cTs"])
    P.add("dve", lambda e: e.memset(ones[:], 1.0), w=["ones"])
    P.add("dve", lambda e: e.memset(onesb[:], 1.0), w=["onesb"])
    P.add("dve", lambda e: e.memset(cvals[:, 0:1], LN_EPS), w=["cvals"])
    P.add("dve", lambda e: e.memset(cvals[:, 1:2], RMS_EPS), r=["cvals"], w=["cvals"])
    P.add("dve", lambda e: e.memset(cvals[:, 2:3], -0.5), r=["cvals"], w=["cvals"])
    P.add("dve", lambda e: e.memset(cvals[:, 3:4], 1.0), r=["cvals"], w=["cvals"])
    P.add("dve", lambda e: e.memset(kplain[:], 128.0 ** -0.5), w=["kplain"])
    act(siluT[:], cTs[:], AF.Silu, ["cTs"], ["siluT"])

    bcol = arT.alloc([128, L, 48], F32, "bcol")
    browR = Ring(arT, 2, [1, 512], F32, "brow")
    dma("sp", bcol[:], bcol_d, [], ["bcol"])
    waR = Ring(arT, 2, [128, KC, 512], BF16, "wa")
    growR = Ring(arT, 2, [1, 512], F32, "grow")
    pr = [0]
    for l in range(L):
        for nb in range(12):
            wa, wak = waR.next()
            dma("pool", wa[:], wada_d[l, :, :, nb * 512:(nb + 1) * 512], [], [wak])
            blk = nb // 2
            if blk in (2, 5):
                brow, browk = browR.next()
                dma("sp", brow[:], bada_d[l:l + 1, nb * 512:(nb + 1) * 512], [], [browk])
                for s in range(NSEQ):
                    b = pr[0] % 4
                    pr[0] += 1
                    for kc in range(KC):
                        mm(ps[b][0:1, :], siluT[:, kc, s:s + 1], wa[:, kc, :], kc == 0, kc == KC - 1, [wak, "siluT"], [pk(b)])
                    gr, grk = growR.next()
                    n0 = nb * 512
                    tt(gr[:], ps[b][0:1, :], brow[:], ALU.add, [pk(b), browk], [grk])
                    ts(gr[:], gr[:], 1.0, None, ALU.add, None, [grk], [grk])
                    wi = 0 if blk == 2 else 1
                    hf = nb % 2
                    dma("sp", adag_d[s, l, wi:wi + 1, hf * 512:(hf + 1) * 512], gr[:], [grk], [("adag", s, l, wi)])
            else:
                wi = {0: 0, 1: 1, 3: 2, 4: 3}[blk]
                b = 4 + pr[0] % 4
                pr[0] += 1
                for j in range(4):
                    for kc in range(KC):
                        mm(ps[b][:, j * 8:j * 8 + NSEQ], wa[:, kc, j * 128:(j + 1) * 128], siluT[:, kc, :], kc == 0, kc == KC - 1,
                           [wak, "siluT"], [pk(b)])
                for j in range(4):
                    fb = (nb % 2) * 4 + j
                    cb = nb * 4 + j
                    act(adaT[:, l, wi, fb, :], ps[b][:, j * 8:j * 8 + NSEQ], AF.Identity, [pk(b), "bcol"], ["adaT"],
                        bias=bcol[:, l, cb:cb + 1], scale=1.0)
    for l in range(L):
        for wi in (1, 3):
            ts(adaT[:, l, wi], adaT[:, l, wi], 1.0, None, ALU.add, None, ["adaT"], ["adaT"])
    fence()
    arT.reset()

    def h_tile(s, l, which, t, xnR, banks):
        shw, scw = (0, 1) if which == 0 else (2, 3)
        st_, sk = ln_stats(X[:, t, :], [("X", t)], 0)
        xn, xk = xnR.next()
        act(xn[:], X[:, t, :], AF.Identity, [("X", t), sk], [xk], bias=st_[:, 17:18], scale=st_[:, 16:17])
        b = banks[t % len(banks)]
        pb = ps[b][:].bitcast(BF16)
        for kc in range(KC):
            tp(pb[:, kc * 128:(kc + 1) * 128], xn[:, kc * 128:(kc + 1) * 128], 128, [xk], [pk(b)])
        for kc in range(KC):
            act(HT[:, kc, t * 128:(t + 1) * 128], pb[:, kc * 128:(kc + 1) * 128], AF.Identity, [pk(b), "adaT"], [("HT", t)],
                bias=adaT[:, l, shw, kc, s:s + 1], scale=adaT[:, l, scw, kc, s:s + 1])

    def phase_H(s, l, which):
        xnR = Ring(arT, 3, [128, 1024], BF16, "xn")
        for t in range(NT):
            h_tile(s, l, which, t, xnR, (0, 1, 2, 3))
        fence()
        arT.reset()

    def layer_consts(l):
        dma("sp", dec[:], dec_d[l:l + 1, :].partition_broadcast(128), [], ["dec"])
        dma("sp", gncol[:], gn_d[l], [], ["gncol"])
        dma("sp", qgcol[:], qg_d[l], [], ["qgcol"])
        dma("sp", kvgcol[:], kvg_d[l], [], ["kvgcol"])
        act(lg[:], dec[:], AF.Sigmoid, ["dec"], ["lg"])
        act(lg[:], lg[:], AF.Ln, ["lg"], ["lg"])
        act(gC[:], lg[:], AF.Exp, ["lg"], ["gC"], scale=float(C))
        e1 = arT.alloc([128, 128], F32, "e1")
        e2 = arT.alloc([128, 128], F32, "e2")
        for h in range(4):
            act(e1[:], cst4[:, 0, :], AF.Exp, ["cst4", "lg"], ["e1"], scale=lg[:, h:h + 1])
            tt(e1[:], e1[:], cst4[:, 2, :], ALU.mult, ["e1", "cst4"], ["e1"])
            act(e2[:], cst4[:, 1, :], AF.Exp, ["cst4", "lg"], ["e2"], scale=lg[:, 4 + h:5 + h])
            tt(e2[:], e2[:], cst4[:, 3, :], ALU.mult, ["e2", "cst4"], ["e2"])
            tt(maskT[:, h, :], e1[:], e2[:], ALU.add, ["e1", "e2"], ["maskT"])
            act(dcol[:, h, 0:1], dexp[:, 0:1], AF.Exp, ["dexp", "lg"], ["dcol"], scale=lg[:, h:h + 1])
            act(dcol[:, h, 1:2], dexp[:, 1:2], AF.Exp, ["dexp", "lg"], ["dcol"], scale=lg[:, 4 + h:5 + h])
            act(dcol[:, h, 2:3], dexp[:, 2:3], AF.Exp, ["dexp", "lg"], ["dcol"], scale=lg[:, h:h + 1])
            act(dcol[:, h, 3:4], dexp[:, 3:4], AF.Exp, ["dexp", "lg"], ["dcol"], scale=lg[:, 4 + h:5 + h])
            ts(dcol[:, h, 2:4], dcol[:, h, 2:4], 128.0 ** -0.5, None, ALU.mult, None, ["dcol"], ["dcol"])
        fence()
        arT.reset()

    def rope(dst, src, cos_b, ss, half, nh, tmpA, tmpB, r, w, kA, kB, addeng="dve"):
        tt(tmpA, src, cos_b, ALU.mult, list(r) + ["rope"], [kA])
        tt(tmpB[:, :, 0, :], src[:, :, 1, :], ss[:, :, 0, :], ALU.mult, list(r) + ["rope"], [kB])
        tt(tmpB[:, :, 1, :], src[:, :, 0, :], ss[:, :, 1, :], ALU.mult, list(r) + ["rope", kB], [kB])
        tt(dst, tmpA, tmpB, ALU.add, [kA, kB], w, eng=addeng)

    def phase_mla(s, l):
        arT.reset()
        cqT = arT.alloc([128, 2, S], BF16, "cqT")
        ckvT = arT.alloc([128, S], BF16, "ckvT")
        krT = arT.alloc([128, 2, S], BF16, "krT")
        qrT = arT.alloc([128, 4, S], BF16, "qrT")
        P.add("dve", lambda e: e.memset(krT[:], 0.0), w=["krT0"], c=2 * S / 0.9 + 150.0)
        base_top = arT.top
        wl = arT.alloc([128, KC, 448], BF16, "wl")
        wuqr = arT.alloc([128, 2, 8, 64], BF16, "wuqr")
        dma("pool", wl[:], win_d[l, :, :, 3072:3520], [], ["wl"])
        dma("pool", wuqr[:], wuq_d[l].rearrange("p k (h d) -> p k h d", h=8)[:, :, :, 128:192], [], ["wuqr"])
        latR = Ring(arT, 2, [128, 512], BF16, "lat")
        tAR = Ring(arT, 1, [128, 512], F32, "tA")
        tBR = Ring(arT, 1, [128, 512], F32, "tB")
        tAsR = Ring(arT, 1, [128, 64], F32, "tAs")
        tBsR = Ring(arT, 1, [128, 64], F32, "tBs")
        sqj = arT.alloc([128, 256], BF16, "sqj")
        qrtR = Ring(arT, 2, [128, 512], BF16, "qrt")
        for t in range(NT):
            b = 0 + (t % 2)
            for kc in range(KC):
                mm(ps[b][:, 0:448], HT[:, kc, t * 128:(t + 1) * 128], wl[:, kc, :], kc == 0, kc == KC - 1, [("HT", t), "wl"], [pk(b)])
            st_, sk = nstat()
            act(sqj[:, 0:256], ps[b][:, 0:256], AF.Square, [pk(b)], ["sqj", sk], accum=st_[:, 0:1])
            act(sqj[:, 0:128], ps[b][:, 256:384], AF.Square, [pk(b), sk], ["sqj", sk], accum=st_[:, 1:2])
            ts(st_[:, 2:3], st_[:, 0:1], 1.0 / 256.0, RMS_EPS, ALU.mult, ALU.add, [sk], [sk])
            ts(st_[:, 3:4], st_[:, 1:2], 1.0 / 128.0, RMS_EPS, ALU.mult, ALU.add, [sk], [sk])
            rsq(st_[:, 4:5], st_[:, 2:3], 1, [sk], [sk])
            rsq(st_[:, 5:6], st_[:, 3:4], 1, [sk], [sk])
            lat, lk = latR.next()
            act(lat[:, 0:256], ps[b][:, 0:256], AF.Identity, [pk(b), sk], [lk], scale=st_[:, 4:5], bias=0.0)
            act(lat[:, 256:384], ps[b][:, 256:384], AF.Identity, [pk(b), sk, lk], [lk], scale=st_[:, 5:6], bias=0.0)
            tA, tAk = tAsR.next()
            tB, tBk = tBsR.next()
            v4 = lambda ap: ap.rearrange("p (h a d) -> p h a d", h=1, a=2)
            cosb = cosm[:, t, :].unsqueeze(1).unsqueeze(1).broadcast_to([128, 1, 2, 32])
            ssb = ssm[:, t, :].rearrange("p (h a d) -> p h a d", h=1, a=2)
            rope(v4(lat[:, 384:448]), v4(ps[b][:, 384:448]), cosb, ssb, 32, 1, v4(tA[:, 0:64]), v4(tB[:, 0:64]),
                 [pk(b), lk], [lk], tAk, tBk)
            cp(lat[:, 448:512], lat[:, 384:448], [lk], [lk])
            b2 = 2 + (t % 2)
            pb = ps[b2][:].bitcast(BF16)
            for j in range(4):
                tp(pb[:, j * 128:(j + 1) * 128], lat[:, j * 128:(j + 1) * 128], 128, [lk], [pk(b2)])
            tsl = slice(t * 128, (t + 1) * 128)
            act(cqT[:, 0, tsl], pb[:, 0:128], AF.Identity, [pk(b2), "qgcol"], [("cqT", t)], scale=qgcol[:, 0:1], bias=0.0)
            act(cqT[:, 1, tsl], pb[:, 128:256], AF.Identity, [pk(b2), "qgcol", ("cqT", t)], [("cqT", t)], scale=qgcol[:, 1:2], bias=0.0)
            act(ckvT[:, tsl], pb[:, 256:384], AF.Identity, [pk(b2), "kvgcol"], [("ckvT", t)], scale=kvgcol[:, 0:1], bias=0.0)
            cp(krT[0:64, 0, tsl], pb[0:64, 384:512], [pk(b2), "krT0"], [("krT", t)])
            cp(krT[64:128, 1, tsl], pb[64:128, 384:512], [pk(b2), "krT0", ("krT", t)], [("krT", t)])
            b3 = 4 + (t % 2)
            for kc in range(2):
                mm(ps[b3][:, 0:512], cqT[:, kc, tsl], wuqr[:, kc].rearrange("p h d -> p (h d)"), kc == 0, kc == 1,
                   [("cqT", t), "wuqr"], [pk(b3)])
            tA, tAk = tAR.next()
            tB, tBk = tBR.next()
            qrt, qk = qrtR.next()
            v8 = lambda ap: ap.rearrange("p (h a d) -> p h a d", h=8, a=2)
            cosb8 = cosm[:, t, :].unsqueeze(1).unsqueeze(1).broadcast_to([128, 8, 2, 32])
            ssb8 = ssm[:, t, :].rearrange("p (a d) -> p a d", a=2).unsqueeze(1).broadcast_to([128, 8, 2, 32])
            rope(v8(qrt[:]), v8(ps[b3][:, 0:512]), cosb8, ssb8, 32, 8, v8(tA[:]), v8(tB[:]), [pk(b3)], [qk], tAk, tBk)
            b4 = 6 + (t % 2)
            pb4 = ps[b4][:].bitcast(BF16)
            for j in range(4):
                tp(pb4[:, j * 128:(j + 1) * 128], qrt[:, j * 128:(j + 1) * 128], 128, [qk], [pk(b4)])
            cp(qrT[:, :, tsl], pb4[:, 0:512].rearrange("p (j t) -> p j t", j=4), [pk(b4)], [("qrT", t)])
        fence()
        arT.top = base_top
        wkvR = Ring(arT, 1, [128, 2, 128], BF16, "wkv")
        wqnR = Ring(arT, 2, [128, 2, 128], BF16, "wqn")
        knT = arT.alloc([128, S], BF16, "knT")
        vh = arT.alloc([128, NT, 128], BF16, "vh")
        qnR = Ring(arT, 1, [128, QB], BF16, "qn")
        ptR = Ring(arT, 5, [128, QB], BF16, "pt")
        accR = Ring(arT, 1, [128, QB], F32, "acc")
        scale = 192.0 ** -0.5
        sc_i = [0]
        allT = [("ckvT", t) for t in range(NT)]
        for h in range(8):
            wqn, wqk = wqnR.next()
            wkv, wkvk = wkvR.next()
            dma("pool", wqn[:], wuq_d[l, :, :, h * 192:h * 192 + 128], [], [wqk])
            dma("pool", wkv[:, 0, :], wuk_d[l, :, h * 128:(h + 1) * 128], [], [wkvk])
            dma("pool", wkv[:, 1, :], wuv_d[l, :, h * 128:(h + 1) * 128], [], [wkvk])
            for qb in range(NQB):
                qs = slice(qb * QB, (qb + 1) * QB)
                mm(ps[7][:, 0:QB], wkv[:, 0, :], ckvT[:, qs], True, True, allT + [wkvk], [pk(7)])
                cp(knT[:, qs], ps[7][:, 0:QB], [pk(7)], ["knT"])
            for t4 in range(0, NT, 4):
                n4 = min(4, NT - t4)
                for j in range(n4):
                    t = t4 + j
                    mm(ps[7][:, j * 128:(j + 1) * 128], ckvT[:, t * 128:(t + 1) * 128], wkv[:, 1, :], True, True,
                       allT + [wkvk], [pk(7)])
                act(vh[:, t4:t4 + n4, :], ps[7][:, 0:n4 * 128].rearrange("p (j d) -> p j d", j=n4), AF.Copy, [pk(7)], ["vh"])
            pb_ = (h % 2) * 64
            for qb in range(NQB):
                qs = slice(qb * QB, (qb + 1) * QB)
                qn, qnk = qnR.next()
                for kc in range(2):
                    mm(ps[7][:, 0:QB], wqn[:, kc, :], cqT[:, kc, qs], kc == 0, kc == 1, [("cqT", t) for t in range(NT)] + [wqk], [pk(7)])
                cp(qn[:], ps[7][:, 0:QB], [pk(7)], [qnk])
                bo = 3 + (qb % 2)
                bd = 5 + (qb % 2)
                acc, acck = accR.next()
                odd = [k_ for k_ in range(NT) if k_ % 2 == 1]
                pt1 = None
                for kt in range(NT):
                    bs = sc_i[0] % 3
                    sc_i[0] += 1
                    ks = slice(kt * 128, (kt + 1) * 128)
                    mm(ps[bs][:, 0:QB], knT[:, ks], qn[:], True, False, ["knT", qnk], [pk(bs)])
                    mm(ps[bs][:, 0:QB], krT[:, h % 2, ks], qrT[:, h // 2, qs], False, True,
                       [("krT", kt)] + [("qrT", t) for t in range(NT)], [pk(bs)])
                    pt, ptk = ptR.next()
                    act(pt[:], ps[bs][:, 0:QB], AF.Exp, [pk(bs)], [ptk], scale=scale)
                    mm(ps[bo][:, 0:QB], vh[:, kt, :], pt[:], kt == 0, kt == NT - 1, ["vh", ptk], [pk(bo)])
                    if kt % 2 == 0:
                        mm(ps[bd][:, 0:QB], onesb[:], pt[:], kt == 0, False, ["onesb", ptk], [pk(bd)])
                    elif len(odd) == 1:
                        cp(acc[:], pt[:], [ptk], [acck])
                    elif kt == 1:
                        pt1, pt1k = pt, ptk
                    elif kt == 3:
                        tt(acc[:], pt1[:], pt[:], ALU.add, [pt1k, ptk], [acck])
                    else:
                        tt(acc[:], acc[:], pt[:], ALU.add, [ptk, acck], [acck])
                mm(ps[bd][:, 0:QB], ones[:], acc[:], False, True, ["ones", acck], [pk(bd)])
                P.add("dve", lambda e, acc=acc, bd=bd: e.reciprocal(out=acc[:], in_=ps[bd][:, 0:QB]), r=[pk(bd)], w=[acck], c=700.0)
                tt(R2[:, h, qs], ps[bo][:, 0:QB], acc[:], ALU.mult, [pk(bo), acck], [("R2", h)])
        fence()
        arT.reset()

    keep = {}

    def phase_gate(l, wsrc_d, gcol0, first):
        arT.reset()
        if not first:
            keep["wout"] = arT.alloc([128, KC, D], BF16, "wout")
        keep_top = arT.top
        woR = Ring(arT, 2, [128, KC, 128], BF16, "wo")
        wgR = Ring(arT, 2, [128, KC, 128], BF16, "wg")
        sgR = Ring(arT, 2, [128, QB], F32, "sg")
        mbR = Ring(arT, 2, [128, QB], BF16, "mb")
        mpR = Ring(arT, 2, [128, QB], BF16, "mp")
        i = 0
        allH = [("HT", t) for t in range(NT)]
        allR = [("R2", h) for h in range(8)]
        for c in range(8):
            wo, wok = woR.next()
            wg, wgk = wgR.next()
            dma("pool", wo[:], wsrc_d[l, :, :, c * 128:(c + 1) * 128], [], [wok])
            dma("pool", wg[:], win_d[l, :, :, gcol0 + c * 128:gcol0 + (c + 1) * 128], [], [wgk])
            if c == 1 and not first:
                dma("pool", keep["wout"][:], wout_d[l], [], ["wout"])
            for qb in range(NQB):
                qs = slice(qb * QB, (qb + 1) * QB)
                by = (i % 2)
                bg = 2 + (i % 2)
                i += 1
                for kc in range(KC):
                    mm(ps[by][:, 0:QB], wo[:, kc, :], R2[:, kc, qs], kc == 0, kc == KC - 1, allR + [wok], [pk(by)])
                for kc in range(KC):
                    mm(ps[bg][:, 0:QB], wg[:, kc, :], HT[:, kc, qs], kc == 0, kc == KC - 1, allH + [wgk], [pk(bg)])
                sg, sgk = sgR.next()
                act(sg[:], ps[bg][:, 0:QB], AF.Sigmoid, [pk(bg)], [sgk])
                mb, mbk = mbR.next()
                if first:
                    tt(mb[:], ps[by][:, 0:QB], sg[:], ALU.mult, [pk(by), sgk], [mbk])
                else:
                    mp, mpk = mpR.next()
                    dma("sp", mp[:], M_d[c, :, qs], [("M", c, qb)], [mpk])
                    tt(sg[:], ps[by][:, 0:QB], sg[:], ALU.mult, [pk(by), sgk], [sgk])
                    tt(mb[:], sg[:], mp[:], ALU.add, [sgk, mpk], [mbk])
                dma("sp", M_d[c, :, qs], mb[:], [mbk], [("M", c, qb)])
        fence()
        arT.top = keep_top

    def phase_ret(s, l):
        arT.reset()
        wr = arT.alloc([128, KC, 768], BF16, "wr")
        kT = arT.alloc([128, S], BF16, "kT")
        kf = arT.alloc([128, NT, 128], BF16, "kf")
        vtm = arT.alloc([128, NT, 256], BF16, "vtm")
        Sb = arT.alloc([128, NT, 256], BF16, "Sb")
        Rf = arT.alloc([128, 256], F32, "Rf")
        Rb = arT.alloc([128, 256], F32, "Rb")
        SfR = Ring(arT, 2, [128, 256], BF16, "Sf")
        tAR = Ring(arT, 2, [128, 128], F32, "rA")
        tBR = Ring(arT, 2, [128, 128], F32, "rB")
        krR = Ring(arT, 2, [128, 128], F32, "kr")
        k3R = Ring(arT, 2, [128, 3, 128], BF16, "k3")
        q3TR = Ring(arT, 2, [128, 3, 128], BF16, "q3T")
        sgR = Ring(arT, 2, [128, 256], BF16, "sgr")
        ptR = Ring(arT, 2, [128, 128], BF16, "ptr")
        ronR = Ring(arT, 2, [128, 256], F32, "ron")
        abR = Ring(arT, 2, [128, 256], BF16, "ab")
        v3 = lambda ap: ap.rearrange("p (h a d) -> p h a d", h=1, a=2)
        for h in range(4):
            dma("pool", wr[:, :, 384:512], win_d[l, :, :, 512 + h * 128:512 + (h + 1) * 128], [], ["wrkv"])
            dma("pool", wr[:, :, 512:768], win_d[l, :, :, 1024 + h * 256:1024 + (h + 1) * 256], [], ["wrkv"])
            dma("pool", wr[:, :, 0:128], win_d[l, :, :, h * 128:(h + 1) * 128], [], ["wrqg"])
            dma("pool", wr[:, :, 128:384], win_d[l, :, :, 2048 + h * 256:2048 + (h + 1) * 256], [], ["wrqg"])
            P.add("dve", lambda e: e.memset(Rf[:], 0.0), w=["Rf"])
            P.add("dve", lambda e: e.memset(Rb[:], 0.0), w=["Rb"])
            for n in range(NT - 1, -1, -1):
                ns = slice(n * 128, (n + 1) * 128)
                b = n % 2
                for kc in range(KC):
                    mm(ps[b][:, 0:384], HT[:, kc, ns], wr[:, kc, 384:768], kc == 0, kc == KC - 1, [("HT", n), "wrkv"], [pk(b)])
                tA, tAk = tAR.next()
                tB, tBk = tBR.next()
                kr, krk = krR.next()
                cosb = cosr[:, n, :].unsqueeze(1).unsqueeze(1).broadcast_to([128, 1, 2, 64])
                ssb = ssr[:, n, :].rearrange("p (h a d) -> p h a d", h=1, a=2)
                rope(v3(kr[:]), v3(ps[b][:, 0:128]), cosb, ssb, 64, 1, v3(tA[:]), v3(tB[:]), [pk(b)], [krk], tAk, tBk)
                k3, k3k = k3R.next()
                act(k3[:, 0, :], kr[:], AF.Identity, [krk, "kplain"], [k3k], scale=kplain[:, 0:1], bias=0.0)
                act(kf[:, n, :], kr[:], AF.Identity, [krk, "dcol"], [("kf", n)], scale=dcol[:, h, 2:3], bias=0.0)
                act(k3[:, 2, :], kr[:], AF.Identity, [krk, "dcol", k3k], [k3k], scale=dcol[:, h, 3:4], bias=0.0)
                cp(vtm[:, n, :], ps[b][:, 128:384], [pk(b)], [("vtm", n)], eng="act")
                b2 = 2 + (n % 2)
                pb = ps[b2][:].bitcast(BF16)
                tp(pb[:, 0:128], k3[:, 0, :], 128, [k3k], [pk(b2)])
                cp(kT[:, ns], pb[:, 0:128], [pk(b2)], [("kT", n)], eng="act")
                cp(Sb[:, n, :], Rb[:], ["Rb"], [("Sb", n)])
                b3 = 4 + (n % 2)
                mm(ps[b3][:, 0:256], k3[:, 2, :], vtm[:, n, :], True, True, [k3k, ("vtm", n)], [pk(b3)])
                stt(Rb[:], Rb[:], gC[:, 4 + h:5 + h], ps[b3][:, 0:256], ALU.mult, ALU.add, ["Rb", "gC", pk(b3)], ["Rb"])
            for n in range(NT):
                ns = slice(n * 128, (n + 1) * 128)
                b = n % 2
                for kc in range(KC):
                    mm(ps[b][:, 0:384], HT[:, kc, ns], wr[:, kc, 0:384], kc == 0, kc == KC - 1, [("HT", n), "wrqg"], [pk(b)])
                tA, tAk = tAR.next()
                tB, tBk = tBR.next()
                kr, krk = krR.next()
                cosb = cosr[:, n, :].unsqueeze(1).unsqueeze(1).broadcast_to([128, 1, 2, 64])
                ssb = ssr[:, n, :].rearrange("p (h a d) -> p h a d", h=1, a=2)
                rope(v3(kr[:]), v3(ps[b][:, 0:128]), cosb, ssb, 64, 1, v3(tA[:]), v3(tB[:]), [pk(b)], [krk], tAk, tBk)
                q3, q3k = k3R.next()
                act(q3[:, 0, :], kr[:], AF.Copy, [krk], [q3k])
                act(q3[:, 1, :], kr[:], AF.Identity, [krk, "dcol", q3k], [q3k], scale=dcol[:, h, 0:1], bias=0.0)
                act(q3[:, 2, :], kr[:], AF.Identity, [krk, "dcol", q3k], [q3k], scale=dcol[:, h, 1:2], bias=0.0)
                sg, sgk = sgR.next()
                act(sg[:], ps[b][:, 128:384], AF.Silu, [pk(b)], [sgk])
                b2 = 2 + (n % 2)
                pb = ps[b2][:].bitcast(BF16)
                for j in range(3):
                    tp(pb[:, j * 128:(j + 1) * 128], q3[:, j, :], 128, [q3k], [pk(b2)])
                q3T, q3Tk = q3TR.next()
                cp(q3T[:].rearrange("p j t -> p (j t)"), pb[:, 0:384], [pk(b2)], [q3Tk])
                b3 = 4 + (n % 2)
                mm(ps[b3][:, 0:128], kT[:, ns], q3T[:, 0, :], True, True, [("kT", n), q3Tk], [pk(b3)])
                pt, ptk = ptR.next()
                tt(pt[:], ps[b3][:, 0:128], maskT[:, h, :], ALU.mult, [pk(b3), "maskT"], [ptk])
                Sf, Sfk = SfR.next()
                cp(Sf[:], Rf[:], ["Rf"], [Sfk])
                b4 = 6 + (n % 2)
                mm(ps[b4][:, 0:256], pt[:], vtm[:, n, :], True, False, [ptk, ("vtm", n)], [pk(b4)])
                mm(ps[b4][:, 0:256], q3T[:, 1, :], Sf[:], False, False, [q3Tk, Sfk], [pk(b4)])
                mm(ps[b4][:, 0:256], q3T[:, 2, :], Sb[:, n, :], False, True, [q3Tk, ("Sb", n)], [pk(b4)])
                mm(ps[b3][:, 256:512], kf[:, n, :], vtm[:, n, :], True, True, [("kf", n), ("vtm", n), ptk], [pk(b3)])
                stt(Rf[:], Rf[:], gC[:, h:h + 1], ps[b3][:, 256:512], ALU.mult, ALU.add, ["Rf", "gC", pk(b3), Sfk], ["Rf"])
                st_, sk = nstat()
                P.add("dve", lambda e, st_=st_, b4=b4: e.bn_stats(out=st_[:, 0:6], in_=ps[b4][:, 0:256]), r=[pk(b4)], w=[sk])
                P.add("dve", lambda e, st_=st_: e.bn_aggr(out=st_[:, 12:14], in_=st_[:, 0:6].rearrange("p (a b) -> p a b", a=1)), r=[sk], w=[sk])
                ts(st_[:, 14:15], st_[:, 13:14], LN_EPS, None, ALU.add, None, [sk], [sk])
                rsq(st_[:, 16:17], st_[:, 14:15], 1, [sk], [sk])
                stt(st_[:, 17:18], st_[:, 12:13], -1.0, st_[:, 16:17], ALU.mult, ALU.mult, [sk], [sk])
                ron, ronk = ronR.next()
                act(ron[:], ps[b4][:, 0:256], AF.Identity, [pk(b4), sk], [ronk], bias=st_[:, 17:18], scale=st_[:, 16:17])
                ab, abk = abR.next()
                tt(ab[:], ron[:], sg[:], ALU.mult, [ronk, sgk], [abk])
                for j in range(2):
                    tp(pb[:, 512 + j * 128:512 + (j + 1) * 128], ab[:, j * 128:(j + 1) * 128], 128, [abk, q3Tk], [pk(b2)])
                for j in range(2):
                    act(R2[:, 2 * h + j, ns], pb[:, 512 + j * 128:512 + (j + 1) * 128], AF.Identity, [pk(b2), "gncol"], [("R2", 2 * h + j)],
                        scale=gncol[:, 2 * h + j:2 * h + j + 1], bias=0.0)
        fence()
        arT.reset()

    def ln_epilogue(s, l, t, bps, gb, lng, lnb, zR, last, nexth=None):
        z, zk = zR.next()
        for hf in range(2):
            tt(z[:, hf * 512:(hf + 1) * 512], ps[bps[hf]][:, :], gb[:, hf * 512:(hf + 1) * 512], ALU.mult,
               [pk(bps[hf]), "gb"] + ([zk] if hf else []), [zk])
        stt(z[:], X[:, t, :], ALPHA, z[:], ALU.mult, ALU.add, [("X", t), zk], [zk])
        st_, sk = ln_stats(z, [zk], 0)
        act(z[:], z[:], AF.Identity, [zk, sk], [zk], bias=st_[:, 17:18], scale=st_[:, 16:17])
        tt(z[:], z[:], lng[:], ALU.mult, [zk, "lnrow"], [zk])
        tt(X[:, t, :], z[:], lnb[:], ALU.add, [zk, "lnrow"], [("X", t)])
        if last:
            dma("sp", y_d[s, t * 128:(t + 1) * 128, :], X[:, t, :], [("X", t)], [("y", s, t)])
        if nexth is not None:
            nexth(t)

    def phase_out(s, l):
        arR.reset()
        wout = keep["wout"]
        mTR = Ring(arT, 2, [128, KC, 128], BF16, "mT")
        xnR = Ring(arT, 2, [128, 1024], BF16, "xn")
        gb = arR.alloc([128, D], F32, "gb")
        lng = arR.alloc([128, D], F32, "lng")
        lnb = arR.alloc([128, D], F32, "lnb")
        zR = Ring(arR, 3, [128, D], F32, "z")
        dma("sp", gb[:], adag_d[s, l, 0:1, :].partition_broadcast(128), [("adag", s, l, 0)], ["gb"])
        dma("sp", lng[:], lnr_d[l, 0:1, :].partition_broadcast(128), [], ["lnrow"])
        dma("sp", lnb[:], lnr_d[l, 1:2, :].partition_broadcast(128), [], ["lnrow"])
        for t in range(NT):
            mT, mTk = mTR.next()
            qb = (t * 128) // QB
            dma("sp", mT[:], M_d[:, :, t * 128:(t + 1) * 128].rearrange("c p t -> p c t"), [("M", c, qb) for c in range(8)], [mTk])
            bps = (2 * (t % 2), 2 * (t % 2) + 1)
            for hf in range(2):
                for kc in range(KC):
                    mm(ps[bps[hf]][:, :], mT[:, kc, :], wout[:, kc, hf * 512:(hf + 1) * 512], kc == 0, kc == KC - 1, [mTk, "wout"], [pk(bps[hf])])
            ln_epilogue(s, l, t, bps, gb, lng, lnb, zR, False, nexth=lambda t_: h_tile(s, l, 1, t_, xnR, (4, 5, 6, 7)))
        fence()
        arT.reset()
        arR.reset()

    def phase_mlp(s, l, last):
        arT.reset()
        arR.reset()
        wd = arT.alloc([128, NFC, D], BF16, "wd")
        wd_top = arT.top
        convc = arT.alloc([128, NFC, 4], F32, "convc")
        dma("sp", convc[:], convc_d[l], [], ["convc"])
        wuR = Ring(arT, 2, [128, KC, 256], BF16, "wu")
        aext = arR.alloc([128, S + 2], F32, "aext")
        u = arR.alloc([128, S], F32, "u")
        geR = Ring(arR, 1, [128, S], BF16, "ge")
        gR = Ring(arR, 2, [128, S], BF16, "g")
        P.add("dve", lambda e: e.memset(aext[:, 0:1], 0.0), w=["aext"])
        P.add("dve", lambda e: e.memset(aext[:, S + 1:S + 2], 0.0), r=["aext"], w=["aext"])
        allH = [("HT", t) for t in range(NT)]
        i = 0
        for fc in range(NFC):
            wu, wuk_ = wuR.next()
            dma("pool", wu[:, :, 0:128], wup_d[l, :, :, fc * 128:(fc + 1) * 128], [], [wuk_])
            dma("pool", wu[:, :, 128:256], wup_d[l, :, :, FF + fc * 128:FF + (fc + 1) * 128], [], [wuk_])
            ge, gek = geR.next()
            g, gk = gR.next()
            bbs = []
            for qb in range(NQB):
                qs = slice(qb * QB, (qb + 1) * QB)
                ba = i % 4
                bb = 4 + (i % 4)
                i += 1
                bbs.append(bb)
                for kc in range(KC):
                    mm(ps[ba][:, 0:QB], wu[:, kc, 0:128], HT[:, kc, qs], kc == 0, kc == KC - 1, allH + [wuk_], [pk(ba)])
                for kc in range(KC):
                    mm(ps[bb][:, 0:QB], wu[:, kc, 128:256], HT[:, kc, qs], kc == 0, kc == KC - 1, allH + [wuk_], [pk(bb)])
                act(aext[:, 1 + qb * QB:1 + (qb + 1) * QB], ps[ba][:, 0:QB], AF.Copy, [pk(ba)], ["aext"])
            act(u[:], aext[:, 1:S + 1], AF.Identity, ["aext", "convc"], ["u"], scale=convc[:, fc, 1:2], bias=convc[:, fc, 3:4])
            stt(u[:], aext[:, 0:S], convc[:, fc, 0:1], u[:], ALU.mult, ALU.add, ["aext", "convc", "u"], ["u"])
            stt(u[:], aext[:, 2:S + 2], convc[:, fc, 2:3], u[:], ALU.mult, ALU.add, ["aext", "convc", "u"], ["u"])
            act(ge[:], u[:], AF.Gelu, ["u"], [gek])
            for qb in range(NQB):
                qs = slice(qb * QB, (qb + 1) * QB)
                tt(g[:, qs], ge[:, qs], ps[bbs[qb]][:, 0:QB], ALU.mult, [gek, pk(bbs[qb])] + ([gk] if qb else []), [gk])
            dma("sp", G_d[fc], g[:], [gk], [("G", fc)])
            if fc in (3, 10):
                q4 = 0 if fc == 3 else 1
                dma("pool", wd[:, q4 * 11:(q4 + 1) * 11, :], wdn_d[l, :, q4 * 11:(q4 + 1) * 11, :], [], ["wd"])
        fence()
        arR.reset()
        arT.top = wd_top
        xnR = Ring(arT, 2, [128, 1024], BF16, "xn")
        gtR = Ring(arR, 2, [128, NFC, 128], BF16, "gt")
        gb = arR.alloc([128, D], F32, "gb2")
        lng = arR.alloc([128, D], F32, "lng2")
        lnb = arR.alloc([128, D], F32, "lnb2")
        zR = Ring(arR, 2, [128, D], F32, "z2")
        dma("sp", gb[:], adag_d[s, l, 1:2, :].partition_broadcast(128), [("adag", s, l, 1)], ["gb"])
        dma("sp", lng[:], lnr_d[l, 2:3, :].partition_broadcast(128), [], ["lnrow"])
        dma("sp", lnb[:], lnr_d[l, 3:4, :].partition_broadcast(128), [], ["lnrow"])
        for t in range(NT):
            gt, gtk = gtR.next()
            dma("sp", gt[:], G_d[:, :, t * 128:(t + 1) * 128].rearrange("c p t -> p c t"), [("G", fc) for fc in range(NFC)], [gtk])
            bps = (2 * (t % 2), 2 * (t % 2) + 1)
            for hf in range(2):
                for fc in range(NFC):
                    mm(ps[bps[hf]][:, :], gt[:, fc, :], wd[:, fc, hf * 512:(hf + 1) * 512], fc == 0, fc == NFC - 1, [gtk, "wd"], [pk(bps[hf])])
            ln_epilogue(s, l, t, bps, gb, lng, lnb, zR, last,
                        nexth=None if last else (lambda t_: h_tile(s, l + 1, 0, t_, xnR, (4, 5, 6, 7))))
        fence()
        arT.reset()
        arR.reset()

    for s in range(NSEQ):
        for t in range(NT):
            dma("sp", X[:, t, :], x_d[s, t * 128:(t + 1) * 128, :], [], [("X", t)])
        for l in range(L):
            layer_consts(l)
            if l == 0:
                phase_H(s, l, 0)
            phase_mla(s, l)
            phase_gate(l, wmo_d, 3520 + 1024, True)
            phase_ret(s, l)
            phase_gate(l, wro_d, 3520, False)
            phase_out(s, l)
            phase_mlp(s, l, l == L - 1)
    P.add("sp", lambda e: None, r=[("y", s, t) for s in range(NSEQ) for t in range(NT)])

    es = ExitStack()
    P.emit(nc, es)
    es.close()
    return nc, P


def host_consts(S):
    NT = S // 128
    pos = np.arange(S, dtype=np.float32)

    def tables(dim):
        inv = (1.0 / (10000.0 ** (np.arange(0, dim, 2, dtype=np.float32) / np.float32(dim)))).astype(np.float32)
        ang = pos[:, None] * inv[None, :]
        return np.cos(ang).astype(np.float32), np.sin(ang).astype(np.float32)

    def tm(a):
        return np.ascontiguousarray(a.reshape(NT, 128, -1).transpose(1, 0, 2))

    cr, sr = tables(128)
    cm, sm = tables(64)
    k = np.arange(128, dtype=np.float32)[:, None]
    q = np.arange(128, dtype=np.float32)[None, :]
    cst4 = np.stack([np.maximum(q - k, 0.0), np.maximum(k - q, 0.0), (q >= k).astype(np.float32), (k > q).astype(np.float32)], axis=1)
    c = np.arange(128, dtype=np.float32)
    dexp = np.stack([c + 1.0, 128.0 - c, 127.0 - c, c], axis=1)
    return {
        "ident": np.eye(128, dtype=np.float32),
        "cosr": tm(cr), "ssr": tm(np.concatenate([-sr, sr], axis=1)),
        "cosm": tm(cm), "ssm": tm(np.concatenate([-sm, sm], axis=1)),
        "cst4": np.ascontiguousarray(cst4.astype(np.float32)), "dexp": np.ascontiguousarray(dexp.astype(np.float32)),
    }


def host_weights(inp, L):
    f = lambda a: np.ascontiguousarray(np.asarray(a, dtype=np.float32))

    def pk(w, kc):
        w = np.asarray(w, dtype=np.float32)
        return np.ascontiguousarray(w.reshape(L, kc, 128, w.shape[-1]).transpose(0, 2, 1, 3))

    def col(v, nb):
        v = np.asarray(v, dtype=np.float32)
        return np.ascontiguousarray(v.reshape(L, nb, 128).transpose(0, 2, 1))

    conv = np.concatenate([np.asarray(inp["conv_w"], np.float32), np.asarray(inp["conv_b"], np.float32)[:, None, :]], axis=1)
    convcol = np.ascontiguousarray(conv.reshape(L, 4, NFC, 128).transpose(0, 3, 2, 1))
    b_ada = np.asarray(inp["b_ada"], np.float32)
    return {
        "w_ada": pk(inp["w_ada"], KC), "b_ada": f(b_ada),
        "b_adacol": np.ascontiguousarray(b_ada.reshape(L, 48, 128).transpose(2, 0, 1)),
        "w_in": pk(inp["w_in"], KC),
        "dec": f(np.concatenate([np.asarray(inp["ret_decay_fwd"], np.float32), np.asarray(inp["ret_decay_bwd"], np.float32)], axis=1)),
        "gncol": col(inp["ret_gn_g"], 8), "w_ret_o": pk(inp["w_ret_o"], KC),
        "qgcol": col(inp["q_norm_g"], 2), "kvgcol": col(inp["kv_norm_g"], 1),
        "w_uq": pk(inp["w_uq"], 2), "w_uk": f(inp["w_uk"]), "w_uv": f(inp["w_uv"]),
        "w_mla_o": pk(inp["w_mla_o"], KC), "w_out": pk(inp["w_out"], KC),
        "lnrows": f(np.stack([np.asarray(inp[k], np.float32) for k in ("ln1_g", "ln1_b", "ln2_g", "ln2_b")], axis=1)),
        "w_up": pk(inp["w_up"], KC), "convcol": convcol, "w_down": pk(inp["w_down"], NFC),
    }


_cache = {}
_runkw = {}
_last = [None]


def run(xs, cs, inp, n_cores, L):
    NTOT, S, _ = xs.shape
    NSEQ = NTOT // n_cores
    key = (NSEQ, S, L)
    if key not in _cache:
        _cache[key] = build(NSEQ, S, L)[0]
    nc = _cache[key]
    shared = dict(host_consts(S))
    shared.update(host_weights(inp, L))
    in_maps = []
    for i in range(n_cores):
        m = dict(shared)
        m["x"] = np.ascontiguousarray(xs[i * NSEQ:(i + 1) * NSEQ])
        cc = cs[i * NSEQ:(i + 1) * NSEQ]
        m["cT"] = np.ascontiguousarray(cc.reshape(NSEQ, KC, 128).transpose(2, 1, 0))
        in_maps.append(m)
    res = run_bass_kernel_spmd(nc, in_maps, core_ids=list(range(n_cores)), **_runkw)
    _last[0] = res
    return np.concatenate([r["y"] for r in res.results], axis=0)


def kernel(**inp):
    xp = np.asarray(inp["x_prompt"], np.float32)
    xsm = np.asarray(inp["x_sample"], np.float32)
    xs = np.concatenate([xp, xsm], axis=0)
    cs = np.concatenate([np.asarray(inp["c_prompt"], np.float32), np.asarray(inp["c_sample"], np.float32)], axis=0)
    L = np.asarray(inp["w_in"]).shape[0]
    y = run(xs, cs, inp, 8, L)
    nb = xp.shape[0]
    return (np.ascontiguousarray(y[:nb]), np.ascontiguousarray(y[nb:]))
```

```python
import math
from contextlib import ExitStack

import numpy as np
import concourse.bass as bass
import concourse.mybir as mybir
from concourse.bass_utils import run_bass_kernel_spmd

F32 = mybir.dt.float32
BF16 = mybir.dt.bfloat16
AF = mybir.ActivationFunctionType
ALU = mybir.AluOpType

D = 1024
KC = 8
FF = 2816
NFC = 22
INW = 5568
LN_EPS = 1e-5
RMS_EPS = 1e-6
DEPTH_FULL = 4
ALPHA = (2.0 * DEPTH_FULL) ** 0.25
NSLOT = 12
AKEY = "__arena__"


class Op:
    __slots__ = ("eng", "fn", "deps", "dma", "slot", "slotval", "signal", "sigval", "idx", "epoch", "fence", "cost", "pos")


NEP = 8
SCHED = True
import os as _os
WINDOW = int(_os.environ.get('K_WINDOW', '48'))
PRIO = True
PRIO_EPS = 1.0
LAT_NS = float(_os.environ.get('K_LAT', '120'))
ENGS = ("pe", "act", "dve", "pool", "sp")


class Prog:
    def __init__(self):
        self.ops = []
        self.lastw = {}
        self.rd = {}
        self.epoch = 0

    def add(self, eng, fn, r=(), w=(), dma=False, fence=False, c=200.0):
        op = Op()
        op.eng, op.fn, op.dma, op.idx = eng, fn, dma, len(self.ops)
        op.signal = dma
        op.sigval = 0
        op.epoch = self.epoch
        op.fence = fence
        op.cost = c
        if fence:
            r, w = (), (AKEY,)
            self.epoch += 1
        else:
            r = tuple(r) + (AKEY,)
        deps = set()
        for k in r:
            d = self.lastw.get(k)
            if d is not None:
                deps.add(d)
            if isinstance(k, tuple) and k[0] == "ps":
                for d in self.rd.get(k, ()):
                    if self.ops[d].eng != eng:
                        deps.add(d)
        for k in w:
            d = self.lastw.get(k)
            if d is not None:
                deps.add(d)
            deps.update(self.rd.get(k, ()))
        deps.discard(op.idx)
        op.deps = deps
        for k in w:
            self.lastw[k] = op.idx
            self.rd[k] = []
        for k in r:
            self.rd.setdefault(k, []).append(op.idx)
        self.ops.append(op)
        return op

    def schedule(self):
        import heapq
        ops = self.ops
        order = {e: [] for e in ENGS}
        nep = self.epoch + 1
        byep = [[] for _ in range(nep)]
        for op in ops:
            byep[op.epoch].append(op.idx)
        done = [None] * len(ops)
        LAT = LAT_NS
        for ep in range(nep):
            idxs = byep[ep]
            if not idxs:
                continue
            if not SCHED:
                for i in idxs:
                    order[ops[i].eng].append(i)
                continue
            pend = {e: [i for i in idxs if ops[i].eng == e] for e in ENGS}
            head = {e: 0 for e in ENGS}
            taken = set()
            free = {e: 0.0 for e in ENGS}
            dmapipe = 0.0
            ev = [0.0]
            remaining = len(idxs)
            ldeps = {}
            for i in idxs:
                ldeps[i] = [d for d in ops[i].deps if ops[d].epoch == ep and not ops[d].fence]
            blev = {}
            if PRIO:
                for i in reversed(idxs):
                    b_ = blev.get(i, 0.0) + ops[i].cost + (2000.0 if ops[i].dma else LAT)
                    blev[i] = b_
                    for d in ldeps[i]:
                        if blev.get(d, 0.0) < b_:
                            blev[d] = b_
            while remaining:
                T = heapq.heappop(ev)
                while ev and ev[0] <= T:
                    heapq.heappop(ev)
                for e in ENGS:
                    if free[e] > T:
                        continue
                    pl = pend[e]
                    h = head[e]
                    while h < len(pl) and pl[h] in taken:
                        h += 1
                    head[e] = h
                    if h >= len(pl):
                        continue
                    pick = None
                    scanned = 0
                    j = h
                    while j < len(pl) and scanned < WINDOW:
                        i = pl[j]
                        j += 1
                        if i in taken:
                            continue
                        scanned += 1
                        ok = True
                        for d in ldeps[i]:
                            dt_ = done[d]
                            if dt_ is None or dt_ + LAT > T:
                                ok = False
                                break
                        if ok:
                            if not PRIO:
                                pick = i
                                break
                            if pick is None or blev[i] > blev[pick] + PRIO_EPS:
                                pick = i
                        if ops[i].fence:
                            break
                    if pick is None:
                        continue
                    op = ops[pick]
                    taken.add(pick)
                    remaining -= 1
                    order[e].append(pick)
                    if op.dma:
                        issue = 100.0 if e != "pool" else 600.0
                        free[e] = T + issue
                        st = max(T + issue, dmapipe)
                        dmapipe = st + op.cost
                        done[pick] = dmapipe + 2000.0
                    else:
                        free[e] = T + op.cost
                        done[pick] = T + op.cost
                    heapq.heappush(ev, free[e])
                    heapq.heappush(ev, done[pick] + LAT)
                if not ev and remaining:
                    raise RuntimeError("scheduler stuck")
        return order

    def finalize(self):
        ops = self.ops
        order = self.schedule()
        self.order = order
        for e in ENGS:
            for p, i in enumerate(order[e]):
                ops[i].pos = p
        dmak = {}
        for e in ENGS:
            for i in order[e]:
                op = ops[i]
                if op.dma:
                    k = dmak.get(op.eng, 0)
                    op.slot = (op.eng, k % NSLOT)
                    op.slotval = 16 * (k // NSLOT + 1)
                    dmak[op.eng] = k + 1
        lastfence = None
        fences = {}
        for op in ops:
            if op.fence:
                fences[op.epoch] = op.idx
        for op in ops:
            best = {}
            bestd = {}
            for d in op.deps:
                o = ops[d]
                if o.fence or o.epoch != op.epoch:
                    continue
                if o.dma:
                    if o.slot not in bestd or ops[bestd[o.slot]].slotval < o.slotval:
                        bestd[o.slot] = d
                else:
                    if o.eng == "pe" and op.eng == "pe" and not op.dma:
                        continue
                    if o.eng not in best or ops[best[o.eng]].pos < o.pos:
                        best[o.eng] = d
            op.deps = list(best.values()) + list(bestd.values())
            if op.epoch > 0:
                op.deps = [fences[op.epoch - 1]] + op.deps
            for d in op.deps:
                ops[d].signal = True
        cnt = {}
        nf = 0
        for op in ops:
            if op.fence:
                nf += 1
                op.sigval = nf
        for e in ENGS:
            for i in order[e]:
                op = ops[i]
                if not op.fence and not op.dma and op.signal:
                    key = (op.eng, op.epoch % NEP)
                    cnt[key] = cnt.get(key, 0) + 1
                    op.sigval = cnt[key]
        self.sigcounts = cnt
        self.dmacounts = dmak

    def emit(self, nc, es):
        self.finalize()
        ops = self.ops
        csem = {(e, j): es.enter_context(nc.semaphore("c_%s%d" % (e, j))) for e in ENGS for j in range(NEP)}
        fsem = es.enter_context(nc.semaphore("fence"))
        dsem = {}
        for q in ENGS:
            if self.dmacounts.get(q, 0) > 0:
                for s in range(NSLOT):
                    dsem[(q, s)] = es.enter_context(nc.semaphore("d_%s%d" % (q, s)))
        block = es.enter_context(nc.Block())
        order = self.order

        def run(e, eng):
            seen = {}
            for i in order[e]:
                op = ops[i]
                for d in op.deps:
                    o = ops[d]
                    if o.fence:
                        key, sem, val = "fence", fsem, o.sigval
                    elif o.dma:
                        key, sem, val = ("d",) + o.slot, dsem[o.slot], o.slotval
                    else:
                        key = ("c", o.eng, o.epoch % NEP)
                        sem, val = csem[key[1:]], o.sigval
                    if seen.get(key, 0) < val:
                        eng.wait_ge(sem, val)
                        seen[key] = val
                if op.dma:
                    prev = op.slotval - 16
                    if prev > 0 and seen.get(("d",) + op.slot, 0) < prev:
                        eng.wait_ge(dsem[op.slot], prev)
                        seen[("d",) + op.slot] = prev
                ins = op.fn(eng)
                if ins is None:
                    continue
                if op.fence:
                    ins.then_inc(fsem, 1)
                elif op.dma:
                    ins.then_inc(dsem[op.slot], 16)
                elif op.signal:
                    ins.then_inc(csem[(op.eng, op.epoch % NEP)], 1)

        @block.tensor
        def _(eng):
            run("pe", eng)

        @block.scalar
        def _(eng):
            run("act", eng)

        @block.vector
        def _(eng):
            run("dve", eng)

        @block.gpsimd
        def _(eng):
            run("pool", eng)

        @block.sync
        def _(eng):
            run("sp", eng)


_uid = [0]


class Arena:
    def __init__(self, nc, lo, hi):
        self.nc, self.lo, self.hi, self.top = nc, lo, hi, lo

    def alloc(self, shape, dt, name="t"):
        esz = 4 if dt == F32 else 2
        nb = esz
        for s in shape[1:]:
            nb *= s
        nb = (nb + 31) // 32 * 32
        off = self.top
        self.top += nb
        assert self.top <= self.hi, "SBUF arena overflow: %s %s need=%d over=%d" % (name, shape, nb, self.top - self.hi)
        _uid[0] += 1
        return self.nc.alloc_sbuf_tensor_at("%s_%d" % (name, _uid[0]), list(shape), dt, offset=off)

    def reset(self):
        self.top = self.lo


class Ring:
    def __init__(self, ar, n, shape, dt, name):
        self.t = [ar.alloc(shape, dt, name) for _ in range(n)]
        self.i = -1
        _uid[0] += 1
        self.name = "%s#%d" % (name, _uid[0])

    def next(self):
        self.i += 1
        j = self.i % len(self.t)
        return self.t[j], (self.name, j)


def build(NSEQ, S, L):
    NT = S // 128
    QB = min(512, S)
    NQB = S // QB
    TPQ = QB // 128
    C = 128
    nc = bass.Bass("TRN2", target_bir_lowering=False)

    def din(name, shape):
        return nc.dram_tensor(name, list(shape), F32, kind="ExternalInput").ap()

    x_d = din("x", [NSEQ, S, D])
    cT_d = din("cT", [128, KC, NSEQ])
    wada_d = din("w_ada", [L, 128, KC, 6 * D])
    bada_d = din("b_ada", [L, 6 * D])
    bcol_d = din("b_adacol", [128, L, 48])
    win_d = din("w_in", [L, 128, KC, INW])
    dec_d = din("dec", [L, 8])
    gn_d = din("gncol", [L, 128, 8])
    wro_d = din("w_ret_o", [L, 128, KC, D])
    qg_d = din("qgcol", [L, 128, 2])
    kvg_d = din("kvgcol", [L, 128, 1])
    wuq_d = din("w_uq", [L, 128, 2, 1536])
    wuk_d = din("w_uk", [L, 128, 1024])
    wuv_d = din("w_uv", [L, 128, 1024])
    wmo_d = din("w_mla_o", [L, 128, KC, D])
    wout_d = din("w_out", [L, 128, KC, D])
    lnr_d = din("lnrows", [L, 4, D])
    wup_d = din("w_up", [L, 128, KC, 2 * FF])
    convc_d = din("convcol", [L, 128, NFC, 4])
    wdn_d = din("w_down", [L, 128, NFC, D])
    ident_d = din("ident", [128, 128])
    cosr_d = din("cosr", [128, NT, 64])
    ssr_d = din("ssr", [128, NT, 128])
    cosm_d = din("cosm", [128, NT, 32])
    ssm_d = din("ssm", [128, NT, 64])
    cst4_d = din("cst4", [128, 4, 128])
    dexp_d = din("dexp", [128, 4])
    y_d = nc.dram_tensor("y", [NSEQ, S, D], F32, kind="ExternalOutput").ap()
    adag_d = nc.dram_tensor("adag_scr", [NSEQ, L, 2, D], F32, kind="Internal").ap()
    M_d = nc.dram_tensor("M_scr", [KC, 128, S], BF16, kind="Internal").ap()
    G_d = nc.dram_tensor("G_scr", [NFC, 128, S], BF16, kind="Internal").ap()

    P = Prog()
    ps = [nc.alloc_psum_tensor("ps%d" % i, [128, 512], F32) for i in range(8)]

    def pk(b):
        return ("ps", b)

    LO, HI = 16512, 229376
    XB = NT * 1024 * 4
    HB = max(8 * S * 2, 32768)
    off = LO
    arX = Arena(nc, off, off + XB); off += XB
    arH = Arena(nc, off, off + HB); off += HB
    arR = Arena(nc, off, off + HB); off += HB
    arC = Arena(nc, off, off + 26112); off += 26112
    arT = Arena(nc, off, HI)
    X = arX.alloc([128, NT, 1024], F32, "X")
    HT = arH.alloc([128, 8, S], BF16, "HT")
    R2 = arR.alloc([128, 8, S], BF16, "R2")
    ident = arC.alloc([128, 128], BF16, "ident")
    ones = arC.alloc([128, 128], F32, "ones")
    onesb = arC.alloc([128, 128], BF16, "onesb")
    cosr = arC.alloc([128, NT, 64], F32, "cosr")
    ssr = arC.alloc([128, NT, 128], F32, "ssr")
    cosm = arC.alloc([128, NT, 32], F32, "cosm")
    ssm = arC.alloc([128, NT, 64], F32, "ssm")
    cst4 = arC.alloc([128, 4, 128], F32, "cst4")
    dexp = arC.alloc([128, 4], F32, "dexp")
    cvals = arC.alloc([128, 4], F32, "cvals")
    adaT = arC.alloc([128, L, 4, 8, NSEQ], F32, "adaT")
    siluT = arC.alloc([128, KC, NSEQ], BF16, "siluT")
    cTs = arC.alloc([128, KC, NSEQ], F32, "cTs")
    dec = arC.alloc([128, 8], F32, "dec")
    lg = arC.alloc([128, 8], F32, "lg")
    gC = arC.alloc([128, 8], F32, "gC")
    dcol = arC.alloc([128, 4, 4], F32, "dcol")
    kplain = arC.alloc([128, 1], F32, "kplain")
    maskT = arC.alloc([128, 4, 128], F32, "maskT")
    gncol = arC.alloc([128, 8], F32, "gncol")
    qgcol = arC.alloc([128, 2], F32, "qgcol")
    kvgcol = arC.alloc([128, 1], F32, "kvgcol")
    stat = [arC.alloc([128, 24], F32, "stat%d" % i) for i in range(4)]
    stat_i = [-1]

    def nstat():
        stat_i[0] += 1
        j = stat_i[0] % 4
        return stat[j], ("stat", j)

    def mm(out, lhsT, rhs, st, sp, r, w):
        P.add("pe", lambda e: e.matmul(out, lhsT=lhsT, rhs=rhs, start=st, stop=sp), r=r, w=w, c=0.5 * rhs.free_size() + 30.0)

    def tp(out, in_, n, r, w):
        P.add("pe", lambda e: e.transpose(out, in_, ident[0:n, 0:n]), r=tuple(r) + ("ident",), w=w, c=100.0)

    def act(out, in_, func, r, w, bias=None, scale=None, accum=None):
        kw = {}
        if bias is not None:
            kw["bias"] = bias
        if scale is not None:
            kw["scale"] = scale
        if accum is not None:
            kw["accum_out"] = accum
        P.add("act", lambda e: e.activation(out=out, in_=in_, func=func, **kw), r=r, w=w, c=out.free_size() / 1.2 + 250.0)

    def tt(out, in0, in1, op, r, w, eng="dve"):
        P.add(eng, lambda e: e.tensor_tensor(out=out, in0=in0, in1=in1, op=op), r=r, w=w,
              c=out.free_size() / (0.9 if eng == "dve" else 0.45) + 150.0)

    def ts(out, in0, s1, s2, op0, op1, r, w):
        if s2 is None:
            P.add("dve", lambda e: e.tensor_scalar(out=out, in0=in0, scalar1=s1, scalar2=None, op0=op0), r=r, w=w, c=out.free_size() / 0.9 + 150.0)
        else:
            P.add("dve", lambda e: e.tensor_scalar(out=out, in0=in0, scalar1=s1, scalar2=s2, op0=op0, op1=op1), r=r, w=w, c=out.free_size() / 0.9 + 150.0)

    def stt(out, in0, sc, in1, op0, op1, r, w, eng="dve"):
        P.add(eng, lambda e: e.scalar_tensor_tensor(out=out, in0=in0, scalar=sc, in1=in1, op0=op0, op1=op1), r=r, w=w,
              c=out.free_size() / (0.9 if eng == "dve" else 0.45) + 150.0)

    def cp(out, in_, r, w, eng="dve"):
        if eng == "act":
            act(out, in_, AF.Copy, r, w)
            return
        k = 0.9 if eng == "dve" else 0.45
        P.add(eng, lambda e: e.tensor_copy(out=out, in_=in_), r=r, w=w, c=out.free_size() / k + 150.0)

    def dma(q, out, in_, r, w):
        P.add(q, lambda e: e.dma_start(out=out, in_=in_), r=r, w=w, dma=True, c=out.free_nbytes() * out.partition_size() / 120.0 + 300.0)

    def fence():
        P.add("sp", lambda e: e.nop(), fence=True)

    def rsq(out, in_, n, r, w):
        P.add("pool", lambda e: e.tensor_tensor(out=out, in0=in_, in1=cvals[:, 2:3], op=ALU.pow),
              r=tuple(r) + ("cvals",), w=w, c=600.0)

    def ln_stats(src_ap, srckeys, eps_col):
        stt_, sk = nstat()
        P.add("dve", lambda e: e.bn_stats(out=stt_[:, 0:6], in_=src_ap[:, 0:512]), r=srckeys, w=[sk])
        P.add("dve", lambda e: e.bn_stats(out=stt_[:, 6:12], in_=src_ap[:, 512:1024]), r=list(srckeys) + [sk], w=[sk])
        P.add("dve", lambda e: e.bn_aggr(out=stt_[:, 12:14], in_=stt_[:, 0:12].rearrange("p (a b) -> p a b", a=2)), r=[sk], w=[sk])
        ts(stt_[:, 14:15], stt_[:, 13:14], LN_EPS, None, ALU.add, None, [sk], [sk])
        rsq(stt_[:, 16:17], stt_[:, 14:15], 1, [sk], [sk])
        stt(stt_[:, 17:18], stt_[:, 12:13], -1.0, stt_[:, 16:17], ALU.mult, ALU.mult, [sk], [sk])
        return stt_, sk

    dma("pool", ident[:], ident_d, [], ["ident"])
    dma("sp", cosr[:], cosr_d, [], ["rope"])
    dma("sp", ssr[:], ssr_d, [], ["rope"])
    dma("sp", cosm[:], cosm_d, [], ["rope"])
    dma("sp", ssm[:], ssm_d, [], ["rope"])
    dma("sp", cst4[:], cst4_d, [], ["cst4"])
    dma("sp", dexp[:], dexp_d, [], ["dexp"])
    dma("sp", cTs[:], cT_d, [], ["cTs"])
    P.add("dve", lambda e: e.memset(ones[:], 1.0), w=["ones"])
    P.add("dve", lambda e: e.memset(onesb[:], 1.0), w=["onesb"])
    P.add("dve", lambda e: e.memset(cvals[:, 0:1], LN_EPS), w=["cvals"])
    P.add("dve", lambda e: e.memset(cvals[:, 1:2], RMS_EPS), r=["cvals"], w=["cvals"])
    P.add("dve", lambda e: e.memset(cvals[:, 2:3], -0.5), r=["cvals"], w=["cvals"])
    P.add("dve", lambda e: e.memset(cvals[:, 3:4], 1.0), r=["cvals"], w=["cvals"])
    P.add("dve", lambda e: e.memset(kplain[:], 128.0 ** -0.5), w=["kplain"])
    act(siluT[:], cTs[:], AF.Silu, ["cTs"], ["siluT"])

    bcol = arT.alloc([128, L, 48], F32, "bcol")
    browR = Ring(arT, 2, [1, 512], F32, "brow")
    dma("sp", bcol[:], bcol_d, [], ["bcol"])
    waR = Ring(arT, 2, [128, KC, 512], BF16, "wa")
    growR = Ring(arT, 2, [1, 512], F32, "grow")
    pr = [0]
    for l in range(L):
        for nb in range(12):
            wa, wak = waR.next()
            dma("pool", wa[:], wada_d[l, :, :, nb * 512:(nb + 1) * 512], [], [wak])
            blk = nb // 2
            if blk in (2, 5):
                brow, browk = browR.next()
                dma("sp", brow[:], bada_d[l:l + 1, nb * 512:(nb + 1) * 512], [], [browk])
                for s in range(NSEQ):
                    b = pr[0] % 4
                    pr[0] += 1
                    for kc in range(KC):
                        mm(ps[b][0:1, :], siluT[:, kc, s:s + 1], wa[:, kc, :], kc == 0, kc == KC - 1, [wak, "siluT"], [pk(b)])
                    gr, grk = growR.next()
                    n0 = nb * 512
                    tt(gr[:], ps[b][0:1, :], brow[:], ALU.add, [pk(b), browk], [grk])
                    ts(gr[:], gr[:], 1.0, None, ALU.add, None, [grk], [grk])
                    wi = 0 if blk == 2 else 1
                    hf = nb % 2
                    dma("sp", adag_d[s, l, wi:wi + 1, hf * 512:(hf + 1) * 512], gr[:], [grk], [("adag", s, l, wi)])
            else:
                wi = {0: 0, 1: 1, 3: 2, 4: 3}[blk]
                b = 4 + pr[0] % 4
                pr[0] += 1
                for j in range(4):
                    for kc in range(KC):
                        mm(ps[b][:, j * 8:j * 8 + NSEQ], wa[:, kc, j * 128:(j + 1) * 128], siluT[:, kc, :], kc == 0, kc == KC - 1,
                           [wak, "siluT"], [pk(b)])
                for j in range(4):
                    fb = (nb % 2) * 4 + j
                    cb = nb * 4 + j
                    act(adaT[:, l, wi, fb, :], ps[b][:, j * 8:j * 8 + NSEQ], AF.Identity, [pk(b), "bcol"], ["adaT"],
                        bias=bcol[:, l, cb:cb + 1], scale=1.0)
    for l in range(L):
        for wi in (1, 3):
            ts(adaT[:, l, wi], adaT[:, l, wi], 1.0, None, ALU.add, None, ["adaT"], ["adaT"])
    fence()
    arT.reset()

    def h_tile(s, l, which, t, xnR, banks):
        shw, scw = (0, 1) if which == 0 else (2, 3)
        st_, sk = ln_stats(X[:, t, :], [("X", t)], 0)
        xn, xk = xnR.next()
        act(xn[:], X[:, t, :], AF.Identity, [("X", t), sk], [xk], bias=st_[:, 17:18], scale=st_[:, 16:17])
        b = banks[t % len(banks)]
        pb = ps[b][:].bitcast(BF16)
        for kc in range(KC):
            tp(pb[:, kc * 128:(kc + 1) * 128], xn[:, kc * 128:(kc + 1) * 128], 128, [xk], [pk(b)])
        for kc in range(KC):
            act(HT[:, kc, t * 128:(t + 1) * 128], pb[:, kc * 128:(kc + 1) * 128], AF.Identity, [pk(b), "adaT"], [("HT", t)],
                bias=adaT[:, l, shw, kc, s:s + 1], scale=adaT[:, l, scw, kc, s:s + 1])

    def phase_H(s, l, which):
        xnR = Ring(arT, 3, [128, 1024], BF16, "xn")
        for t in range(NT):
            h_tile(s, l, which, t, xnR, (0, 1, 2, 3))
        fence()
        arT.reset()

    def layer_consts(l):
        dma("sp", dec[:], dec_d[l:l + 1, :].partition_broadcast(128), [], ["dec"])
        dma("sp", gncol[:], gn_d[l], [], ["gncol"])
        dma("sp", qgcol[:], qg_d[l], [], ["qgcol"])
        dma("sp", kvgcol[:], kvg_d[l], [], ["kvgcol"])
        act(lg[:], dec[:], AF.Sigmoid, ["dec"], ["lg"])
        act(lg[:], lg[:], AF.Ln, ["lg"], ["lg"])
        act(gC[:], lg[:], AF.Exp, ["lg"], ["gC"], scale=float(C))
        e1 = arT.alloc([128, 128], F32, "e1")
        e2 = arT.alloc([128, 128], F32, "e2")
        for h in range(4):
            act(e1[:], cst4[:, 0, :], AF.Exp, ["cst4", "lg"], ["e1"], scale=lg[:, h:h + 1])
            tt(e1[:], e1[:], cst4[:, 2, :], ALU.mult, ["e1", "cst4"], ["e1"])
            act(e2[:], cst4[:, 1, :], AF.Exp, ["cst4", "lg"], ["e2"], scale=lg[:, 4 + h:5 + h])
            tt(e2[:], e2[:], cst4[:, 3, :], ALU.mult, ["e2", "cst4"], ["e2"])
            tt(maskT[:, h, :], e1[:], e2[:], ALU.add, ["e1", "e2"], ["maskT"])
            act(dcol[:, h, 0:1], dexp[:, 0:1], AF.Exp, ["dexp", "lg"], ["dcol"], scale=lg[:, h:h + 1])
            act(dcol[:, h, 1:2], dexp[:, 1:2], AF.Exp, ["dexp", "lg"], ["dcol"], scale=lg[:, 4 + h:5 + h])
            act(dcol[:, h, 2:3], dexp[:, 2:3], AF.Exp, ["dexp", "lg"], ["dcol"], scale=lg[:, h:h + 1])
            act(dcol[:, h, 3:4], dexp[:, 3:4], AF.Exp, ["dexp", "lg"], ["dcol"], scale=lg[:, 4 + h:5 + h])
            ts(dcol[:, h, 2:4], dcol[:, h, 2:4], 128.0 ** -0.5, None, ALU.mult, None, ["dcol"], ["dcol"])
        fence()
        arT.reset()

    def rope(dst, src, cos_b, ss, half, nh, tmpA, tmpB, r, w, kA, kB, addeng="dve"):
        tt(tmpA, src, cos_b, ALU.mult, list(r) + ["rope"], [kA])
        tt(tmpB[:, :, 0, :], src[:, :, 1, :], ss[:, :, 0, :], ALU.mult, list(r) + ["rope"], [kB])
        tt(tmpB[:, :, 1, :], src[:, :, 0, :], ss[:, :, 1, :], ALU.mult, list(r) + ["rope", kB], [kB])
        tt(dst, tmpA, tmpB, ALU.add, [kA, kB], w, eng=addeng)

    def phase_mla(s, l):
        arT.reset()
        cqT = arT.alloc([128, 2, S], BF16, "cqT")
        ckvT = arT.alloc([128, S], BF16, "ckvT")
        krT = arT.alloc([128, 2, S], BF16, "krT")
        qrT = arT.alloc([128, 4, S], BF16, "qrT")
        P.add("dve", lambda e: e.memset(krT[:], 0.0), w=["krT0"], c=2 * S / 0.9 + 150.0)
        base_top = arT.top
        wl = arT.alloc([128, KC, 448], BF16, "wl")
        wuqr = arT.alloc([128, 2, 8, 64], BF16, "wuqr")
        dma("pool", wl[:], win_d[l, :, :, 3072:3520], [], ["wl"])
        dma("pool", wuqr[:], wuq_d[l].rearrange("p k (h d) -> p k h d", h=8)[:, :, :, 128:192], [], ["wuqr"])
        latR = Ring(arT, 2, [128, 512], BF16, "lat")
        tAR = Ring(arT, 1, [128, 512], F32, "tA")
        tBR = Ring(arT, 1, [128, 512], F32, "tB")
        tAsR = Ring(arT, 1, [128, 64], F32, "tAs")
        tBsR = Ring(arT, 1, [128, 64], F32, "tBs")
        sqj = arT.alloc([128, 256], BF16, "sqj")
        qrtR = Ring(arT, 2, [128, 512], BF16, "qrt")
        for t in range(NT):
            b = 0 + (t % 2)
            for kc in range(KC):
                mm(ps[b][:, 0:448], HT[:, kc, t * 128:(t + 1) * 128], wl[:, kc, :], kc == 0, kc == KC - 1, [("HT", t), "wl"], [pk(b)])
            st_, sk = nstat()
            act(sqj[:, 0:256], ps[b][:, 0:256], AF.Square, [pk(b)], ["sqj", sk], accum=st_[:, 0:1])
            act(sqj[:, 0:128], ps[b][:, 256:384], AF.Square, [pk(b), sk], ["sqj", sk], accum=st_[:, 1:2])
            ts(st_[:, 2:3], st_[:, 0:1], 1.0 / 256.0, RMS_EPS, ALU.mult, ALU.add, [sk], [sk])
            ts(st_[:, 3:4], st_[:, 1:2], 1.0 / 128.0, RMS_EPS, ALU.mult, ALU.add, [sk], [sk])
            rsq(st_[:, 4:5], st_[:, 2:3], 1, [sk], [sk])
            rsq(st_[:, 5:6], st_[:, 3:4], 1, [sk], [sk])
            lat, lk = latR.next()
            act(lat[:, 0:256], ps[b][:, 0:256], AF.Identity, [pk(b), sk], [lk], scale=st_[:, 4:5], bias=0.0)
            act(lat[:, 256:384], ps[b][:, 256:384], AF.Identity, [pk(b), sk, lk], [lk], scale=st_[:, 5:6], bias=0.0)
            tA, tAk = tAsR.next()
            tB, tBk = tBsR.next()
            v4 = lambda ap: ap.rearrange("p (h a d) -> p h a d", h=1, a=2)
            cosb = cosm[:, t, :].unsqueeze(1).unsqueeze(1).broadcast_to([128, 1, 2, 32])
            ssb = ssm[:, t, :].rearrange("p (h a d) -> p h a d", h=1, a=2)
            rope(v4(lat[:, 384:448]), v4(ps[b][:, 384:448]), cosb, ssb, 32, 1, v4(tA[:, 0:64]), v4(tB[:, 0:64]),
                 [pk(b), lk], [lk], tAk, tBk)
            cp(lat[:, 448:512], lat[:, 384:448], [lk], [lk])
            b2 = 2 + (t % 2)
            pb = ps[b2][:].bitcast(BF16)
            for j in range(4):
                tp(pb[:, j * 128:(j + 1) * 128], lat[:, j * 128:(j + 1) * 128], 128, [lk], [pk(b2)])
            tsl = slice(t * 128, (t + 1) * 128)
            act(cqT[:, 0, tsl], pb[:, 0:128], AF.Identity, [pk(b2), "qgcol"], [("cqT", t)], scale=qgcol[:, 0:1], bias=0.0)
            act(cqT[:, 1, tsl], pb[:, 128:256], AF.Identity, [pk(b2), "qgcol", ("cqT", t)], [("cqT", t)], scale=qgcol[:, 1:2], bias=0.0)
            act(ckvT[:, tsl], pb[:, 256:384], AF.Identity, [pk(b2), "kvgcol"], [("ckvT", t)], scale=kvgcol[:, 0:1], bias=0.0)
            cp(krT[0:64, 0, tsl], pb[0:64, 384:512], [pk(b2), "krT0"], [("krT", t)])
            cp(krT[64:128, 1, tsl], pb[64:128, 384:512], [pk(b2), "krT0", ("krT", t)], [("krT", t)])
            b3 = 4 + (t % 2)
            for kc in range(2):
                mm(ps[b3][:, 0:512], cqT[:, kc, tsl], wuqr[:, kc].rearrange("p h d -> p (h d)"), kc == 0, kc == 1,
                   [("cqT", t), "wuqr"], [pk(b3)])
            tA, tAk = tAR.next()
            tB, tBk = tBR.next()
            qrt, qk = qrtR.next()
            v8 = lambda ap: ap.rearrange("p (h a d) -> p h a d", h=8, a=2)
            cosb8 = cosm[:, t, :].unsqueeze(1).unsqueeze(1).broadcast_to([128, 8, 2, 32])
            ssb8 = ssm[:, t, :].rearrange("p (a d) -> p a d", a=2).unsqueeze(1).broadcast_to([128, 8, 2, 32])
            rope(v8(qrt[:]), v8(ps[b3][:, 0:512]), cosb8, ssb8, 32, 8, v8(tA[:]), v8(tB[:]), [pk(b3)], [qk], tAk, tBk)
            b4 = 6 + (t % 2)
            pb4 = ps[b4][:].bitcast(BF16)
            for j in range(4):
                tp(pb4[:, j * 128:(j + 1) * 128], qrt[:, j * 128:(j + 1) * 128], 128, [qk], [pk(b4)])
            cp(qrT[:, :, tsl], pb4[:, 0:512].rearrange("p (j t) -> p j t", j=4), [pk(b4)], [("qrT", t)])
        fence()
        arT.top = base_top
        wkvR = Ring(arT, 1, [128, 2, 128], BF16, "wkv")
        wqnR = Ring(arT, 1, [128, 2, 128], BF16, "wqn")
        knT = arT.alloc([128, S], BF16, "knT")
        vh = arT.alloc([128, NT, 128], BF16, "vh")
        qnR = Ring(arT, 2, [128, QB], BF16, "qn")
        ptR = Ring(arT, 5, [128, QB], BF16, "pt")
        accR = Ring(arT, 1, [128, QB], F32, "acc")
        scale = 192.0 ** -0.5
        sc_i = [0]
        allT = [("ckvT", t) for t in range(NT)]
        for h in range(8):
            wqn, wqk = wqnR.next()
            wkv, wkvk = wkvR.next()
            dma("pool", wqn[:], wuq_d[l, :, :, h * 192:h * 192 + 128], [], [wqk])
            dma("pool", wkv[:, 0, :], wuk_d[l, :, h * 128:(h + 1) * 128], [], [wkvk])
            dma("pool", wkv[:, 1, :], wuv_d[l, :, h * 128:(h + 1) * 128], [], [wkvk])
            for qb in range(NQB):
                qs = slice(qb * QB, (qb + 1) * QB)
                mm(ps[7][:, 0:QB], wkv[:, 0, :], ckvT[:, qs], True, True, allT + [wkvk], [pk(7)])
                cp(knT[:, qs], ps[7][:, 0:QB], [pk(7)], ["knT"])
            for t4 in range(0, NT, 4):
                n4 = min(4, NT - t4)
                for j in range(n4):
                    t = t4 + j
                    mm(ps[7][:, j * 128:(j + 1) * 128], ckvT[:, t * 128:(t + 1) * 128], wkv[:, 1, :], True, True,
                       allT + [wkvk], [pk(7)])
                act(vh[:, t4:t4 + n4, :], ps[7][:, 0:n4 * 128].rearrange("p (j d) -> p j d", j=n4), AF.Copy, [pk(7)], ["vh"])
            pb_ = (h % 2) * 64
            for qb in range(NQB):
                qs = slice(qb * QB, (qb + 1) * QB)
                qn, qnk = qnR.next()
                for kc in range(2):
                    mm(ps[7][:, 0:QB], wqn[:, kc, :], cqT[:, kc, qs], kc == 0, kc == 1, [("cqT", t) for t in range(NT)] + [wqk], [pk(7)])
                cp(qn[:], ps[7][:, 0:QB], [pk(7)], [qnk])
                bo = 3 + (qb % 2)
                bd = 5 + (qb % 2)
                acc, acck = accR.next()
                odd = [k_ for k_ in range(NT) if k_ % 2 == 1]
                pt1 = None
                for kt in range(NT):
                    bs = sc_i[0] % 3
                    sc_i[0] += 1
                    ks = slice(kt * 128, (kt + 1) * 128)
                    mm(ps[bs][:, 0:QB], knT[:, ks], qn[:], True, False, ["knT", qnk], [pk(bs)])
                    mm(ps[bs][:, 0:QB], krT[:, h % 2, ks], qrT[:, h // 2, qs], False, True,
                       [("krT", kt)] + [("qrT", t) for t in range(NT)], [pk(bs)])
                    pt, ptk = ptR.next()
                    act(pt[:], ps[bs][:, 0:QB], AF.Exp, [pk(bs)], [ptk], scale=scale)
                    mm(ps[bo][:, 0:QB], vh[:, kt, :], pt[:], kt == 0, kt == NT - 1, ["vh", ptk], [pk(bo)])
                    if kt % 2 == 0:
                        mm(ps[bd][:, 0:QB], onesb[:], pt[:], kt == 0, False, ["onesb", ptk], [pk(bd)])
                    elif len(odd) == 1:
                        cp(acc[:], pt[:], [ptk], [acck])
                    elif kt == 1:
                        pt1, pt1k = pt, ptk
                    elif kt == 3:
                        tt(acc[:], pt1[:], pt[:], ALU.add, [pt1k, ptk], [acck])
                    else:
                        tt(acc[:], acc[:], pt[:], ALU.add, [ptk, acck], [acck])
                mm(ps[bd][:, 0:QB], ones[:], acc[:], False, True, ["ones", acck], [pk(bd)])
                P.add("dve", lambda e, acc=acc, bd=bd: e.reciprocal(out=acc[:], in_=ps[bd][:, 0:QB]), r=[pk(bd)], w=[acck], c=700.0)
                tt(R2[:, h, qs], ps[bo][:, 0:QB], acc[:], ALU.mult, [pk(bo), acck], [("R2", h)])
        fence()
        arT.reset()

    keep = {}

    def phase_gate(l, wsrc_d, gcol0, first):
        arT.reset()
        if not first:
            keep["wout"] = arT.alloc([128, KC, D], BF16, "wout")
        keep_top = arT.top
        woR = Ring(arT, 2, [128, KC, 128], BF16, "wo")
        wgR = Ring(arT, 2, [128, KC, 128], BF16, "wg")
        sgR = Ring(arT, 2, [128, QB], F32, "sg")
        mbR = Ring(arT, 2, [128, QB], BF16, "mb")
        mpR = Ring(arT, 2, [128, QB], BF16, "mp")
        i = 0
        allH = [("HT", t) for t in range(NT)]
        allR = [("R2", h) for h in range(8)]
        for c in range(8):
            wo, wok = woR.next()
            wg, wgk = wgR.next()
            dma("pool", wo[:], wsrc_d[l, :, :, c * 128:(c + 1) * 128], [], [wok])
            dma("pool", wg[:], win_d[l, :, :, gcol0 + c * 128:gcol0 + (c + 1) * 128], [], [wgk])
            if c == 1 and not first:
                dma("pool", keep["wout"][:], wout_d[l], [], ["wout"])
            for qb in range(NQB):
                qs = slice(qb * QB, (qb + 1) * QB)
                by = (i % 2)
                bg = 2 + (i % 2)
                i += 1
                for kc in range(KC):
                    mm(ps[by][:, 0:QB], wo[:, kc, :], R2[:, kc, qs], kc == 0, kc == KC - 1, allR + [wok], [pk(by)])
                for kc in range(KC):
                    mm(ps[bg][:, 0:QB], wg[:, kc, :], HT[:, kc, qs], kc == 0, kc == KC - 1, allH + [wgk], [pk(bg)])
                sg, sgk = sgR.next()
                act(sg[:], ps[bg][:, 0:QB], AF.Sigmoid, [pk(bg)], [sgk])
                mb, mbk = mbR.next()
                if first:
                    tt(mb[:], ps[by][:, 0:QB], sg[:], ALU.mult, [pk(by), sgk], [mbk])
                else:
                    mp, mpk = mpR.next()
                    dma("sp", mp[:], M_d[c, :, qs], [("M", c, qb)], [mpk])
                    tt(sg[:], ps[by][:, 0:QB], sg[:], ALU.mult, [pk(by), sgk], [sgk])
                    tt(mb[:], sg[:], mp[:], ALU.add, [sgk, mpk], [mbk])
                dma("sp", M_d[c, :, qs], mb[:], [mbk], [("M", c, qb)])
        fence()
        arT.top = keep_top

    def phase_ret(s, l):
        arT.reset()
        wr = arT.alloc([128, KC, 768], BF16, "wr")
        kT = arT.alloc([128, S], BF16, "kT")
        kf = arT.alloc([128, NT, 128], BF16, "kf")
        vtm = arT.alloc([128, NT, 256], BF16, "vtm")
        Sb = arT.alloc([128, NT, 256], BF16, "Sb")
        Rf = arT.alloc([128, 256], F32, "Rf")
        Rb = arT.alloc([128, 256], F32, "Rb")
        SfR = Ring(arT, 2, [128, 256], BF16, "Sf")
        tAR = Ring(arT, 2, [128, 128], F32, "rA")
        tBR = Ring(arT, 2, [128, 128], F32, "rB")
        krR = Ring(arT, 2, [128, 128], F32, "kr")
        k3R = Ring(arT, 2, [128, 3, 128], BF16, "k3")
        q3TR = Ring(arT, 2, [128, 3, 128], BF16, "q3T")
        sgR = Ring(arT, 2, [128, 256], BF16, "sgr")
        ptR = Ring(arT, 2, [128, 128], BF16, "ptr")
        ronR = Ring(arT, 2, [128, 256], F32, "ron")
        abR = Ring(arT, 2, [128, 256], BF16, "ab")
        v3 = lambda ap: ap.rearrange("p (h a d) -> p h a d", h=1, a=2)
        for h in range(4):
            dma("pool", wr[:, :, 384:512], win_d[l, :, :, 512 + h * 128:512 + (h + 1) * 128], [], ["wrkv"])
            dma("pool", wr[:, :, 512:768], win_d[l, :, :, 1024 + h * 256:1024 + (h + 1) * 256], [], ["wrkv"])
            dma("pool", wr[:, :, 0:128], win_d[l, :, :, h * 128:(h + 1) * 128], [], ["wrqg"])
            dma("pool", wr[:, :, 128:384], win_d[l, :, :, 2048 + h * 256:2048 + (h + 1) * 256], [], ["wrqg"])
            P.add("dve", lambda e: e.memset(Rf[:], 0.0), w=["Rf"])
            P.add("dve", lambda e: e.memset(Rb[:], 0.0), w=["Rb"])
            for n in range(NT - 1, -1, -1):
                ns = slice(n * 128, (n + 1) * 128)
                b = n % 2
                for kc in range(KC):
                    mm(ps[b][:, 0:384], HT[:, kc, ns], wr[:, kc, 384:768], kc == 0, kc == KC - 1, [("HT", n), "wrkv"], [pk(b)])
                tA, tAk = tAR.next()
                tB, tBk = tBR.next()
                kr, krk = krR.next()
                cosb = cosr[:, n, :].unsqueeze(1).unsqueeze(1).broadcast_to([128, 1, 2, 64])
                ssb = ssr[:, n, :].rearrange("p (h a d) -> p h a d", h=1, a=2)
                rope(v3(kr[:]), v3(ps[b][:, 0:128]), cosb, ssb, 64, 1, v3(tA[:]), v3(tB[:]), [pk(b)], [krk], tAk, tBk)
                k3, k3k = k3R.next()
                act(k3[:, 0, :], kr[:], AF.Identity, [krk, "kplain"], [k3k], scale=kplain[:, 0:1], bias=0.0)
                act(kf[:, n, :], kr[:], AF.Identity, [krk, "dcol"], [("kf", n)], scale=dcol[:, h, 2:3], bias=0.0)
                act(k3[:, 2, :], kr[:], AF.Identity, [krk, "dcol", k3k], [k3k], scale=dcol[:, h, 3:4], bias=0.0)
                cp(vtm[:, n, :], ps[b][:, 128:384], [pk(b)], [("vtm", n)], eng="act")
                b2 = 2 + (n % 2)
                pb = ps[b2][:].bitcast(BF16)
                tp(pb[:, 0:128], k3[:, 0, :], 128, [k3k], [pk(b2)])
                cp(kT[:, ns], pb[:, 0:128], [pk(b2)], [("kT", n)], eng="act")
                cp(Sb[:, n, :], Rb[:], ["Rb"], [("Sb", n)])
                b3 = 4 + (n % 2)
                mm(ps[b3][:, 0:256], k3[:, 2, :], vtm[:, n, :], True, True, [k3k, ("vtm", n)], [pk(b3)])
                stt(Rb[:], Rb[:], gC[:, 4 + h:5 + h], ps[b3][:, 0:256], ALU.mult, ALU.add, ["Rb", "gC", pk(b3)], ["Rb"])
            for n in range(NT):
                ns = slice(n * 128, (n + 1) * 128)
                b = n % 2
                for kc in range(KC):
                    mm(ps[b][:, 0:384], HT[:, kc, ns], wr[:, kc, 0:384], kc == 0, kc == KC - 1, [("HT", n), "wrqg"], [pk(b)])
                tA, tAk = tAR.next()
                tB, tBk = tBR.next()
                kr, krk = krR.next()
                cosb = cosr[:, n, :].unsqueeze(1).unsqueeze(1).broadcast_to([128, 1, 2, 64])
                ssb = ssr[:, n, :].rearrange("p (h a d) -> p h a d", h=1, a=2)
                rope(v3(kr[:]), v3(ps[b][:, 0:128]), cosb, ssb, 64, 1, v3(tA[:]), v3(tB[:]), [pk(b)], [krk], tAk, tBk)
                q3, q3k = k3R.next()
                act(q3[:, 0, :], kr[:], AF.Copy, [krk], [q3k])
                act(q3[:, 1, :], kr[:], AF.Identity, [krk, "dcol", q3k], [q3k], scale=dcol[:, h, 0:1], bias=0.0)
                act(q3[:, 2, :], kr[:], AF.Identity, [krk, "dcol", q3k], [q3k], scale=dcol[:, h, 1:2], bias=0.0)
                sg, sgk = sgR.next()
                act(sg[:], ps[b][:, 128:384], AF.Silu, [pk(b)], [sgk])
                b2 = 2 + (n % 2)
                pb = ps[b2][:].bitcast(BF16)
                for j in range(3):
                    tp(pb[:, j * 128:(j + 1) * 128], q3[:, j, :], 128, [q3k], [pk(b2)])
                q3T, q3Tk = q3TR.next()
                cp(q3T[:].rearrange("p j t -> p (j t)"), pb[:, 0:384], [pk(b2)], [q3Tk])
                b3 = 4 + (n % 2)
                mm(ps[b3][:, 0:128], kT[:, ns], q3T[:, 0, :], True, True, [("kT", n), q3Tk], [pk(b3)])
                pt, ptk = ptR.next()
                tt(pt[:], ps[b3][:, 0:128], maskT[:, h, :], ALU.mult, [pk(b3), "maskT"], [ptk])
                Sf, Sfk = SfR.next()
                cp(Sf[:], Rf[:], ["Rf"], [Sfk])
                b4 = 6 + (n % 2)
                mm(ps[b4][:, 0:256], pt[:], vtm[:, n, :], True, False, [ptk, ("vtm", n)], [pk(b4)])
                mm(ps[b4][:, 0:256], q3T[:, 1, :], Sf[:], False, False, [q3Tk, Sfk], [pk(b4)])
                mm(ps[b4][:, 0:256], q3T[:, 2, :], Sb[:, n, :], False, True, [q3Tk, ("Sb", n)], [pk(b4)])
                mm(ps[b3][:, 256:512], kf[:, n, :], vtm[:, n, :], True, True, [("kf", n), ("vtm", n), ptk], [pk(b3)])
                stt(Rf[:], Rf[:], gC[:, h:h + 1], ps[b3][:, 256:512], ALU.mult, ALU.add, ["Rf", "gC", pk(b3), Sfk], ["Rf"])
                st_, sk = nstat()
                P.add("dve", lambda e, st_=st_, b4=b4: e.bn_stats(out=st_[:, 0:6], in_=ps[b4][:, 0:256]), r=[pk(b4)], w=[sk])
                P.add("dve", lambda e, st_=st_: e.bn_aggr(out=st_[:, 12:14], in_=st_[:, 0:6].rearrange("p (a b) -> p a b", a=1)), r=[sk], w=[sk])
                ts(st_[:, 14:15], st_[:, 13:14], LN_EPS, None, ALU.add, None, [sk], [sk])
                rsq(st_[:, 16:17], st_[:, 14:15], 1, [sk], [sk])
                stt(st_[:, 17:18], st_[:, 12:13], -1.0, st_[:, 16:17], ALU.mult, ALU.mult, [sk], [sk])
                ron, ronk = ronR.next()
                act(ron[:], ps[b4][:, 0:256], AF.Identity, [pk(b4), sk], [ronk], bias=st_[:, 17:18], scale=st_[:, 16:17])
                ab, abk = abR.next()
                tt(ab[:], ron[:], sg[:], ALU.mult, [ronk, sgk], [abk])
                for j in range(2):
                    tp(pb[:, 512 + j * 128:512 + (j + 1) * 128], ab[:, j * 128:(j + 1) * 128], 128, [abk, q3Tk], [pk(b2)])
                for j in range(2):
                    act(R2[:, 2 * h + j, ns], pb[:, 512 + j * 128:512 + (j + 1) * 128], AF.Identity, [pk(b2), "gncol"], [("R2", 2 * h + j)],
                        scale=gncol[:, 2 * h + j:2 * h + j + 1], bias=0.0)
        fence()
        arT.reset()

    def ln_epilogue(s, l, t, bps, gb, lng, lnb, zR, last, nexth=None):
        z, zk = zR.next()
        for hf in range(2):
            tt(z[:, hf * 512:(hf + 1) * 512], ps[bps[hf]][:, :], gb[:, hf * 512:(hf + 1) * 512], ALU.mult,
               [pk(bps[hf]), "gb"] + ([zk] if hf else []), [zk])
        stt(z[:], X[:, t, :], ALPHA, z[:], ALU.mult, ALU.add, [("X", t), zk], [zk])
        st_, sk = ln_stats(z, [zk], 0)
        act(z[:], z[:], AF.Identity, [zk, sk], [zk], bias=st_[:, 17:18], scale=st_[:, 16:17])
        tt(z[:], z[:], lng[:], ALU.mult, [zk, "lnrow"], [zk])
        tt(X[:, t, :], z[:], lnb[:], ALU.add, [zk, "lnrow"], [("X", t)])
        if last:
            dma("sp", y_d[s, t * 128:(t + 1) * 128, :], X[:, t, :], [("X", t)], [("y", s, t)])
        if nexth is not None:
            nexth(t)

    def phase_out(s, l):
        arR.reset()
        wout = keep["wout"]
        mTR = Ring(arT, 2, [128, KC, 128], BF16, "mT")
        xnR = Ring(arT, 2, [128, 1024], BF16, "xn")
        gb = arR.alloc([128, D], F32, "gb")
        lng = arR.alloc([128, D], F32, "lng")
        lnb = arR.alloc([128, D], F32, "lnb")
        zR = Ring(arR, 3, [128, D], F32, "z")
        dma("sp", gb[:], adag_d[s, l, 0:1, :].partition_broadcast(128), [("adag", s, l, 0)], ["gb"])
        dma("sp", lng[:], lnr_d[l, 0:1, :].partition_broadcast(128), [], ["lnrow"])
        dma("sp", lnb[:], lnr_d[l, 1:2, :].partition_broadcast(128), [], ["lnrow"])
        for t in range(NT):
            mT, mTk = mTR.next()
            qb = (t * 128) // QB
            dma("sp", mT[:], M_d[:, :, t * 128:(t + 1) * 128].rearrange("c p t -> p c t"), [("M", c, qb) for c in range(8)], [mTk])
            bps = (2 * (t % 2), 2 * (t % 2) + 1)
            for hf in range(2):
                for kc in range(KC):
                    mm(ps[bps[hf]][:, :], mT[:, kc, :], wout[:, kc, hf * 512:(hf + 1) * 512], kc == 0, kc == KC - 1, [mTk, "wout"], [pk(bps[hf])])
            ln_epilogue(s, l, t, bps, gb, lng, lnb, zR, False, nexth=lambda t_: h_tile(s, l, 1, t_, xnR, (4, 5, 6, 7)))
        fence()
        arT.reset()
        arR.reset()

    def phase_mlp(s, l, last):
        arT.reset()
        arR.reset()
        wd = arT.alloc([128, NFC, D], BF16, "wd")
        wd_top = arT.top
        convc = arT.alloc([128, NFC, 4], F32, "convc")
        dma("sp", convc[:], convc_d[l], [], ["convc"])
        wuR = Ring(arT, 2, [128, KC, 256], BF16, "wu")
        aext = arR.alloc([128, S + 2], F32, "aext")
        u = arR.alloc([128, S], F32, "u")
        geR = Ring(arR, 1, [128, S], BF16, "ge")
        gR = Ring(arR, 2, [128, S], BF16, "g")
        P.add("dve", lambda e: e.memset(aext[:, 0:1], 0.0), w=["aext"])
        P.add("dve", lambda e: e.memset(aext[:, S + 1:S + 2], 0.0), r=["aext"], w=["aext"])
        allH = [("HT", t) for t in range(NT)]
        i = 0
        for fc in range(NFC):
            wu, wuk_ = wuR.next()
            dma("pool", wu[:, :, 0:128], wup_d[l, :, :, fc * 128:(fc + 1) * 128], [], [wuk_])
            dma("pool", wu[:, :, 128:256], wup_d[l, :, :, FF + fc * 128:FF + (fc + 1) * 128], [], [wuk_])
            ge, gek = geR.next()
            g, gk = gR.next()
            bbs = []
            for qb in range(NQB):
                qs = slice(qb * QB, (qb + 1) * QB)
                ba = i % 4
                bb = 4 + (i % 4)
                i += 1
                bbs.append(bb)
                for kc in range(KC):
                    mm(ps[ba][:, 0:QB], wu[:, kc, 0:128], HT[:, kc, qs], kc == 0, kc == KC - 1, allH + [wuk_], [pk(ba)])
                for kc in range(KC):
                    mm(ps[bb][:, 0:QB], wu[:, kc, 128:256], HT[:, kc, qs], kc == 0, kc == KC - 1, allH + [wuk_], [pk(bb)])
                act(aext[:, 1 + qb * QB:1 + (qb + 1) * QB], ps[ba][:, 0:QB], AF.Copy, [pk(ba)], ["aext"])
            act(u[:], aext[:, 1:S + 1], AF.Identity, ["aext", "convc"], ["u"], scale=convc[:, fc, 1:2], bias=convc[:, fc, 3:4])
            stt(u[:], aext[:, 0:S], convc[:, fc, 0:1], u[:], ALU.mult, ALU.add, ["aext", "convc", "u"], ["u"])
            stt(u[:], aext[:, 2:S + 2], convc[:, fc, 2:3], u[:], ALU.mult, ALU.add, ["aext", "convc", "u"], ["u"])
            act(ge[:], u[:], AF.Gelu, ["u"], [gek])
            for qb in range(NQB):
                qs = slice(qb * QB, (qb + 1) * QB)
                tt(g[:, qs], ge[:, qs], ps[bbs[qb]][:, 0:QB], ALU.mult, [gek, pk(bbs[qb])] + ([gk] if qb else []), [gk])
            dma("sp", G_d[fc], g[:], [gk], [("G", fc)])
            if fc in (3, 10):
                q4 = 0 if fc == 3 else 1
                dma("pool", wd[:, q4 * 11:(q4 + 1) * 11, :], wdn_d[l, :, q4 * 11:(q4 + 1) * 11, :], [], ["wd"])
        fence()
        arR.reset()
        arT.top = wd_top
        xnR = Ring(arT, 2, [128, 1024], BF16, "xn")
        gtR = Ring(arR, 2, [128, NFC, 128], BF16, "gt")
        gb = arR.alloc([128, D], F32, "gb2")
        lng = arR.alloc([128, D], F32, "lng2")
        lnb = arR.alloc([128, D], F32, "lnb2")
        zR = Ring(arR, 2, [128, D], F32, "z2")
        dma("sp", gb[:], adag_d[s, l, 1:2, :].partition_broadcast(128), [("adag", s, l, 1)], ["gb"])
        dma("sp", lng[:], lnr_d[l, 2:3, :].partition_broadcast(128), [], ["lnrow"])
        dma("sp", lnb[:], lnr_d[l, 3:4, :].partition_broadcast(128), [], ["lnrow"])
        for t in range(NT):
            gt, gtk = gtR.next()
            dma("sp", gt[:], G_d[:, :, t * 128:(t + 1) * 128].rearrange("c p t -> p c t"), [("G", fc) for fc in range(NFC)], [gtk])
            bps = (2 * (t % 2), 2 * (t % 2) + 1)
            for hf in range(2):
                for fc in range(NFC):
                    mm(ps[bps[hf]][:, :], gt[:, fc, :], wd[:, fc, hf * 512:(hf + 1) * 512], fc == 0, fc == NFC - 1, [gtk, "wd"], [pk(bps[hf])])
            ln_epilogue(s, l, t, bps, gb, lng, lnb, zR, last,
                        nexth=None if last else (lambda t_: h_tile(s, l + 1, 0, t_, xnR, (4, 5, 6, 7))))
        fence()
        arT.reset()
        arR.reset()

    for s in range(NSEQ):
        for t in range(NT):
            dma("sp", X[:, t, :], x_d[s, t * 128:(t + 1) * 128, :], [], [("X", t)])
        for l in range(L):
            layer_consts(l)
            if l == 0:
                phase_H(s, l, 0)
            phase_mla(s, l)
            phase_gate(l, wmo_d, 3520 + 1024, True)
            phase_ret(s, l)
            phase_gate(l, wro_d, 3520, False)
            phase_out(s, l)
            phase_mlp(s, l, l == L - 1)
    P.add("sp", lambda e: None, r=[("y", s, t) for s in range(NSEQ) for t in range(NT)])

    es = ExitStack()
    P.emit(nc, es)
    es.close()
    return nc, P


def host_consts(S):
    NT = S // 128
    pos = np.arange(S, dtype=np.float32)

    def tables(dim):
        inv = (1.0 / (10000.0 ** (np.arange(0, dim, 2, dtype=np.float32) / np.float32(dim)))).astype(np.float32)
        ang = pos[:, None] * inv[None, :]
        return np.cos(ang).astype(np.float32), np.sin(ang).astype(np.float32)

    def tm(a):
        return np.ascontiguousarray(a.reshape(NT, 128, -1).transpose(1, 0, 2))

    cr, sr = tables(128)
    cm, sm = tables(64)
    k = np.arange(128, dtype=np.float32)[:, None]
    q = np.arange(128, dtype=np.float32)[None, :]
    cst4 = np.stack([np.maximum(q - k, 0.0), np.maximum(k - q, 0.0), (q >= k).astype(np.float32), (k > q).astype(np.float32)], axis=1)
    c = np.arange(128, dtype=np.float32)
    dexp = np.stack([c + 1.0, 128.0 - c, 127.0 - c, c], axis=1)
    return {
        "ident": np.eye(128, dtype=np.float32),
        "cosr": tm(cr), "ssr": tm(np.concatenate([-sr, sr], axis=1)),
        "cosm": tm(cm), "ssm": tm(np.concatenate([-sm, sm], axis=1)),
        "cst4": np.ascontiguousarray(cst4.astype(np.float32)), "dexp": np.ascontiguousarray(dexp.astype(np.float32)),
    }


def host_weights(inp, L):
    f = lambda a: np.ascontiguousarray(np.asarray(a, dtype=np.float32))

    def pk(w, kc):
        w = np.asarray(w, dtype=np.float32)
        return np.ascontiguousarray(w.reshape(L, kc, 128, w.shape[-1]).transpose(0, 2, 1, 3))

    def col(v, nb):
        v = np.asarray(v, dtype=np.float32)
        return np.ascontiguousarray(v.reshape(L, nb, 128).transpose(0, 2, 1))

    conv = np.concatenate([np.asarray(inp["conv_w"], np.float32), np.asarray(inp["conv_b"], np.float32)[:, None, :]], axis=1)
    convcol = np.ascontiguousarray(conv.reshape(L, 4, NFC, 128).transpose(0, 3, 2, 1))
    b_ada = np.asarray(inp["b_ada"], np.float32)
    return {
        "w_ada": pk(inp["w_ada"], KC), "b_ada": f(b_ada),
        "b_adacol": np.ascontiguousarray(b_ada.reshape(L, 48, 128).transpose(2, 0, 1)),
        "w_in": pk(inp["w_in"], KC),
        "dec": f(np.concatenate([np.asarray(inp["ret_decay_fwd"], np.float32), np.asarray(inp["ret_decay_bwd"], np.float32)], axis=1)),
        "gncol": col(inp["ret_gn_g"], 8), "w_ret_o": pk(inp["w_ret_o"], KC),
        "qgcol": col(inp["q_norm_g"], 2), "kvgcol": col(inp["kv_norm_g"], 1),
        "w_uq": pk(inp["w_uq"], 2), "w_uk": f(inp["w_uk"]), "w_uv": f(inp["w_uv"]),
        "w_mla_o": pk(inp["w_mla_o"], KC), "w_out": pk(inp["w_out"], KC),
        "lnrows": f(np.stack([np.asarray(inp[k], np.float32) for k in ("ln1_g", "ln1_b", "ln2_g", "ln2_b")], axis=1)),
        "w_up": pk(inp["w_up"], KC), "convcol": convcol, "w_down": pk(inp["w_down"], NFC),
    }


_cache = {}
_runkw = {}
_last = [None]


def run(xs, cs, inp, n_cores, L):
    NTOT, S, _ = xs.shape
    NSEQ = NTOT // n_cores
    key = (NSEQ, S, L)
    if key not in _cache:
        _cache[key] = build(NSEQ, S, L)[0]
    nc = _cache[key]
    shared = dict(host_consts(S))
    shared.update(host_weights(inp, L))
    in_maps = []
    for i in range(n_cores):
        m = dict(shared)
        m["x"] = np.ascontiguousarray(xs[i * NSEQ:(i + 1) * NSEQ])
        cc = cs[i * NSEQ:(i + 1) * NSEQ]
        m["cT"] = np.ascontiguousarray(cc.reshape(NSEQ, KC, 128).transpose(2, 1, 0))
        in_maps.append(m)
    res = run_bass_kernel_spmd(nc, in_maps, core_ids=list(range(n_cores)), **_runkw)
    _last[0] = res
    return np.concatenate([r["y"] for r in res.results], axis=0)


def kernel(**inp):
    xp = np.asarray(inp["x_prompt"], np.float32)
    xsm = np.asarray(inp["x_sample"], np.float32)
    xs = np.concatenate([xp, xsm], axis=0)
    cs = np.concatenate([np.asarray(inp["c_prompt"], np.float32), np.asarray(inp["c_sample"], np.float32)], axis=0)
    L = np.asarray(inp["w_in"]).shape[0]
    y = run(xs, cs, inp, 8, L)
    nb = xp.shape[0]
    return (np.ascontiguousarray(y[:nb]), np.ascontiguousarray(y[nb:]))
```

```python
import math
from contextlib import ExitStack

import numpy as np
import concourse.bass as bass
import concourse.mybir as mybir
from concourse.bass_utils import run_bass_kernel_spmd

F32 = mybir.dt.float32
BF16 = mybir.dt.bfloat16
AF = mybir.ActivationFunctionType
ALU = mybir.AluOpType

D = 1024
KC = 8
FF = 2816
NFC = 22
INW = 5568
LN_EPS = 1e-5
RMS_EPS = 1e-6
DEPTH_FULL = 4
ALPHA = (2.0 * DEPTH_FULL) ** 0.25
NSLOT = 12
AKEY = "__arena__"


class Op:
    __slots__ = ("eng", "fn", "deps", "dma", "slot", "slotval", "signal", "sigval", "idx", "epoch", "fence", "cost", "pos")


NEP = 8
SCHED = True
import os as _os
WINDOW = int(_os.environ.get('K_WINDOW', '48'))
PRIO = True
PRIO_EPS = 1.0
LAT_NS = float(_os.environ.get('K_LAT', '120'))
ENGS = ("pe", "act", "dve", "pool", "sp")


class Prog:
    def __init__(self):
        self.ops = []
        self.lastw = {}
        self.rd = {}
        self.epoch = 0

    def add(self, eng, fn, r=(), w=(), dma=False, fence=False, c=200.0):
        op = Op()
        op.eng, op.fn, op.dma, op.idx = eng, fn, dma, len(self.ops)
        op.signal = dma
        op.sigval = 0
        op.epoch = self.epoch
        op.fence = fence
        op.cost = c
        if fence:
            r, w = (), (AKEY,)
            self.epoch += 1
        else:
            r = tuple(r) + (AKEY,)
        deps = set()
        for k in r:
            d = self.lastw.get(k)
            if d is not None:
                deps.add(d)
            if isinstance(k, tuple) and k[0] == "ps":
                for d in self.rd.get(k, ()):
                    if self.ops[d].eng != eng:
                        deps.add(d)
        for k in w:
            d = self.lastw.get(k)
            if d is not None:
                deps.add(d)
            deps.update(self.rd.get(k, ()))
        deps.discard(op.idx)
        op.deps = deps
        for k in w:
            self.lastw[k] = op.idx
            self.rd[k] = []
        for k in r:
            self.rd.setdefault(k, []).append(op.idx)
        self.ops.append(op)
        return op

    def schedule(self):
        import heapq
        ops = self.ops
        order = {e: [] for e in ENGS}
        nep = self.epoch + 1
        byep = [[] for _ in range(nep)]
        for op in ops:
            byep[op.epoch].append(op.idx)
        done = [None] * len(ops)
        LAT = LAT_NS
        for ep in range(nep):
            idxs = byep[ep]
            if not idxs:
                continue
            if not SCHED:
                for i in idxs:
                    order[ops[i].eng].append(i)
                continue
            pend = {e: [i for i in idxs if ops[i].eng == e] for e in ENGS}
            head = {e: 0 for e in ENGS}
            taken = set()
            free = {e: 0.0 for e in ENGS}
            dmapipe = 0.0
            ev = [0.0]
            remaining = len(idxs)
            ldeps = {}
            for i in idxs:
                ldeps[i] = [d for d in ops[i].deps if ops[d].epoch == ep and not ops[d].fence]
            blev = {}
            if PRIO:
                for i in reversed(idxs):
                    b_ = blev.get(i, 0.0) + ops[i].cost + (2000.0 if ops[i].dma else LAT)
                    blev[i] = b_
                    for d in ldeps[i]:
                        if blev.get(d, 0.0) < b_:
                            blev[d] = b_
            while remaining:
                T = heapq.heappop(ev)
                while ev and ev[0] <= T:
                    heapq.heappop(ev)
                for e in ENGS:
                    if free[e] > T:
                        continue
                    pl = pend[e]
                    h = head[e]
                    while h < len(pl) and pl[h] in taken:
                        h += 1
                    head[e] = h
                    if h >= len(pl):
                        continue
                    pick = None
                    scanned = 0
                    j = h
                    while j < len(pl) and scanned < WINDOW:
                        i = pl[j]
                        j += 1
                        if i in taken:
                            continue
                        scanned += 1
                        ok = True
                        for d in ldeps[i]:
                            dt_ = done[d]
                            if dt_ is None or dt_ + LAT > T:
                                ok = False
                                break
                        if ok:
                            if not PRIO:
                                pick = i
                                break
                            if pick is None or blev[i] > blev[pick] + PRIO_EPS:
                                pick = i
                        if ops[i].fence:
                            break
                    if pick is None:
                        continue
                    op = ops[pick]
                    taken.add(pick)
                    remaining -= 1
                    order[e].append(pick)
                    if op.dma:
                        issue = 100.0 if e != "pool" else 600.0
                        free[e] = T + issue
                        st = max(T + issue, dmapipe)
                        dmapipe = st + op.cost
                        done[pick] = dmapipe + 2000.0
                    else:
                        free[e] = T + op.cost
                        done[pick] = T + op.cost
                    heapq.heappush(ev, free[e])
                    heapq.heappush(ev, done[pick] + LAT)
                if not ev and remaining:
                    raise RuntimeError("scheduler stuck")
        return order

    def finalize(self):
        ops = self.ops
        order = self.schedule()
        self.order = order
        for e in ENGS:
            for p, i in enumerate(order[e]):
                ops[i].pos = p
        dmak = {}
        for e in ENGS:
            for i in order[e]:
                op = ops[i]
                if op.dma:
                    k = dmak.get(op.eng, 0)
                    op.slot = (op.eng, k % NSLOT)
                    op.slotval = 16 * (k // NSLOT + 1)
                    dmak[op.eng] = k + 1
        lastfence = None
        fences = {}
        for op in ops:
            if op.fence:
                fences[op.epoch] = op.idx
        for op in ops:
            best = {}
            bestd = {}
            for d in op.deps:
                o = ops[d]
                if o.fence or o.epoch != op.epoch:
                    continue
                if o.dma:
                    if o.slot not in bestd or ops[bestd[o.slot]].slotval < o.slotval:
                        bestd[o.slot] = d
                else:
                    if o.eng == "pe" and op.eng == "pe" and not op.dma:
                        continue
                    if o.eng not in best or ops[best[o.eng]].pos < o.pos:
                        best[o.eng] = d
            op.deps = list(best.values()) + list(bestd.values())
            if op.epoch > 0:
                op.deps = [fences[op.epoch - 1]] + op.deps
            for d in op.deps:
                ops[d].signal = True
        cnt = {}
        nf = 0
        for op in ops:
            if op.fence:
                nf += 1
                op.sigval = nf
        for e in ENGS:
            for i in order[e]:
                op = ops[i]
                if not op.fence and not op.dma and op.signal:
                    key = (op.eng, op.epoch % NEP)
                    cnt[key] = cnt.get(key, 0) + 1
                    op.sigval = cnt[key]
        self.sigcounts = cnt
        self.dmacounts = dmak

    def emit(self, nc, es):
        self.finalize()
        ops = self.ops
        csem = {(e, j): es.enter_context(nc.semaphore("c_%s%d" % (e, j))) for e in ENGS for j in range(NEP)}
        fsem = es.enter_context(nc.semaphore("fence"))
        dsem = {}
        for q in ENGS:
            if self.dmacounts.get(q, 0) > 0:
                for s in range(NSLOT):
                    dsem[(q, s)] = es.enter_context(nc.semaphore("d_%s%d" % (q, s)))
        block = es.enter_context(nc.Block())
        order = self.order

        def run(e, eng):
            seen = {}
            for i in order[e]:
                op = ops[i]
                for d in op.deps:
                    o = ops[d]
                    if o.fence:
                        key, sem, val = "fence", fsem, o.sigval
                    elif o.dma:
                        key, sem, val = ("d",) + o.slot, dsem[o.slot], o.slotval
                    else:
                        key = ("c", o.eng, o.epoch % NEP)
                        sem, val = csem[key[1:]], o.sigval
                    if seen.get(key, 0) < val:
                        eng.wait_ge(sem, val)
                        seen[key] = val
                if op.dma:
                    prev = op.slotval - 16
                    if prev > 0 and seen.get(("d",) + op.slot, 0) < prev:
                        eng.wait_ge(dsem[op.slot], prev)
                        seen[("d",) + op.slot] = prev
                ins = op.fn(eng)
                if ins is None:
                    continue
                if op.fence:
                    ins.then_inc(fsem, 1)
                elif op.dma:
                    ins.then_inc(dsem[op.slot], 16)
                elif op.signal:
                    ins.then_inc(csem[(op.eng, op.epoch % NEP)], 1)

        @block.tensor
        def _(eng):
            run("pe", eng)

        @block.scalar
        def _(eng):
            run("act", eng)

        @block.vector
        def _(eng):
            run("dve", eng)

        @block.gpsimd
        def _(eng):
            run("pool", eng)

        @block.sync
        def _(eng):
            run("sp", eng)


_uid = [0]


class Arena:
    def __init__(self, nc, lo, hi):
        self.nc, self.lo, self.hi, self.top = nc, lo, hi, lo

    def alloc(self, shape, dt, name="t"):
        esz = 4 if dt == F32 else 2
        nb = esz
        for s in shape[1:]:
            nb *= s
        nb = (nb + 31) // 32 * 32
        off = self.top
        self.top += nb
        assert self.top <= self.hi, "SBUF arena overflow: %s %s need=%d over=%d" % (name, shape, nb, self.top - self.hi)
        _uid[0] += 1
        return self.nc.alloc_sbuf_tensor_at("%s_%d" % (name, _uid[0]), list(shape), dt, offset=off)

    def reset(self):
        self.top = self.lo


class Ring:
    def __init__(self, ar, n, shape, dt, name):
        self.t = [ar.alloc(shape, dt, name) for _ in range(n)]
        self.i = -1
        _uid[0] += 1
        self.name = "%s#%d" % (name, _uid[0])

    def next(self):
        self.i += 1
        j = self.i % len(self.t)
        return self.t[j], (self.name, j)


def build(NSEQ, S, L):
    NT = S // 128
    QB = min(512, S)
    NQB = S // QB
    TPQ = QB // 128
    C = 128
    nc = bass.Bass("TRN2", target_bir_lowering=False)

    def din(name, shape):
        return nc.dram_tensor(name, list(shape), F32, kind="ExternalInput").ap()

    x_d = din("x", [NSEQ, S, D])
    cT_d = din("cT", [128, KC, NSEQ])
    wada_d = din("w_ada", [L, 128, KC, 6 * D])
    bada_d = din("b_ada", [L, 6 * D])
    bcol_d = din("b_adacol", [128, L, 48])
    win_d = din("w_in", [L, 128, KC, INW])
    dec_d = din("dec", [L, 8])
    gn_d = din("gncol", [L, 128, 8])
    wro_d = din("w_ret_o", [L, 128, KC, D])
    qg_d = din("qgcol", [L, 128, 2])
    kvg_d = din("kvgcol", [L, 128, 1])
    wuq_d = din("w_uq", [L, 128, 2, 1536])
    wuk_d = din("w_uk", [L, 128, 1024])
    wuv_d = din("w_uv", [L, 128, 1024])
    wmo_d = din("w_mla_o", [L, 128, KC, D])
    wout_d = din("w_out", [L, 128, KC, D])
    lnr_d = din("lnrows", [L, 4, D])
    wup_d = din("w_up", [L, 128, KC, 2 * FF])
    convc_d = din("convcol", [L, 128, NFC, 4])
    wdn_d = din("w_down", [L, 128, NFC, D])
    ident_d = din("ident", [128, 128])
    cosr_d = din("cosr", [128, NT, 64])
    ssr_d = din("ssr", [128, NT, 128])
    cosm_d = din("cosm", [128, NT, 32])
    ssm_d = din("ssm", [128, NT, 64])
    cst4_d = din("cst4", [128, 4, 128])
    dexp_d = din("dexp", [128, 4])
    y_d = nc.dram_tensor("y", [NSEQ, S, D], F32, kind="ExternalOutput").ap()
    adag_d = nc.dram_tensor("adag_scr", [NSEQ, L, 2, D], F32, kind="Internal").ap()
    M_d = nc.dram_tensor("M_scr", [KC, 128, S], BF16, kind="Internal").ap()
    G_d = nc.dram_tensor("G_scr", [NFC, 128, S], BF16, kind="Internal").ap()

    P = Prog()
    ps = [nc.alloc_psum_tensor("ps%d" % i, [128, 512], F32) for i in range(8)]

    def pk(b):
        return ("ps", b)

    LO, HI = 16512, 229376
    XB = NT * 1024 * 4
    HB = max(8 * S * 2, 32768)
    off = LO
    arX = Arena(nc, off, off + XB); off += XB
    arH = Arena(nc, off, off + HB); off += HB
    arR = Arena(nc, off, off + HB); off += HB
    arC = Arena(nc, off, off + 26112); off += 26112
    arT = Arena(nc, off, HI)
    X = arX.alloc([128, NT, 1024], F32, "X")
    HT = arH.alloc([128, 8, S], BF16, "HT")
    R2 = arR.alloc([128, 8, S], BF16, "R2")
    ident = arC.alloc([128, 128], BF16, "ident")
    ones = arC.alloc([128, 128], F32, "ones")
    onesb = arC.alloc([128, 128], BF16, "onesb")
    cosr = arC.alloc([128, NT, 64], F32, "cosr")
    ssr = arC.alloc([128, NT, 128], F32, "ssr")
    cosm = arC.alloc([128, NT, 32], F32, "cosm")
    ssm = arC.alloc([128, NT, 64], F32, "ssm")
    cst4 = arC.alloc([128, 4, 128], F32, "cst4")
    dexp = arC.alloc([128, 4], F32, "dexp")
    cvals = arC.alloc([128, 4], F32, "cvals")
    adaT = arC.alloc([128, L, 4, 8, NSEQ], F32, "adaT")
    siluT = arC.alloc([128, KC, NSEQ], BF16, "siluT")
    cTs = arC.alloc([128, KC, NSEQ], F32, "cTs")
    dec = arC.alloc([128, 8], F32, "dec")
    lg = arC.alloc([128, 8], F32, "lg")
    gC = arC.alloc([128, 8], F32, "gC")
    dcol = arC.alloc([128, 4, 4], F32, "dcol")
    kplain = arC.alloc([128, 1], F32, "kplain")
    maskT = arC.alloc([128, 4, 128], F32, "maskT")
    gncol = arC.alloc([128, 8], F32, "gncol")
    qgcol = arC.alloc([128, 2], F32, "qgcol")
    kvgcol = arC.alloc([128, 1], F32, "kvgcol")
    stat = [arC.alloc([128, 24], F32, "stat%d" % i) for i in range(4)]
    stat_i = [-1]

    def nstat():
        stat_i[0] += 1
        j = stat_i[0] % 4
        return stat[j], ("stat", j)

    def mm(out, lhsT, rhs, st, sp, r, w):
        P.add("pe", lambda e: e.matmul(out, lhsT=lhsT, rhs=rhs, start=st, stop=sp), r=r, w=w, c=0.5 * rhs.free_size() + 30.0)

    def tp(out, in_, n, r, w):
        P.add("pe", lambda e: e.transpose(out, in_, ident[0:n, 0:n]), r=tuple(r) + ("ident",), w=w, c=100.0)

    def act(out, in_, func, r, w, bias=None, scale=None, accum=None):
        kw = {}
        if bias is not None:
            kw["bias"] = bias
        if scale is not None:
            kw["scale"] = scale
        if accum is not None:
            kw["accum_out"] = accum
        P.add("act", lambda e: e.activation(out=out, in_=in_, func=func, **kw), r=r, w=w, c=out.free_size() / 1.2 + 250.0)

    def tt(out, in0, in1, op, r, w, eng="dve"):
        P.add(eng, lambda e: e.tensor_tensor(out=out, in0=in0, in1=in1, op=op), r=r, w=w,
              c=out.free_size() / (0.9 if eng == "dve" else 0.45) + 150.0)

    def ts(out, in0, s1, s2, op0, op1, r, w):
        if s2 is None:
            P.add("dve", lambda e: e.tensor_scalar(out=out, in0=in0, scalar1=s1, scalar2=None, op0=op0), r=r, w=w, c=out.free_size() / 0.9 + 150.0)
        else:
            P.add("dve", lambda e: e.tensor_scalar(out=out, in0=in0, scalar1=s1, scalar2=s2, op0=op0, op1=op1), r=r, w=w, c=out.free_size() / 0.9 + 150.0)

    def stt(out, in0, sc, in1, op0, op1, r, w, eng="dve"):
        P.add(eng, lambda e: e.scalar_tensor_tensor(out=out, in0=in0, scalar=sc, in1=in1, op0=op0, op1=op1), r=r, w=w,
              c=out.free_size() / (0.9 if eng == "dve" else 0.45) + 150.0)

    def cp(out, in_, r, w, eng="dve"):
        if eng == "act":
            act(out, in_, AF.Copy, r, w)
            return
        k = 0.9 if eng == "dve" else 0.45
        P.add(eng, lambda e: e.tensor_copy(out=out, in_=in_), r=r, w=w, c=out.free_size() / k + 150.0)

    def dma(q, out, in_, r, w):
        P.add(q, lambda e: e.dma_start(out=out, in_=in_), r=r, w=w, dma=True, c=out.free_nbytes() * out.partition_size() / 120.0 + 300.0)

    def fence():
        P.add("sp", lambda e: e.nop(), fence=True)

    def rsq(out, in_, n, r, w):
        P.add("pool", lambda e: e.tensor_tensor(out=out, in0=in_, in1=cvals[:, 2:3], op=ALU.pow),
              r=tuple(r) + ("cvals",), w=w, c=600.0)

    def ln_stats(src_ap, srckeys, eps_col):
        stt_, sk = nstat()
        P.add("dve", lambda e: e.bn_stats(out=stt_[:, 0:6], in_=src_ap[:, 0:512]), r=srckeys, w=[sk])
        P.add("dve", lambda e: e.bn_stats(out=stt_[:, 6:12], in_=src_ap[:, 512:1024]), r=list(srckeys) + [sk], w=[sk])
        P.add("dve", lambda e: e.bn_aggr(out=stt_[:, 12:14], in_=stt_[:, 0:12].rearrange("p (a b) -> p a b", a=2)), r=[sk], w=[sk])
        ts(stt_[:, 14:15], stt_[:, 13:14], LN_EPS, None, ALU.add, None, [sk], [sk])
        rsq(stt_[:, 16:17], stt_[:, 14:15], 1, [sk], [sk])
        stt(stt_[:, 17:18], stt_[:, 12:13], -1.0, stt_[:, 16:17], ALU.mult, ALU.mult, [sk], [sk])
        return stt_, sk

    dma("pool", ident[:], ident_d, [], ["ident"])
    dma("sp", cosr[:], cosr_d, [], ["rope"])
    dma("sp", ssr[:], ssr_d, [], ["rope"])
    dma("sp", cosm[:], cosm_d, [], ["rope"])
    dma("sp", ssm[:], ssm_d, [], ["rope"])
    dma("sp", cst4[:], cst4_d, [], ["cst4"])
    dma("sp", dexp[:], dexp_d, [], ["dexp"])
    dma("sp", cTs[:], cT_d, [], ["cTs"])
    P.add("dve", lambda e: e.memset(ones[:], 1.0), w=["ones"])
    P.add("dve", lambda e: e.memset(onesb[:], 1.0), w=["onesb"])
    P.add("dve", lambda e: e.memset(cvals[:, 0:1], LN_EPS), w=["cvals"])
    P.add("dve", lambda e: e.memset(cvals[:, 1:2], RMS_EPS), r=["cvals"], w=["cvals"])
    P.add("dve", lambda e: e.memset(cvals[:, 2:3], -0.5), r=["cvals"], w=["cvals"])
    P.add("dve", lambda e: e.memset(cvals[:, 3:4], 1.0), r=["cvals"], w=["cvals"])
    P.add("dve", lambda e: e.memset(kplain[:], 128.0 ** -0.5), w=["kplain"])
    act(siluT[:], cTs[:], AF.Silu, ["cTs"], ["siluT"])

    bcol = arT.alloc([128, L, 48], F32, "bcol")
    browR = Ring(arT, 2, [1, 512], F32, "brow")
    dma("sp", bcol[:], bcol_d, [], ["bcol"])
    waR = Ring(arT, 2, [128, KC, 512], BF16, "wa")
    growR = Ring(arT, 2, [1, 512], F32, "grow")
    pr = [0]
    for l in range(L):
        for nb in range(12):
            wa, wak = waR.next()
            dma("pool", wa[:], wada_d[l, :, :, nb * 512:(nb + 1) * 512], [], [wak])
            blk = nb // 2
            if blk in (2, 5):
                brow, browk = browR.next()
                dma("sp", brow[:], bada_d[l:l + 1, nb * 512:(nb + 1) * 512], [], [browk])
                for s in range(NSEQ):
                    b = pr[0] % 4
                    pr[0] += 1
                    for kc in range(KC):
                        mm(ps[b][0:1, :], siluT[:, kc, s:s + 1], wa[:, kc, :], kc == 0, kc == KC - 1, [wak, "siluT"], [pk(b)])
                    gr, grk = growR.next()
                    n0 = nb * 512
                    tt(gr[:], ps[b][0:1, :], brow[:], ALU.add, [pk(b), browk], [grk])
                    ts(gr[:], gr[:], 1.0, None, ALU.add, None, [grk], [grk])
                    wi = 0 if blk == 2 else 1
                    hf = nb % 2
                    dma("sp", adag_d[s, l, wi:wi + 1, hf * 512:(hf + 1) * 512], gr[:], [grk], [("adag", s, l, wi)])
            else:
                wi = {0: 0, 1: 1, 3: 2, 4: 3}[blk]
                b = 4 + pr[0] % 4
                pr[0] += 1
                for j in range(4):
                    for kc in range(KC):
                        mm(ps[b][:, j * 8:j * 8 + NSEQ], wa[:, kc, j * 128:(j + 1) * 128], siluT[:, kc, :], kc == 0, kc == KC - 1,
                           [wak, "siluT"], [pk(b)])
                for j in range(4):
                    fb = (nb % 2) * 4 + j
                    cb = nb * 4 + j
                    act(adaT[:, l, wi, fb, :], ps[b][:, j * 8:j * 8 + NSEQ], AF.Identity, [pk(b), "bcol"], ["adaT"],
                        bias=bcol[:, l, cb:cb + 1], scale=1.0)
    for l in range(L):
        for wi in (1, 3):
            ts(adaT[:, l, wi], adaT[:, l, wi], 1.0, None, ALU.add, None, ["adaT"], ["adaT"])
    fence()
    arT.reset()

    def h_tile(s, l, which, t, xnR, banks):
        shw, scw = (0, 1) if which == 0 else (2, 3)
        st_, sk = ln_stats(X[:, t, :], [("X", t)], 0)
        xn, xk = xnR.next()
        act(xn[:], X[:, t, :], AF.Identity, [("X", t), sk], [xk], bias=st_[:, 17:18], scale=st_[:, 16:17])
        b = banks[t % len(banks)]
        pb = ps[b][:].bitcast(BF16)
        for kc in range(KC):
            tp(pb[:, kc * 128:(kc + 1) * 128], xn[:, kc * 128:(kc + 1) * 128], 128, [xk], [pk(b)])
        for kc in range(KC):
            act(HT[:, kc, t * 128:(t + 1) * 128], pb[:, kc * 128:(kc + 1) * 128], AF.Identity, [pk(b), "adaT"], [("HT", t)],
                bias=adaT[:, l, shw, kc, s:s + 1], scale=adaT[:, l, scw, kc, s:s + 1])

    def phase_H(s, l, which):
        xnR = Ring(arT, 3, [128, 1024], BF16, "xn")
        for t in range(NT):
            h_tile(s, l, which, t, xnR, (0, 1, 2, 3))
        fence()
        arT.reset()

    def layer_consts(l):
        dma("sp", dec[:], dec_d[l:l + 1, :].partition_broadcast(128), [], ["dec"])
        dma("sp", gncol[:], gn_d[l], [], ["gncol"])
        dma("sp", qgcol[:], qg_d[l], [], ["qgcol"])
        dma("sp", kvgcol[:], kvg_d[l], [], ["kvgcol"])
        act(lg[:], dec[:], AF.Sigmoid, ["dec"], ["lg"])
        act(lg[:], lg[:], AF.Ln, ["lg"], ["lg"])
        act(gC[:], lg[:], AF.Exp, ["lg"], ["gC"], scale=float(C))
        e1 = arT.alloc([128, 128], F32, "e1")
        e2 = arT.alloc([128, 128], F32, "e2")
        for h in range(4):
            act(e1[:], cst4[:, 0, :], AF.Exp, ["cst4", "lg"], ["e1"], scale=lg[:, h:h + 1])
            tt(e1[:], e1[:], cst4[:, 2, :], ALU.mult, ["e1", "cst4"], ["e1"])
            act(e2[:], cst4[:, 1, :], AF.Exp, ["cst4", "lg"], ["e2"], scale=lg[:, 4 + h:5 + h])
            tt(e2[:], e2[:], cst4[:, 3, :], ALU.mult, ["e2", "cst4"], ["e2"])
            tt(maskT[:, h, :], e1[:], e2[:], ALU.add, ["e1", "e2"], ["maskT"])
            act(dcol[:, h, 0:1], dexp[:, 0:1], AF.Exp, ["dexp", "lg"], ["dcol"], scale=lg[:, h:h + 1])
            act(dcol[:, h, 1:2], dexp[:, 1:2], AF.Exp, ["dexp", "lg"], ["dcol"], scale=lg[:, 4 + h:5 + h])
            act(dcol[:, h, 2:3], dexp[:, 2:3], AF.Exp, ["dexp", "lg"], ["dcol"], scale=lg[:, h:h + 1])
            act(dcol[:, h, 3:4], dexp[:, 3:4], AF.Exp, ["dexp", "lg"], ["dcol"], scale=lg[:, 4 + h:5 + h])
            ts(dcol[:, h, 2:4], dcol[:, h, 2:4], 128.0 ** -0.5, None, ALU.mult, None, ["dcol"], ["dcol"])
        fence()
        arT.reset()

    def rope(dst, src, cos_b, ss, half, nh, tmpA, tmpB, r, w, kA, kB, addeng="dve"):
        tt(tmpA, src, cos_b, ALU.mult, list(r) + ["rope"], [kA])
        tt(tmpB[:, :, 0, :], src[:, :, 1, :], ss[:, :, 0, :], ALU.mult, list(r) + ["rope"], [kB])
        tt(tmpB[:, :, 1, :], src[:, :, 0, :], ss[:, :, 1, :], ALU.mult, list(r) + ["rope", kB], [kB])
        tt(dst, tmpA, tmpB, ALU.add, [kA, kB], w, eng=addeng)

    def phase_mla(s, l):
        arT.reset()
        cqT = arT.alloc([128, 2, S], BF16, "cqT")
        ckvT = arT.alloc([128, S], BF16, "ckvT")
        krT = arT.alloc([128, 2, S], BF16, "krT")
        qrT = arT.alloc([128, 4, S], BF16, "qrT")
        P.add("dve", lambda e: e.memset(krT[:], 0.0), w=["krT0"], c=2 * S / 0.9 + 150.0)
        base_top = arT.top
        wl = arT.alloc([128, KC, 448], BF16, "wl")
        wuqr = arT.alloc([128, 2, 8, 64], BF16, "wuqr")
        dma("pool", wl[:], win_d[l, :, :, 3072:3520], [], ["wl"])
        dma("pool", wuqr[:], wuq_d[l].rearrange("p k (h d) -> p k h d", h=8)[:, :, :, 128:192], [], ["wuqr"])
        latR = Ring(arT, 2, [128, 512], BF16, "lat")
        tAR = Ring(arT, 1, [128, 512], F32, "tA")
        tBR = Ring(arT, 1, [128, 512], F32, "tB")
        tAsR = Ring(arT, 1, [128, 64], F32, "tAs")
        tBsR = Ring(arT, 1, [128, 64], F32, "tBs")
        sqj = arT.alloc([128, 256], BF16, "sqj")
        qrtR = Ring(arT, 2, [128, 512], BF16, "qrt")
        for t in range(NT):
            b = 0 + (t % 2)
            for kc in range(KC):
                mm(ps[b][:, 0:448], HT[:, kc, t * 128:(t + 1) * 128], wl[:, kc, :], kc == 0, kc == KC - 1, [("HT", t), "wl"], [pk(b)])
            st_, sk = nstat()
            act(sqj[:, 0:256], ps[b][:, 0:256], AF.Square, [pk(b)], ["sqj", sk], accum=st_[:, 0:1])
            act(sqj[:, 0:128], ps[b][:, 256:384], AF.Square, [pk(b), sk], ["sqj", sk], accum=st_[:, 1:2])
            ts(st_[:, 2:3], st_[:, 0:1], 1.0 / 256.0, RMS_EPS, ALU.mult, ALU.add, [sk], [sk])
            ts(st_[:, 3:4], st_[:, 1:2], 1.0 / 128.0, RMS_EPS, ALU.mult, ALU.add, [sk], [sk])
            rsq(st_[:, 4:5], st_[:, 2:3], 1, [sk], [sk])
            rsq(st_[:, 5:6], st_[:, 3:4], 1, [sk], [sk])
            lat, lk = latR.next()
            act(lat[:, 0:256], ps[b][:, 0:256], AF.Identity, [pk(b), sk], [lk], scale=st_[:, 4:5], bias=0.0)
            act(lat[:, 256:384], ps[b][:, 256:384], AF.Identity, [pk(b), sk, lk], [lk], scale=st_[:, 5:6], bias=0.0)
            tA, tAk = tAsR.next()
            tB, tBk = tBsR.next()
            v4 = lambda ap: ap.rearrange("p (h a d) -> p h a d", h=1, a=2)
            cosb = cosm[:, t, :].unsqueeze(1).unsqueeze(1).broadcast_to([128, 1, 2, 32])
            ssb = ssm[:, t, :].rearrange("p (h a d) -> p h a d", h=1, a=2)
            rope(v4(lat[:, 384:448]), v4(ps[b][:, 384:448]), cosb, ssb, 32, 1, v4(tA[:, 0:64]), v4(tB[:, 0:64]),
                 [pk(b), lk], [lk], tAk, tBk)
            cp(lat[:, 448:512], lat[:, 384:448], [lk], [lk])
            b2 = 2 + (t % 2)
            pb = ps[b2][:].bitcast(BF16)
            for j in range(4):
                tp(pb[:, j * 128:(j + 1) * 128], lat[:, j * 128:(j + 1) * 128], 128, [lk], [pk(b2)])
            tsl = slice(t * 128, (t + 1) * 128)
            act(cqT[:, 0, tsl], pb[:, 0:128], AF.Identity, [pk(b2), "qgcol"], [("cqT", t)], scale=qgcol[:, 0:1], bias=0.0)
            act(cqT[:, 1, tsl], pb[:, 128:256], AF.Identity, [pk(b2), "qgcol", ("cqT", t)], [("cqT", t)], scale=qgcol[:, 1:2], bias=0.0)
            act(ckvT[:, tsl], pb[:, 256:384], AF.Identity, [pk(b2), "kvgcol"], [("ckvT", t)], scale=kvgcol[:, 0:1], bias=0.0)
            cp(krT[0:64, 0, tsl], pb[0:64, 384:512], [pk(b2), "krT0"], [("krT", t)])
            cp(krT[64:128, 1, tsl], pb[64:128, 384:512], [pk(b2), "krT0", ("krT", t)], [("krT", t)])
            b3 = 4 + (t % 2)
            for kc in range(2):
                mm(ps[b3][:, 0:512], cqT[:, kc, tsl], wuqr[:, kc].rearrange("p h d -> p (h d)"), kc == 0, kc == 1,
                   [("cqT", t), "wuqr"], [pk(b3)])
            tA, tAk = tAR.next()
            tB, tBk = tBR.next()
            qrt, qk = qrtR.next()
            v8 = lambda ap: ap.rearrange("p (h a d) -> p h a d", h=8, a=2)
            cosb8 = cosm[:, t, :].unsqueeze(1).unsqueeze(1).broadcast_to([128, 8, 2, 32])
            ssb8 = ssm[:, t, :].rearrange("p (a d) -> p a d", a=2).unsqueeze(1).broadcast_to([128, 8, 2, 32])
            rope(v8(qrt[:]), v8(ps[b3][:, 0:512]), cosb8, ssb8, 32, 8, v8(tA[:]), v8(tB[:]), [pk(b3)], [qk], tAk, tBk)
            b4 = 6 + (t % 2)
            pb4 = ps[b4][:].bitcast(BF16)
            for j in range(4):
                tp(pb4[:, j * 128:(j + 1) * 128], qrt[:, j * 128:(j + 1) * 128], 128, [qk], [pk(b4)])
            cp(qrT[:, :, tsl], pb4[:, 0:512].rearrange("p (j t) -> p j t", j=4), [pk(b4)], [("qrT", t)])
        fence()
        arT.top = base_top
        wkvR = Ring(arT, 1, [128, 2, 128], BF16, "wkv")
        wqnR = Ring(arT, 2, [128, 2, 128], BF16, "wqn")
        knT = arT.alloc([128, S], BF16, "knT")
        vh = arT.alloc([128, NT, 128], BF16, "vh")
        qnR = Ring(arT, 1, [128, QB], BF16, "qn")
        ptR = Ring(arT, 5, [128, QB], BF16, "pt")
        accR = Ring(arT, 1, [128, QB], F32, "acc")
        scale = 192.0 ** -0.5
        sc_i = [0]
        allT = [("ckvT", t) for t in range(NT)]
        for h in range(8):
            wqn, wqk = wqnR.next()
            wkv, wkvk = wkvR.next()
            dma("pool", wqn[:], wuq_d[l, :, :, h * 192:h * 192 + 128], [], [wqk])
            dma("pool", wkv[:, 0, :], wuk_d[l, :, h * 128:(h + 1) * 128], [], [wkvk])
            dma("pool", wkv[:, 1, :], wuv_d[l, :, h * 128:(h + 1) * 128], [], [wkvk])
            for qb in range(NQB):
                qs = slice(qb * QB, (qb + 1) * QB)
                mm(ps[7][:, 0:QB], wkv[:, 0, :], ckvT[:, qs], True, True, allT + [wkvk], [pk(7)])
                cp(knT[:, qs], ps[7][:, 0:QB], [pk(7)], ["knT"])
            for t4 in range(0, NT, 4):
                n4 = min(4, NT - t4)
                for j in range(n4):
                    t = t4 + j
                    mm(ps[7][:, j * 128:(j + 1) * 128], ckvT[:, t * 128:(t + 1) * 128], wkv[:, 1, :], True, True,
                       allT + [wkvk], [pk(7)])
                act(vh[:, t4:t4 + n4, :], ps[7][:, 0:n4 * 128].rearrange("p (j d) -> p j d", j=n4), AF.Copy, [pk(7)], ["vh"])
            pb_ = (h % 2) * 64
            for qb in range(NQB):
                qs = slice(qb * QB, (qb + 1) * QB)
                qn, qnk = qnR.next()
                for kc in range(2):
                    mm(ps[7][:, 0:QB], wqn[:, kc, :], cqT[:, kc, qs], kc == 0, kc == 1, [("cqT", t) for t in range(NT)] + [wqk], [pk(7)])
                cp(qn[:], ps[7][:, 0:QB], [pk(7)], [qnk])
                bo = 3 + (qb % 2)
                bd = 5 + (qb % 2)
                acc, acck = accR.next()
                odd = [k_ for k_ in range(NT) if k_ % 2 == 1]
                pt1 = None
                for kt in range(NT):
                    bs = sc_i[0] % 3
                    sc_i[0] += 1
                    ks = slice(kt * 128, (kt + 1) * 128)
                    mm(ps[bs][:, 0:QB], knT[:, ks], qn[:], True, False, ["knT", qnk], [pk(bs)])
                    mm(ps[bs][:, 0:QB], krT[:, h % 2, ks], qrT[:, h // 2, qs], False, True,
                       [("krT", kt)] + [("qrT", t) for t in range(NT)], [pk(bs)])
                    pt, ptk = ptR.next()
                    act(pt[:], ps[bs][:, 0:QB], AF.Exp, [pk(bs)], [ptk], scale=scale)
                    mm(ps[bo][:, 0:QB], vh[:, kt, :], pt[:], kt == 0, kt == NT - 1, ["vh", ptk], [pk(bo)])
                    if kt % 2 == 0:
                        mm(ps[bd][:, 0:QB], onesb[:], pt[:], kt == 0, False, ["onesb", ptk], [pk(bd)])
                    elif len(odd) == 1:
                        cp(acc[:], pt[:], [ptk], [acck])
                    elif kt == 1:
                        pt1, pt1k = pt, ptk
                    elif kt == 3:
                        tt(acc[:], pt1[:], pt[:], ALU.add, [pt1k, ptk], [acck])
                    else:
                        tt(acc[:], acc[:], pt[:], ALU.add, [ptk, acck], [acck])
                mm(ps[bd][:, 0:QB], ones[:], acc[:], False, True, ["ones", acck], [pk(bd)])
                P.add("dve", lambda e, acc=acc, bd=bd: e.reciprocal(out=acc[:], in_=ps[bd][:, 0:QB]), r=[pk(bd)], w=[acck], c=700.0)
                tt(R2[:, h, qs], ps[bo][:, 0:QB], acc[:], ALU.mult, [pk(bo), acck], [("R2", h)])
        fence()
        arT.reset()

    keep = {}

    def phase_gate(l, wsrc_d, gcol0, first):
        arT.reset()
        if not first:
            keep["wout"] = arT.alloc([128, KC, D], BF16, "wout")
        keep_top = arT.top
        woR = Ring(arT, 2, [128, KC, 128], BF16, "wo")
        wgR = Ring(arT, 2, [128, KC, 128], BF16, "wg")
        sgR = Ring(arT, 4, [128, QB], F32, "sg")
        mbR = Ring(arT, 4, [128, QB], BF16, "mb")
        mpR = Ring(arT, 4, [128, QB], BF16, "mp")
        i = 0
        allH = [("HT", t) for t in range(NT)]
        allR = [("R2", h) for h in range(8)]
        for c in range(8):
            wo, wok = woR.next()
            wg, wgk = wgR.next()
            dma("pool", wo[:], wsrc_d[l, :, :, c * 128:(c + 1) * 128], [], [wok])
            dma("pool", wg[:], win_d[l, :, :, gcol0 + c * 128:gcol0 + (c + 1) * 128], [], [wgk])
            if c == 1 and not first:
                dma("pool", keep["wout"][:], wout_d[l], [], ["wout"])
            for qb in range(NQB):
                qs = slice(qb * QB, (qb + 1) * QB)
                by = (i % 4)
                bg = 4 + (i % 4)
                i += 1
                for kc in range(KC):
                    mm(ps[by][:, 0:QB], wo[:, kc, :], R2[:, kc, qs], kc == 0, kc == KC - 1, allR + [wok], [pk(by)])
                for kc in range(KC):
                    mm(ps[bg][:, 0:QB], wg[:, kc, :], HT[:, kc, qs], kc == 0, kc == KC - 1, allH + [wgk], [pk(bg)])
                sg, sgk = sgR.next()
                act(sg[:], ps[bg][:, 0:QB], AF.Sigmoid, [pk(bg)], [sgk])
                mb, mbk = mbR.next()
                if first:
                    tt(mb[:], ps[by][:, 0:QB], sg[:], ALU.mult, [pk(by), sgk], [mbk])
                else:
                    mp, mpk = mpR.next()
                    dma("sp", mp[:], M_d[c, :, qs], [("M", c, qb)], [mpk])
                    tt(sg[:], ps[by][:, 0:QB], sg[:], ALU.mult, [pk(by), sgk], [sgk])
                    tt(mb[:], sg[:], mp[:], ALU.add, [sgk, mpk], [mbk])
                dma("sp", M_d[c, :, qs], mb[:], [mbk], [("M", c, qb)])
        fence()
        arT.top = keep_top

    def phase_ret(s, l):
        arT.reset()
        wr = arT.alloc([128, KC, 768], BF16, "wr")
        kT = arT.alloc([128, S], BF16, "kT")
        kf = arT.alloc([128, NT, 128], BF16, "kf")
        vtm = arT.alloc([128, NT, 256], BF16, "vtm")
        Sb = arT.alloc([128, NT, 256], BF16, "Sb")
        Rf = arT.alloc([128, 256], F32, "Rf")
        Rb = arT.alloc([128, 256], F32, "Rb")
        SfR = Ring(arT, 2, [128, 256], BF16, "Sf")
        tAR = Ring(arT, 2, [128, 128], F32, "rA")
        tBR = Ring(arT, 2, [128, 128], F32, "rB")
        krR = Ring(arT, 2, [128, 128], F32, "kr")
        k3R = Ring(arT, 2, [128, 3, 128], BF16, "k3")
        q3TR = Ring(arT, 2, [128, 3, 128], BF16, "q3T")
        sgR = Ring(arT, 2, [128, 256], BF16, "sgr")
        ptR = Ring(arT, 2, [128, 128], BF16, "ptr")
        ronR = Ring(arT, 2, [128, 256], F32, "ron")
        abR = Ring(arT, 2, [128, 256], BF16, "ab")
        v3 = lambda ap: ap.rearrange("p (h a d) -> p h a d", h=1, a=2)
        for h in range(4):
            dma("pool", wr[:, :, 384:512], win_d[l, :, :, 512 + h * 128:512 + (h + 1) * 128], [], ["wrkv"])
            dma("pool", wr[:, :, 512:768], win_d[l, :, :, 1024 + h * 256:1024 + (h + 1) * 256], [], ["wrkv"])
            dma("pool", wr[:, :, 0:128], win_d[l, :, :, h * 128:(h + 1) * 128], [], ["wrqg"])
            dma("pool", wr[:, :, 128:384], win_d[l, :, :, 2048 + h * 256:2048 + (h + 1) * 256], [], ["wrqg"])
            P.add("dve", lambda e: e.memset(Rf[:], 0.0), w=["Rf"])
            P.add("dve", lambda e: e.memset(Rb[:], 0.0), w=["Rb"])
            for n in range(NT - 1, -1, -1):
                ns = slice(n * 128, (n + 1) * 128)
                b = n % 2
                for kc in range(KC):
                    mm(ps[b][:, 0:384], HT[:, kc, ns], wr[:, kc, 384:768], kc == 0, kc == KC - 1, [("HT", n), "wrkv"], [pk(b)])
                tA, tAk = tAR.next()
                tB, tBk = tBR.next()
                kr, krk = krR.next()
                cosb = cosr[:, n, :].unsqueeze(1).unsqueeze(1).broadcast_to([128, 1, 2, 64])
                ssb = ssr[:, n, :].rearrange("p (h a d) -> p h a d", h=1, a=2)
                rope(v3(kr[:]), v3(ps[b][:, 0:128]), cosb, ssb, 64, 1, v3(tA[:]), v3(tB[:]), [pk(b)], [krk], tAk, tBk)
                k3, k3k = k3R.next()
                act(k3[:, 0, :], kr[:], AF.Identity, [krk, "kplain"], [k3k], scale=kplain[:, 0:1], bias=0.0)
                act(kf[:, n, :], kr[:], AF.Identity, [krk, "dcol"], [("kf", n)], scale=dcol[:, h, 2:3], bias=0.0)
                act(k3[:, 2, :], kr[:], AF.Identity, [krk, "dcol", k3k], [k3k], scale=dcol[:, h, 3:4], bias=0.0)
                cp(vtm[:, n, :], ps[b][:, 128:384], [pk(b)], [("vtm", n)], eng="act")
                b2 = 2 + (n % 2)
                pb = ps[b2][:].bitcast(BF16)
                tp(pb[:, 0:128], k3[:, 0, :], 128, [k3k], [pk(b2)])
                cp(kT[:, ns], pb[:, 0:128], [pk(b2)], [("kT", n)], eng="act")
                cp(Sb[:, n, :], Rb[:], ["Rb"], [("Sb", n)])
                b3 = 4 + (n % 2)
                mm(ps[b3][:, 0:256], k3[:, 2, :], vtm[:, n, :], True, True, [k3k, ("vtm", n)], [pk(b3)])
                stt(Rb[:], Rb[:], gC[:, 4 + h:5 + h], ps[b3][:, 0:256], ALU.mult, ALU.add, ["Rb", "gC", pk(b3)], ["Rb"])
            for n in range(NT):
                ns = slice(n * 128, (n + 1) * 128)
                b = n % 2
                for kc in range(KC):
                    mm(ps[b][:, 0:384], HT[:, kc, ns], wr[:, kc, 0:384], kc == 0, kc == KC - 1, [("HT", n), "wrqg"], [pk(b)])
                tA, tAk = tAR.next()
                tB, tBk = tBR.next()
                kr, krk = krR.next()
                cosb = cosr[:, n, :].unsqueeze(1).unsqueeze(1).broadcast_to([128, 1, 2, 64])
                ssb = ssr[:, n, :].rearrange("p (h a d) -> p h a d", h=1, a=2)
                rope(v3(kr[:]), v3(ps[b][:, 0:128]), cosb, ssb, 64, 1, v3(tA[:]), v3(tB[:]), [pk(b)], [krk], tAk, tBk)
                q3, q3k = k3R.next()
                act(q3[:, 0, :], kr[:], AF.Copy, [krk], [q3k])
                act(q3[:, 1, :], kr[:], AF.Identity, [krk, "dcol", q3k], [q3k], scale=dcol[:, h, 0:1], bias=0.0)
                act(q3[:, 2, :], kr[:], AF.Identity, [krk, "dcol", q3k], [q3k], scale=dcol[:, h, 1:2], bias=0.0)
                sg, sgk = sgR.next()
                act(sg[:], ps[b][:, 128:384], AF.Silu, [pk(b)], [sgk])
                b2 = 2 + (n % 2)
                pb = ps[b2][:].bitcast(BF16)
                for j in range(3):
                    tp(pb[:, j * 128:(j + 1) * 128], q3[:, j, :], 128, [q3k], [pk(b2)])
                q3T, q3Tk = q3TR.next()
                cp(q3T[:].rearrange("p j t -> p (j t)"), pb[:, 0:384], [pk(b2)], [q3Tk])
                b3 = 4 + (n % 2)
                mm(ps[b3][:, 0:128], kT[:, ns], q3T[:, 0, :], True, True, [("kT", n), q3Tk], [pk(b3)])
                pt, ptk = ptR.next()
                tt(pt[:], ps[b3][:, 0:128], maskT[:, h, :], ALU.mult, [pk(b3), "maskT"], [ptk])
                Sf, Sfk = SfR.next()
                cp(Sf[:], Rf[:], ["Rf"], [Sfk])
                b4 = 6 + (n % 2)
                mm(ps[b4][:, 0:256], pt[:], vtm[:, n, :], True, False, [ptk, ("vtm", n)], [pk(b4)])
                mm(ps[b4][:, 0:256], q3T[:, 1, :], Sf[:], False, False, [q3Tk, Sfk], [pk(b4)])
                mm(ps[b4][:, 0:256], q3T[:, 2, :], Sb[:, n, :], False, True, [q3Tk, ("Sb", n)], [pk(b4)])
                mm(ps[b3][:, 256:512], kf[:, n, :], vtm[:, n, :], True, True, [("kf", n), ("vtm", n), ptk], [pk(b3)])
                stt(Rf[:], Rf[:], gC[:, h:h + 1], ps[b3][:, 256:512], ALU.mult, ALU.add, ["Rf", "gC", pk(b3), Sfk], ["Rf"])
                st_, sk = nstat()
                P.add("dve", lambda e, st_=st_, b4=b4: e.bn_stats(out=st_[:, 0:6], in_=ps[b4][:, 0:256]), r=[pk(b4)], w=[sk])
                P.add("dve", lambda e, st_=st_: e.bn_aggr(out=st_[:, 12:14], in_=st_[:, 0:6].rearrange("p (a b) -> p a b", a=1)), r=[sk], w=[sk])
                ts(st_[:, 14:15], st_[:, 13:14], LN_EPS, None, ALU.add, None, [sk], [sk])
                rsq(st_[:, 16:17], st_[:, 14:15], 1, [sk], [sk])
                stt(st_[:, 17:18], st_[:, 12:13], -1.0, st_[:, 16:17], ALU.mult, ALU.mult, [sk], [sk])
                ron, ronk = ronR.next()
                act(ron[:], ps[b4][:, 0:256], AF.Identity, [pk(b4), sk], [ronk], bias=st_[:, 17:18], scale=st_[:, 16:17])
                ab, abk = abR.next()
                tt(ab[:], ron[:], sg[:], ALU.mult, [ronk, sgk], [abk])
                for j in range(2):
                    tp(pb[:, 512 + j * 128:512 + (j + 1) * 128], ab[:, j * 128:(j + 1) * 128], 128, [abk, q3Tk], [pk(b2)])
                for j in range(2):
                    act(R2[:, 2 * h + j, ns], pb[:, 512 + j * 128:512 + (j + 1) * 128], AF.Identity, [pk(b2), "gncol"], [("R2", 2 * h + j)],
                        scale=gncol[:, 2 * h + j:2 * h + j + 1], bias=0.0)
        fence()
        arT.reset()

    def ln_epilogue(s, l, t, bps, gb, lng, lnb, zR, last, nexth=None):
        z, zk = zR.next()
        for hf in range(2):
            tt(z[:, hf * 512:(hf + 1) * 512], ps[bps[hf]][:, :], gb[:, hf * 512:(hf + 1) * 512], ALU.mult,
               [pk(bps[hf]), "gb"] + ([zk] if hf else []), [zk])
        stt(z[:], X[:, t, :], ALPHA, z[:], ALU.mult, ALU.add, [("X", t), zk], [zk])
        st_, sk = ln_stats(z, [zk], 0)
        act(z[:], z[:], AF.Identity, [zk, sk], [zk], bias=st_[:, 17:18], scale=st_[:, 16:17])
        tt(z[:], z[:], lng[:], ALU.mult, [zk, "lnrow"], [zk])
        tt(X[:, t, :], z[:], lnb[:], ALU.add, [zk, "lnrow"], [("X", t)])
        if last:
            dma("sp", y_d[s, t * 128:(t + 1) * 128, :], X[:, t, :], [("X", t)], [("y", s, t)])
        if nexth is not None:
            nexth(t)

    def phase_out(s, l):
        arR.reset()
        wout = keep["wout"]
        mTR = Ring(arT, 2, [128, KC, 128], BF16, "mT")
        xnR = Ring(arT, 2, [128, 1024], BF16, "xn")
        gb = arR.alloc([128, D], F32, "gb")
        lng = arR.alloc([128, D], F32, "lng")
        lnb = arR.alloc([128, D], F32, "lnb")
        zR = Ring(arR, 3, [128, D], F32, "z")
        dma("sp", gb[:], adag_d[s, l, 0:1, :].partition_broadcast(128), [("adag", s, l, 0)], ["gb"])
        dma("sp", lng[:], lnr_d[l, 0:1, :].partition_broadcast(128), [], ["lnrow"])
        dma("sp", lnb[:], lnr_d[l, 1:2, :].partition_broadcast(128), [], ["lnrow"])
        for t in range(NT):
            mT, mTk = mTR.next()
            qb = (t * 128) // QB
            dma("sp", mT[:], M_d[:, :, t * 128:(t + 1) * 128].rearrange("c p t -> p c t"), [("M", c, qb) for c in range(8)], [mTk])
            bps = (2 * (t % 2), 2 * (t % 2) + 1)
            for hf in range(2):
                for kc in range(KC):
                    mm(ps[bps[hf]][:, :], mT[:, kc, :], wout[:, kc, hf * 512:(hf + 1) * 512], kc == 0, kc == KC - 1, [mTk, "wout"], [pk(bps[hf])])
            ln_epilogue(s, l, t, bps, gb, lng, lnb, zR, False, nexth=lambda t_: h_tile(s, l, 1, t_, xnR, (4, 5, 6, 7)))
        fence()
        arT.reset()
        arR.reset()

    def phase_mlp(s, l, last):
        arT.reset()
        arR.reset()
        wd = arT.alloc([128, NFC, D], BF16, "wd")
        wd_top = arT.top
        convc = arT.alloc([128, NFC, 4], F32, "convc")
        dma("sp", convc[:], convc_d[l], [], ["convc"])
        wuR = Ring(arT, 2, [128, KC, 256], BF16, "wu")
        aext = arR.alloc([128, S + 2], F32, "aext")
        u = arR.alloc([128, S], F32, "u")
        geR = Ring(arR, 1, [128, S], BF16, "ge")
        gR = Ring(arR, 2, [128, S], BF16, "g")
        P.add("dve", lambda e: e.memset(aext[:, 0:1], 0.0), w=["aext"])
        P.add("dve", lambda e: e.memset(aext[:, S + 1:S + 2], 0.0), r=["aext"], w=["aext"])
        allH = [("HT", t) for t in range(NT)]
        i = 0
        for fc in range(NFC):
            wu, wuk_ = wuR.next()
            dma("pool", wu[:, :, 0:128], wup_d[l, :, :, fc * 128:(fc + 1) * 128], [], [wuk_])
            dma("pool", wu[:, :, 128:256], wup_d[l, :, :, FF + fc * 128:FF + (fc + 1) * 128], [], [wuk_])
            ge, gek = geR.next()
            g, gk = gR.next()
            bbs = []
            for qb in range(NQB):
                qs = slice(qb * QB, (qb + 1) * QB)
                ba = i % 4
                bb = 4 + (i % 4)
                i += 1
                bbs.append(bb)
                for kc in range(KC):
                    mm(ps[ba][:, 0:QB], wu[:, kc, 0:128], HT[:, kc, qs], kc == 0, kc == KC - 1, allH + [wuk_], [pk(ba)])
                for kc in range(KC):
                    mm(ps[bb][:, 0:QB], wu[:, kc, 128:256], HT[:, kc, qs], kc == 0, kc == KC - 1, allH + [wuk_], [pk(bb)])
                act(aext[:, 1 + qb * QB:1 + (qb + 1) * QB], ps[ba][:, 0:QB], AF.Copy, [pk(ba)], ["aext"])
            act(u[:], aext[:, 1:S + 1], AF.Identity, ["aext", "convc"], ["u"], scale=convc[:, fc, 1:2], bias=convc[:, fc, 3:4])
            stt(u[:], aext[:, 0:S], convc[:, fc, 0:1], u[:], ALU.mult, ALU.add, ["aext", "convc", "u"], ["u"])
            stt(u[:], aext[:, 2:S + 2], convc[:, fc, 2:3], u[:], ALU.mult, ALU.add, ["aext", "convc", "u"], ["u"])
            act(ge[:], u[:], AF.Gelu, ["u"], [gek])
            for qb in range(NQB):
                qs = slice(qb * QB, (qb + 1) * QB)
                tt(g[:, qs], ge[:, qs], ps[bbs[qb]][:, 0:QB], ALU.mult, [gek, pk(bbs[qb])] + ([gk] if qb else []), [gk])
            dma("sp", G_d[fc], g[:], [gk], [("G", fc)])
            if fc in (3, 10):
                q4 = 0 if fc == 3 else 1
                dma("pool", wd[:, q4 * 11:(q4 + 1) * 11, :], wdn_d[l, :, q4 * 11:(q4 + 1) * 11, :], [], ["wd"])
        fence()
        arR.reset()
        arT.top = wd_top
        xnR = Ring(arT, 2, [128, 1024], BF16, "xn")
        gtR = Ring(arR, 2, [128, NFC, 128], BF16, "gt")
        gb = arR.alloc([128, D], F32, "gb2")
        lng = arR.alloc([128, D], F32, "lng2")
        lnb = arR.alloc([128, D], F32, "lnb2")
        zR = Ring(arR, 2, [128, D], F32, "z2")
        dma("sp", gb[:], adag_d[s, l, 1:2, :].partition_broadcast(128), [("adag", s, l, 1)], ["gb"])
        dma("sp", lng[:], lnr_d[l, 2:3, :].partition_broadcast(128), [], ["lnrow"])
        dma("sp", lnb[:], lnr_d[l, 3:4, :].partition_broadcast(128), [], ["lnrow"])
        for t in range(NT):
            gt, gtk = gtR.next()
            dma("sp", gt[:], G_d[:, :, t * 128:(t + 1) * 128].rearrange("c p t -> p c t"), [("G", fc) for fc in range(NFC)], [gtk])
            bps = (2 * (t % 2), 2 * (t % 2) + 1)
            for hf in range(2):
                for fc in range(NFC):
                    mm(ps[bps[hf]][:, :], gt[:, fc, :], wd[:, fc, hf * 512:(hf + 1) * 512], fc == 0, fc == NFC - 1, [gtk, "wd"], [pk(bps[hf])])
            ln_epilogue(s, l, t, bps, gb, lng, lnb, zR, last,
                        nexth=None if last else (lambda t_: h_tile(s, l + 1, 0, t_, xnR, (4, 5, 6, 7))))
        fence()
        arT.reset()
        arR.reset()

    for s in range(NSEQ):
        for t in range(NT):
            dma("sp", X[:, t, :], x_d[s, t * 128:(t + 1) * 128, :], [], [("X", t)])
        for l in range(L):
            layer_consts(l)
            if l == 0:
                phase_H(s, l, 0)
            phase_mla(s, l)
            phase_gate(l, wmo_d, 3520 + 1024, True)
            phase_ret(s, l)
            phase_gate(l, wro_d, 3520, False)
            phase_out(s, l)
            phase_mlp(s, l, l == L - 1)
    P.add("sp", lambda e: None, r=[("y", s, t) for s in range(NSEQ) for t in range(NT)])

    es = ExitStack()
    P.emit(nc, es)
    es.close()
    return nc, P


def host_consts(S):
    NT = S // 128
    pos = np.arange(S, dtype=np.float32)

    def tables(dim):
        inv = (1.0 / (10000.0 ** (np.arange(0, dim, 2, dtype=np.float32) / np.float32(dim)))).astype(np.float32)
        ang = pos[:, None] * inv[None, :]
        return np.cos(ang).astype(np.float32), np.sin(ang).astype(np.float32)

    def tm(a):
        return np.ascontiguousarray(a.reshape(NT, 128, -1).transpose(1, 0, 2))

    cr, sr = tables(128)
    cm, sm = tables(64)
    k = np.arange(128, dtype=np.float32)[:, None]
    q = np.arange(128, dtype=np.float32)[None, :]
    cst4 = np.stack([np.maximum(q - k, 0.0), np.maximum(k - q, 0.0), (q >= k).astype(np.float32), (k > q).astype(np.float32)], axis=1)
    c = np.arange(128, dtype=np.float32)
    dexp = np.stack([c + 1.0, 128.0 - c, 127.0 - c, c], axis=1)
    return {
        "ident": np.eye(128, dtype=np.float32),
        "cosr": tm(cr), "ssr": tm(np.concatenate([-sr, sr], axis=1)),
        "cosm": tm(cm), "ssm": tm(np.concatenate([-sm, sm], axis=1)),
        "cst4": np.ascontiguousarray(cst4.astype(np.float32)), "dexp": np.ascontiguousarray(dexp.astype(np.float32)),
    }


def host_weights(inp, L):
    f = lambda a: np.ascontiguousarray(np.asarray(a, dtype=np.float32))

    def pk(w, kc):
        w = np.asarray(w, dtype=np.float32)
        return np.ascontiguousarray(w.reshape(L, kc, 128, w.shape[-1]).transpose(0, 2, 1, 3))

    def col(v, nb):
        v = np.asarray(v, dtype=np.float32)
        return np.ascontiguousarray(v.reshape(L, nb, 128).transpose(0, 2, 1))

    conv = np.concatenate([np.asarray(inp["conv_w"], np.float32), np.asarray(inp["conv_b"], np.float32)[:, None, :]], axis=1)
    convcol = np.ascontiguousarray(conv.reshape(L, 4, NFC, 128).transpose(0, 3, 2, 1))
    b_ada = np.asarray(inp["b_ada"], np.float32)
    return {
        "w_ada": pk(inp["w_ada"], KC), "b_ada": f(b_ada),
        "b_adacol": np.ascontiguousarray(b_ada.reshape(L, 48, 128).transpose(2, 0, 1)),
        "w_in": pk(inp["w_in"], KC),
        "dec": f(np.concatenate([np.asarray(inp["ret_decay_fwd"], np.float32), np.asarray(inp["ret_decay_bwd"], np.float32)], axis=1)),
        "gncol": col(inp["ret_gn_g"], 8), "w_ret_o": pk(inp["w_ret_o"], KC),
        "qgcol": col(inp["q_norm_g"], 2), "kvgcol": col(inp["kv_norm_g"], 1),
        "w_uq": pk(inp["w_uq"], 2), "w_uk": f(inp["w_uk"]), "w_uv": f(inp["w_uv"]),
        "w_mla_o": pk(inp["w_mla_o"], KC), "w_out": pk(inp["w_out"], KC),
        "lnrows": f(np.stack([np.asarray(inp[k], np.float32) for k in ("ln1_g", "ln1_b", "ln2_g", "ln2_b")], axis=1)),
        "w_up": pk(inp["w_up"], KC), "convcol": convcol, "w_down": pk(inp["w_down"], NFC),
    }


_cache = {}
_runkw = {}
_last = [None]


def run(xs, cs, inp, n_cores, L):
    NTOT, S, _ = xs.shape
    NSEQ = NTOT // n_cores
    key = (NSEQ, S, L)
    if key not in _cache:
        _cache[key] = build(NSEQ, S, L)[0]
    nc = _cache[key]
    shared = dict(host_consts(S))
    shared.update(host_weights(inp, L))
    in_maps = []
    for i in range(n_cores):
        m = dict(shared)
        m["x"] = np.ascontiguousarray(xs[i * NSEQ:(i + 1) * NSEQ])
        cc = cs[i * NSEQ:(i + 1) * NSEQ]
        m["cT"] = np.ascontiguousarray(cc.reshape(NSEQ, KC, 128).transpose(2, 1, 0))
        in_maps.append(m)
    res = run_bass_kernel_spmd(nc, in_maps, core_ids=list(range(n_cores)), **_runkw)
    _last[0] = res
    return np.concatenate([r["y"] for r in res.results], axis=0)


def kernel(**inp):
    xp = np.asarray(inp["x_prompt"], np.float32)
    xsm = np.asarray(inp["x_sample"], np.float32)
    xs = np.concatenate([xp, xsm], axis=0)
    cs = np.concatenate([np.asarray(inp["c_prompt"], np.float32), np.asarray(inp["c_sample"], np.float32)], axis=0)
    L = np.asarray(inp["w_in"]).shape[0]
    y = run(xs, cs, inp, 8, L)
    nb = xp.shape[0]
    return (np.ascontiguousarray(y[:nb]), np.ascontiguousarray(y[nb:]))
```
